# Optimizing a Trainium2 kernel written in Bass

```python
import math
import jax, jax.numpy as jnp
from jax import lax
import numpy as np

D_MODEL = 1024
BATCH = 8
SEQ = 4096
DEPTH = 2

GRID_W = 64
CTX_LEN = 256
MIX_WIDTH = 2 * D_MODEL

SSD_WIDTH = D_MODEL
SSD_HEADDIM = 64
SSD_HEADS = SSD_WIDTH // SSD_HEADDIM
SSD_GROUPS = 2
SSD_HPG = SSD_HEADS // SSD_GROUPS
SSD_STATE = 128
SSD_GN = SSD_GROUPS * SSD_STATE
SSD_CONV_DIM = SSD_WIDTH + 2 * SSD_GN
SSD_CONV_K = 5
SSD_CHUNK = 128

GLA_WIDTH = D_MODEL // 2
GLA_HEADS = 4
GLA_DK = GLA_WIDTH // (2 * GLA_HEADS)
GLA_DV = GLA_WIDTH // GLA_HEADS
GLA_RANK = 16
GLA_TAU = 16.0
GLA_CHUNK = 64

S5_WIDTH = D_MODEL // 2
S5_GROUP = 16
S5_GROUPS = S5_WIDTH // S5_GROUP
S5_STATE = 64

IN_SIZES = (SSD_WIDTH, SSD_CONV_DIM, 2 * SSD_HEADS,
            GLA_HEADS * GLA_DK, GLA_HEADS * GLA_DK, GLA_WIDTH, GLA_WIDTH, 2 * GLA_RANK,
            S5_WIDTH, S5_WIDTH)
IN_DIM = sum(IN_SIZES)
EPS = 1e-6

kernel_name = "hymba_style_ssd_gla_s5_prefix_dit"


def rmsnorm(x, w):
    xf = x.astype(jnp.float32)
    y = xf * lax.rsqrt(jnp.mean(xf * xf, axis=-1, keepdims=True) + EPS)
    return (y * w.astype(jnp.float32)).astype(x.dtype)


def split_in_proj(p):
    parts, start = [], 0
    for size in IN_SIZES:
        parts.append(p[..., start:start + size])
        start += size
    return parts


def dwconv_centred(x, w, b):
    k, ch = w.shape
    y = lax.conv_general_dilated(x, w[:, None, :].astype(x.dtype), window_strides=(1,),
                                 padding=[(k // 2, k // 2)],
                                 dimension_numbers=("NWC", "WIO", "NWC"),
                                 feature_group_count=ch)
    return y + b.astype(x.dtype)


def to_col_major(u, rows):
    bsz, length, d = u.shape
    return u.reshape(bsz, rows, GRID_W, d).transpose(0, 2, 1, 3).reshape(bsz, length, d)


def from_col_major(u, rows):
    bsz, length, d = u.shape
    return u.reshape(bsz, GRID_W, rows, d).transpose(0, 2, 1, 3).reshape(bsz, length, d)


def bidirectional(scan_fwd, scan_bwd, ctx_args, lat_args, h0):
    flip = lambda t: jnp.flip(t, axis=1)
    y_cf, h_cf = scan_fwd(*ctx_args, h0)
    y_lf, _ = scan_fwd(*lat_args, h_cf)
    y_cb, h_cb = scan_bwd(*[flip(t) for t in ctx_args], h0)
    y_lb, _ = scan_bwd(*[flip(t) for t in lat_args], h_cb)
    return y_cf + flip(y_cb), y_lf + flip(y_lb)


def segsum(a):
    t = a.shape[-1]
    cs = jnp.cumsum(a, axis=-1)
    diff = cs[..., :, None] - cs[..., None, :]
    return jnp.where(jnp.tril(jnp.ones((t, t), bool)), diff, -jnp.inf)


def ssd_scan(x, da, bm, cm, h0):
    bsz, length, g, r, p = x.shape
    n = bm.shape[-1]
    nc = length // SSD_CHUNK
    x = x.reshape(bsz, nc, SSD_CHUNK, g, r, p)
    bm = bm.reshape(bsz, nc, SSD_CHUNK, g, n)
    cm = cm.reshape(bsz, nc, SSD_CHUNK, g, n)
    da = da.reshape(bsz, nc, SSD_CHUNK, g, r).transpose(0, 3, 4, 1, 2)
    acs = jnp.cumsum(da, axis=-1)
    decay = jnp.exp(segsum(da))
    cb = jnp.einsum("bclgn,bcsgn->bgcls", cm, bm)
    y_diag = jnp.einsum("bgrcls,bcsgrp->bclgrp", cb[:, :, None] * decay, x)
    decay_states = jnp.exp(acs[..., -1:] - acs)
    states = jnp.einsum("bclgn,bgrcl,bclgrp->bcgrpn", bm, decay_states, x)

    def step(h, inp):
        dec, st = inp
        return h * dec[..., None, None] + st, h

    h_fin, h_prev = lax.scan(step, h0, (jnp.exp(acs[..., -1]).transpose(3, 0, 1, 2),
                                        states.transpose(1, 0, 2, 3, 4, 5)))
    y_off = jnp.einsum("bclgn,cbgrpn,bgrcl->bclgrp", cm, h_prev, jnp.exp(acs))
    return (y_diag + y_off).reshape(bsz, length, g, r, p), h_fin


def gla_scan(q, k, v, g, s0):
    bsz, length, h, dk = q.shape
    dv = v.shape[-1]
    nc = length // GLA_CHUNK
    q = q.reshape(bsz, nc, GLA_CHUNK, h, dk)
    k = k.reshape(bsz, nc, GLA_CHUNK, h, dk)
    v = v.reshape(bsz, nc, GLA_CHUNK, h, dv)
    b = jnp.cumsum(g.reshape(bsz, nc, GLA_CHUNK, h, dk), axis=2)
    b_last = b[:, :, -1]
    qe = q * jnp.exp(b)
    ke = k * jnp.exp(-b)
    kd = k * jnp.exp(b_last[:, :, None] - b)
    lower = jnp.tril(jnp.ones((GLA_CHUNK, GLA_CHUNK), bool))
    attn = jnp.where(lower, jnp.einsum("bcthd,bcshd->bchts", qe, ke), 0.0)
    o = jnp.einsum("bchts,bcshv->bcthv", attn, v)
    upd = jnp.einsum("bcshd,bcshv->bchdv", kd, v)

    def step(s, inp):
        dec, u = inp
        return s * dec[..., None] + u, s

    s_fin, s_prev = lax.scan(step, s0, (jnp.exp(b_last).transpose(1, 0, 2, 3),
                                        upd.transpose(1, 0, 2, 3, 4)))
    o = o + jnp.einsum("bcthd,cbhdv->bcthv", qe, s_prev)
    return o.reshape(bsz, length, h, dv), s_fin


def s5_scan(bu, h0, lam_bar):
    bu = bu.at[:, 0].add(lam_bar * h0)
    a = jnp.broadcast_to(lam_bar, bu.shape)

    def combine(e1, e2):
        a1, b1 = e1
        a2, b2 = e2
        return a1 * a2, a2 * b1 + b2

    _, h = lax.associative_scan(combine, (a, bu), axis=1)
    return h


def ssd_branch(z_c, xbc_c, dt_c, z_l, xbc_l, dt_l, conv_w, conv_b, a_log, dt_bias, d_skip, norm_w):
    f32 = jnp.float32
    a_neg = -jnp.exp(a_log.astype(f32))

    def prep(xbc, dt):
        bsz, length = xbc.shape[:2]
        xbc = jax.nn.silu(dwconv_centred(xbc, conv_w, conv_b)).astype(f32)
        xs = xbc[..., :SSD_WIDTH].reshape(bsz, length, SSD_GROUPS, SSD_HPG, SSD_HEADDIM)
        bm = xbc[..., SSD_WIDTH:SSD_WIDTH + SSD_GN].reshape(bsz, length, SSD_GROUPS, SSD_STATE)
        cm = xbc[..., SSD_WIDTH + SSD_GN:].reshape(bsz, length, SSD_GROUPS, SSD_STATE)
        dt = jax.nn.softplus(dt.astype(f32).reshape(bsz, length, 2, SSD_HEADS) + dt_bias.astype(f32))
        return xs, bm, cm, dt.reshape(bsz, length, 2, SSD_GROUPS, SSD_HPG)

    def make_scan(d):
        a_d = a_neg[d].reshape(SSD_GROUPS, SSD_HPG)

        def scan_fn(xs, bm, cm, dt, h0):
            dt_d = dt[:, :, d]
            return ssd_scan(xs * dt_d[..., None], dt_d * a_d, bm, cm, h0)
        return scan_fn

    ctx_args = prep(xbc_c, dt_c)
    lat_args = prep(xbc_l, dt_l)
    h0 = jnp.zeros((xbc_c.shape[0], SSD_GROUPS, SSD_HPG, SSD_HEADDIM, SSD_STATE), f32)
    y_c, y_l = bidirectional(make_scan(0), make_scan(1), ctx_args, lat_args, h0)
    dsk = d_skip.astype(f32).reshape(SSD_GROUPS, SSD_HPG, 1)

    def finish(y, xs, z):
        bsz, length = z.shape[:2]
        y = (y + dsk * xs).reshape(bsz, length, SSD_WIDTH) * jax.nn.silu(z.astype(f32))
        y = y.reshape(bsz, length, SSD_GROUPS, SSD_WIDTH // SSD_GROUPS)
        y = y * lax.rsqrt(jnp.mean(y * y, axis=-1, keepdims=True) + EPS)
        return (y.reshape(bsz, length, SSD_WIDTH) * norm_w.astype(f32)).astype(z.dtype)

    return finish(y_c, ctx_args[0], z_c), finish(y_l, lat_args[0], z_l)


def gla_branch(q_c, k_c, v_c, lr_c, gate_c, q_l, k_l, v_l, lr_l, gate_l, w_lr, b_lr, norm_w):
    f32 = jnp.float32

    def prep(q, k, v, lr):
        bsz, length = q.shape[:2]
        q = q.astype(f32).reshape(bsz, length, GLA_HEADS, GLA_DK) * GLA_DK ** -0.5
        k = k.astype(f32).reshape(bsz, length, GLA_HEADS, GLA_DK)
        v = v.astype(f32).reshape(bsz, length, GLA_HEADS, GLA_DV)
        lr = lr.astype(f32).reshape(bsz, length, 2, GLA_RANK)
        logit = jnp.einsum("blxr,xrk->blxk", lr, w_lr.astype(f32)) + b_lr.astype(f32)
        g = jax.nn.log_sigmoid(logit) / GLA_TAU
        return q, k, v, g.reshape(bsz, length, 2, GLA_HEADS, GLA_DK)

    def make_scan(d):
        def scan_fn(q, k, v, g, s0):
            return gla_scan(q, k, v, g[:, :, d], s0)
        return scan_fn

    ctx_args = prep(q_c, k_c, v_c, lr_c)
    lat_args = prep(q_l, k_l, v_l, lr_l)
    s0 = jnp.zeros((q_c.shape[0], GLA_HEADS, GLA_DK, GLA_DV), f32)
    o_c, o_l = bidirectional(make_scan(0), make_scan(1), ctx_args, lat_args, s0)

    def finish(o, gate):
        bsz, length = gate.shape[:2]
        o = o * lax.rsqrt(jnp.mean(o * o, axis=-1, keepdims=True) + EPS) * norm_w.astype(f32)
        return (o.reshape(bsz, length, GLA_WIDTH) * jax.nn.silu(gate.astype(f32))).astype(gate.dtype)

    return finish(o_c, gate_c), finish(o_l, gate_l)


def s5_branch(u_c, gate_c, u_l, gate_l, lam_re, lam_im, log_step, b_re, b_im, c_re, c_im,
              d_skip, glu_w, glu_b):
    f32 = jnp.float32
    bmat = lax.complex(b_re.astype(f32), b_im.astype(f32))

    def make_scan(d):
        lam = lax.complex(lam_re[d].astype(f32), lam_im[d].astype(f32))
        lam_bar = jnp.exp(lam * jnp.exp(log_step[d].astype(f32))[:, None])
        b_bar = ((lam_bar - 1.0) / lam)[..., None] * bmat
        cmat = lax.complex(c_re[d].astype(f32), c_im[d].astype(f32))

        def scan_fn(u, h0):
            bu = jnp.einsum("blgh,gph->blgp", u.astype(jnp.complex64), b_bar)
            h = s5_scan(bu, h0, lam_bar)
            return jnp.einsum("blgp,ghp->blgh", h, cmat).real, h[:, -1]
        return scan_fn

    def prep(u):
        bsz, length = u.shape[:2]
        return (u.astype(f32).reshape(bsz, length, S5_GROUPS, S5_GROUP),)

    ctx_args = prep(u_c)
    lat_args = prep(u_l)
    h0 = jnp.zeros((u_c.shape[0], S5_GROUPS, S5_STATE), jnp.complex64)
    y_c, y_l = bidirectional(make_scan(0), make_scan(1), ctx_args, lat_args, h0)
    dsk = d_skip.astype(f32).reshape(S5_GROUPS, S5_GROUP)

    def finish(y, u, gate):
        bsz, length = gate.shape[:2]
        y = jax.nn.gelu((y + dsk * u).reshape(bsz, length, S5_WIDTH))
        pr = y @ glu_w.astype(f32) + glu_b.astype(f32)
        y = pr[..., :S5_WIDTH] * jax.nn.sigmoid(pr[..., S5_WIDTH:])
        return (y * jax.nn.silu(gate.astype(f32))).astype(gate.dtype)

    return finish(y_c, ctx_args[0], gate_c), finish(y_l, lat_args[0], gate_l)


def setup_inputs(seed: int = 0) -> dict:
    key = jax.random.key(seed)
    ks = iter(jax.random.split(key, 40))
    f32 = jnp.float32
    nrm = lambda shape, s: s * jax.random.normal(next(ks), shape, f32)
    x = nrm((BATCH, SEQ, D_MODEL), 1.0)
    c = nrm((BATCH, D_MODEL), 1.0)
    ctx = nrm((BATCH, CTX_LEN, D_MODEL), 1.0)
    c_ctx = nrm((D_MODEL,), 1.0)
    norm_w = 1.0 + nrm((DEPTH, D_MODEL), 0.02)
    mod_w = nrm((DEPTH, D_MODEL, 3 * D_MODEL), 0.5 * D_MODEL ** -0.5)
    mod_b = nrm((DEPTH, 3 * D_MODEL), 0.02)
    w_in = nrm((DEPTH, D_MODEL, IN_DIM), D_MODEL ** -0.5)
    w_out = nrm((DEPTH, MIX_WIDTH, D_MODEL), MIX_WIDTH ** -0.5)
    ssd_conv_w = nrm((DEPTH, SSD_CONV_K, SSD_CONV_DIM), SSD_CONV_K ** -0.5)
    ssd_conv_b = nrm((DEPTH, SSD_CONV_DIM), 0.02)
    ssd_a_log = jnp.log(jax.random.uniform(next(ks), (DEPTH, 2, SSD_HEADS), f32, 1.0, 16.0))
    dt0 = jnp.exp(jax.random.uniform(next(ks), (DEPTH, 2, SSD_HEADS), f32,
                                     math.log(1e-3), math.log(1e-1)))
    ssd_dt_bias = dt0 + jnp.log(-jnp.expm1(-dt0))
    ssd_d = 1.0 + nrm((DEPTH, SSD_HEADS), 0.1)
    ssd_norm_w = 1.0 + nrm((DEPTH, SSD_WIDTH), 0.02)
    gla_w_lr = nrm((DEPTH, 2, GLA_RANK, GLA_HEADS * GLA_DK), GLA_RANK ** -0.5)
    gla_b_lr = nrm((DEPTH, 2, GLA_HEADS * GLA_DK), 0.5)
    gla_norm_w = 1.0 + nrm((DEPTH, GLA_DV), 0.02)
    s5_lam_re = -0.5 + nrm((DEPTH, 2, S5_GROUPS, S5_STATE), 0.01)
    s5_lam_im = jnp.pi * jnp.arange(S5_STATE, dtype=f32) + nrm((DEPTH, 2, S5_GROUPS, S5_STATE), 0.01)
    s5_log_step = jax.random.uniform(next(ks), (DEPTH, 2, S5_GROUPS), f32,
                                     math.log(1e-3), math.log(1e-1))
    s5_b_re = nrm((DEPTH, S5_GROUPS, S5_STATE, S5_GROUP), (2 * S5_GROUP) ** -0.5)
    s5_b_im = nrm((DEPTH, S5_GROUPS, S5_STATE, S5_GROUP), (2 * S5_GROUP) ** -0.5)
    s5_c_re = nrm((DEPTH, 2, S5_GROUPS, S5_GROUP, S5_STATE), (2 * S5_STATE) ** -0.5)
    s5_c_im = nrm((DEPTH, 2, S5_GROUPS, S5_GROUP, S5_STATE), (2 * S5_STATE) ** -0.5)
    s5_d = nrm((DEPTH, S5_WIDTH), 1.0)
    s5_glu_w = nrm((DEPTH, S5_WIDTH, 2 * S5_WIDTH), S5_WIDTH ** -0.5)
    s5_glu_b = nrm((DEPTH, 2 * S5_WIDTH), 0.02)
    final_norm_w = 1.0 + nrm((D_MODEL,), 0.02)
    return {"x": x, "c": c, "ctx": ctx, "c_ctx": c_ctx, "norm_w": norm_w, "mod_w": mod_w,
            "mod_b": mod_b, "w_in": w_in, "w_out": w_out, "ssd_conv_w": ssd_conv_w,
            "ssd_conv_b": ssd_conv_b, "ssd_a_log": ssd_a_log, "ssd_dt_bias": ssd_dt_bias,
            "ssd_d": ssd_d, "ssd_norm_w": ssd_norm_w, "gla_w_lr": gla_w_lr, "gla_b_lr": gla_b_lr,
            "gla_norm_w": gla_norm_w, "s5_lam_re": s5_lam_re, "s5_lam_im": s5_lam_im,
            "s5_log_step": s5_log_step, "s5_b_re": s5_b_re, "s5_b_im": s5_b_im,
            "s5_c_re": s5_c_re, "s5_c_im": s5_c_im, "s5_d": s5_d, "s5_glu_w": s5_glu_w,
            "s5_glu_b": s5_glu_b, "final_norm_w": final_norm_w}


def reference(x, c, ctx, c_ctx, norm_w, mod_w, mod_b, w_in, w_out, ssd_conv_w, ssd_conv_b,
              ssd_a_log, ssd_dt_bias, ssd_d, ssd_norm_w, gla_w_lr, gla_b_lr, gla_norm_w,
              s5_lam_re, s5_lam_im, s5_log_step, s5_b_re, s5_b_im, s5_c_re, s5_c_im, s5_d,
              s5_glu_w, s5_glu_b, final_norm_w):
    length = x.shape[1]
    rows = length // GRID_W
    h_lat, h_ctx = x, ctx
    for l in range(DEPTH):
        col_major = (l % 2 == 1)
        mod = jax.nn.silu(c) @ mod_w[l] + mod_b[l]
        shift, scale, gate = jnp.split(mod, 3, axis=-1)
        mod_c = jax.nn.silu(c_ctx) @ mod_w[l] + mod_b[l]
        shift_c, scale_c, gate_c = jnp.split(mod_c, 3, axis=-1)
        u_lat = rmsnorm(h_lat, norm_w[l]) * (1.0 + scale[:, None]) + shift[:, None]
        u_ctx = rmsnorm(h_ctx, norm_w[l]) * (1.0 + scale_c) + shift_c
        if col_major:
            u_lat = to_col_major(u_lat, rows)
        (z_c, xbc_c, dt_c, q_c, k_c, v_c, gg_c, lr_c, u5_c, sg_c) = split_in_proj(u_ctx @ w_in[l])
        (z_l, xbc_l, dt_l, q_l, k_l, v_l, gg_l, lr_l, u5_l, sg_l) = split_in_proj(u_lat @ w_in[l])
        ssd_c, ssd_l = ssd_branch(z_c, xbc_c, dt_c, z_l, xbc_l, dt_l, ssd_conv_w[l], ssd_conv_b[l],
                                  ssd_a_log[l], ssd_dt_bias[l], ssd_d[l], ssd_norm_w[l])
        gla_c, gla_l = gla_branch(q_c, k_c, v_c, lr_c, gg_c, q_l, k_l, v_l, lr_l, gg_l,
                                  gla_w_lr[l], gla_b_lr[l], gla_norm_w[l])
        s5_c, s5_l = s5_branch(u5_c, sg_c, u5_l, sg_l, s5_lam_re[l], s5_lam_im[l], s5_log_step[l],
                               s5_b_re[l], s5_b_im[l], s5_c_re[l], s5_c_im[l], s5_d[l],
                               s5_glu_w[l], s5_glu_b[l])
        out_lat = jnp.concatenate([ssd_l, gla_l, s5_l], axis=-1) @ w_out[l]
        if col_major:
            out_lat = from_col_major(out_lat, rows)
        h_lat = h_lat + gate[:, None] * out_lat
        if l < DEPTH - 1:
            out_ctx = jnp.concatenate([ssd_c, gla_c, s5_c], axis=-1) @ w_out[l]
            h_ctx = h_ctx + gate_c * out_ctx
    return rmsnorm(h_lat, final_norm_w)
```

```python
import math, os
GSTOP = float(os.environ.get('GSTOP', '99'))
SSTOP = float(os.environ.get('SSTOP', '99'))
import numpy as np
from contextlib import ExitStack
import concourse.bass as bass
import concourse.mybir as mybir
from concourse.bass_utils import run_bass_kernel_spmd

F32 = mybir.dt.float32
BF16 = mybir.dt.bfloat16
AF = mybir.ActivationFunctionType
ALU = mybir.AluOpType

ENGS = ("pe", "act", "dve", "pool", "sp")


class Prog:
    def __init__(self, nc, stack):
        self.nc = nc
        self.stack = stack
        self.lists = {e: [] for e in ENGS}
        self.sems = {}
        self.semval = {}
        self.waited = {e: {} for e in ENGS}
        self.last_write = {}
        self.readers = {}
        for e in ("pe", "act", "dve", "pool"):
            self._sem("E_" + e)

    def _sem(self, name):
        if name not in self.sems:
            self.sems[name] = self.stack.enter_context(self.nc.semaphore(name))
            self.semval[name] = 0
        return self.sems[name]

    def _deps(self, eng, reads, writes):
        toks = []
        for k in reads:
            t = self.last_write.get(k)
            if t is not None:
                toks.append(t)
        for k in writes:
            t = self.last_write.get(k)
            if t is not None:
                toks.append(t)
            toks.extend(self.readers.get(k, ()))
        need = {}
        for (s, v) in toks:
            if s.startswith("D_"):
                v = self.semval[s]
            if eng == "pe" and s == "E_pe":
                continue
            if v > need.get(s, 0):
                need[s] = v
        waits = []
        for s, v in need.items():
            if self.waited[eng].get(s, 0) < v:
                self.waited[eng][s] = v
                waits.append((s, v))
        return waits

    def _commit(self, tok, reads, writes):
        for k in writes:
            self.last_write[k] = tok
            self.readers[k] = []
        for k in reads:
            if k in writes:
                continue
            self.readers.setdefault(k, []).append(tok)

    def op(self, eng, fn, reads=(), writes=()):
        bank_reads = [k for k in reads if len(k) == 2 and k[0] == "b" and k[1].isdigit()]
        if bank_reads:
            writes = list(writes) + [k for k in bank_reads if k not in writes]
        waits = self._deps(eng, reads, writes)
        s = "E_" + eng
        self.semval[s] += 1
        tok = (s, self.semval[s])
        self.lists[eng].append((waits, fn, (s, 1)))
        self._commit(tok, reads, writes)

    def dma(self, fn, stream, reads=(), writes=(), eng="sp"):
        s = "D_" + stream
        self._sem(s)
        waits = self._deps(eng, reads, writes)
        if self.semval[s] > 0 and self.waited[eng].get(s, 0) < self.semval[s]:
            self.waited[eng][s] = self.semval[s]
            waits = [w for w in waits if w[0] != s] + [(s, self.semval[s])]
        self.semval[s] += 16
        tok = (s, self.semval[s])
        self.lists[eng].append((waits, fn, (s, 16)))
        self._commit(tok, reads, writes)

    def barrier(self):
        for eng in ENGS:
            waits = []
            for s, v in self.semval.items():
                if v > 0 and self.waited[eng].get(s, 0) < v and not (eng != "pe" and False):
                    self.waited[eng][s] = v
                    waits.append((s, v))
            self.lists[eng].append((waits, None, None))

    def wait_all(self, eng, keys):
        waits = self._deps(eng, keys, ())
        self.lists[eng].append((waits, None, None))

    def emit(self):
        lists, sems = self.lists, self.sems

        def run(engh, items):
            for waits, fn, inc in items:
                for (s, v) in waits:
                    engh.wait_ge(sems[s], v)
                if fn is not None:
                    fn(engh).then_inc(sems[inc[0]], inc[1])

        with self.nc.Block() as block:
            @block.tensor
            def _(e):
                run(e, lists["pe"])

            @block.scalar
            def _(e):
                run(e, lists["act"])

            @block.vector
            def _(e):
                run(e, lists["dve"])

            @block.gpsimd
            def _(e):
                run(e, lists["pool"])

            @block.sync
            def _(e):
                run(e, lists["sp"])


D = 1024
LQ = 4096
LC = 256
LT = LQ + LC
NCH = LT // 128
DEPTH = 2
EPS = 1e-6
PI = math.pi
OZ, OXBC, ODT, OQ, OK_, OV, OGG, OLR, OU5, OSG = 0, 1024, 2560, 2592, 2848, 3104, 3616, 4128, 4160, 4672


def build(dbg=False, layers=(0, 1), phases=("p0", "ssd", "gla", "s5", "p4")):
    nc = bass.Bass("TRN2", target_bir_lowering=False)
    I = lambda name, shape, dt=F32: nc.dram_tensor(name, shape, dt, kind="ExternalInput").ap()
    skind = "ExternalOutput" if dbg else "Internal"
    S = lambda name, shape, dt=F32: nc.dram_tensor(name, shape, dt, kind=skind).ap()
    x_d = I("x", [LQ, D]); ctx_d = I("ctx", [LC, D])
    ccol_d = I("ccol", [128, 16])
    modw_d = I("mod_w", [DEPTH, D, 3 * D]); modbcol_d = I("modb_col", [DEPTH, 128, 24]); modb_d = I("mod_b", [DEPTH, 3 * D])
    nwcol_d = I("nw_col", [DEPTH, 128, 8])
    win_d = I("w_in", [DEPTH, D, 5184]); wout_d = I("w_out", [DEPTH, 2048, D])
    convw_d = I("convw_col", [DEPTH, 128, 60]); convb_d = I("convb_col", [DEPTH, 128, 12])
    alog_d = I("a_log", [DEPTH, 32]); dtb_d = I("dt_bias", [DEPTH, 32]); ssdd_d = I("ssd_d", [DEPTH, 16])
    ssdnw_d = I("ssd_nw", [DEPTH, 1024])
    wlr_d = I("wlr", [DEPTH, 16, 512]); blr_d = I("blr", [DEPTH, 512]); gnw_d = I("gla_nw", [DEPTH, 128])
    s5p_d = I("s5p", [DEPTH, 2, 128, 48])
    s5b_d = I("s5b", [DEPTH, 128, 512])
    s5c_d = I("s5c", [DEPTH, 2, 128, 512])
    s5d_d = I("s5d_col", [DEPTH, 128, 4]); gluw_d = I("glu_w", [DEPTH, 512, 1024]); glub_d = I("glu_b", [DEPTH, 1024])
    fnw_d = I("fnw", [1024]); s5dr_d = I("s5d_row", [DEPTH, 512]); cst2_d = I("consts2", [128, 1392])
    cst_d = I("consts", [128, 8 * 128])
    out_d = nc.dram_tensor("out", [LQ, D], F32, kind="ExternalOutput").ap()

    uT_d = S("uT", [D, LT], BF16)
    cat_d = (I("cat", [LT, 2048], BF16) if (dbg and not any(p in phases for p in ("ssd", "gla", "s5"))) else S("cat", [LT, 2048], BF16))
    yb_ssd_d = S("yb_ssd", [LT, 1024]); yb_gla_d = S("yb_gla", [LT, 512]); yb_s5_d = S("yb_s5T", [512, LT])
    h1_d = S("h1", [LQ, D]); hc1_d = S("hc1", [LC, D])
    u5_d = S("u5s", [LT, 512]); y5_d = [S("y5f", [LT, 512]), S("y5b", [LT, 512])]
    bc_d = S("bcT", [512, LT], BF16); xb_d = S("xbtm", [LT, 1280], BF16); dt_d = S("dtsp", [LT, 32])
    qk_d = S("qkT", [512, LT], BF16); kv_d = S("kvtm", [LT, 768], BF16); g_d = S("glog", [LT, 512])
    sz_d = S("szc", [LT, 1024]); sgg_d = S("sggc", [LT, 512])

    with ExitStack() as st:
        P = Prog(nc, st)
        uid = [0]

        def sbuf(stack, name, shape, dt=F32):
            uid[0] += 1
            return stack.enter_context(nc.sbuf_tensor(f"{name}_{uid[0]}", shape, dt))

        big = [st.enter_context(nc.psum_tensor(f"pbig{i}", [128, 1024], F32)) for i in range(4)]
        banks = [big[i // 2][:, (i % 2) * 512:(i % 2 + 1) * 512] for i in range(8)]
        BK = [f"b{i}" for i in range(8)]

        def mm(out, lhsT, rhs, rd, wr, start=True, stop=True):
            P.op("pe", lambda e: e.matmul(out, lhsT, rhs, start=start, stop=stop), rd, wr)

        def act(out, in_, func, rd, wr, bias=None, scale=None, accum=None):
            kw = {}
            if bias is not None:
                kw["bias"] = bias
            if scale is not None:
                kw["scale"] = scale
            if accum is not None:
                kw["accum_out"] = accum
            P.op("act", lambda e: e.activation(out, in_, func, **kw), rd, wr)

        def tt(eng, out, a, b, op, rd, wr):
            P.op(eng, lambda e: e.tensor_tensor(out, a, b, op), rd, wr)

        def tsc(eng, out, a, s1, s2, op0, op1, rd, wr):
            if s2 is None:
                P.op(eng, lambda e: e.tensor_scalar(out, a, s1, None, op0), rd, wr)
            else:
                P.op(eng, lambda e: e.tensor_scalar(out, a, s1, s2, op0, op1), rd, wr)

        def stt(out, in0, scalar, in1, op0, op1, rd, wr):
            P.op("dve", lambda e: e.scalar_tensor_tensor(out, in0, scalar, in1, op0, op1), rd, wr)

        def cp(eng, out, in_, rd, wr):
            if eng == "act_id":
                P.op("act", lambda e: e.activation(out, in_, AF.Identity), rd, wr)
            elif eng == "act":
                P.op("act", lambda e: e.copy(out, in_), rd, wr)
            else:
                P.op(eng, lambda e: e.tensor_copy(out, in_), rd, wr)

        def mset(eng, ap, val, wr):
            P.op(eng, lambda e: e.memset(ap, val), (), wr)

        epsc = sbuf(st, "epsc", [128, 1])
        P.op("dve", lambda e: e.memset(epsc[:], EPS), (), ["epsc"])

        def rstd(out, in_, scale, rd, wr):
            act(out, in_, AF.Ln, list(rd) + ["epsc"], wr, bias=epsc[0:out.shape[0], :], scale=scale)
            act(out, out, AF.Exp, wr, wr, scale=-0.5)

        ppc = [0]

        def dma(out, in_, stream, rd, wr, eng="sp"):
            if stream == "pp":
                ppc[0] += 1
                stream = f"pp{ppc[0] % 12}"
            P.dma(lambda e: e.dma_start(out=out, in_=in_), stream, rd, wr, eng=eng)

        cst = sbuf(st, "cst", [128, 8 * 128])
        dma(cst[:], cst_d, "cst", (), ["cst"])
        IDENT = cst[:, 0:128]; UINC = cst[:, 128:256]; LINC = cst[:, 256:384]
        USTR = cst[:, 384:512]; LSTR = cst[:, 512:640]; ONES = cst[:, 640:768]; NVEC = cst[:, 768:896]; NVECR = cst[:, 896:1024]
        identb = sbuf(st, "identb", [128, 128], BF16)
        dma(identb[:], cst_d[:, 0:128], "cstb", (), ["identb"], eng="pool")
        ccol = sbuf(st, "ccol", [128, 16])
        dma(ccol[:], ccol_d, "ccol", (), ["ccol"])
        sc = sbuf(st, "sc", [128, 16])
        act(sc[:], ccol[:], AF.Silu, ["ccol"], ["sc"])

        def hsrc(l, ci):
            if ci < 2:
                src = ctx_d if l == 0 else hc1_d
                return [(src[ci * 128:(ci + 1) * 128, :], 0, 128)]
            lc = ci - 2
            src = x_d if l == 0 else h1_d
            if l % 2 == 0:
                return [(src[lc * 128:(lc + 1) * 128, :], 0, 128)]
            v = src.rearrange("(r c) d -> c r d", c=64)
            return [(v[2 * lc], 0, 64), (v[2 * lc + 1], 64, 128)]

        def hdst(l, ci):
            if ci < 2:
                return [(hc1_d[ci * 128:(ci + 1) * 128, :], 0, 128)]
            lc = ci - 2
            dst = h1_d if l == 0 else out_d
            if l % 2 == 0:
                return [(dst[lc * 128:(lc + 1) * 128, :], 0, 128)]
            v = dst.rearrange("(r c) d -> c r d", c=64)
            return [(v[2 * lc], 0, 64), (v[2 * lc + 1], 64, 128)]

        uT_v = uT_d.rearrange("(k p) t -> p k t", p=128)

        for l in layers:
            last = (l == DEPTH - 1)
            with ExitStack() as LS:
                acol = sbuf(LS, "acol", [128, 16]); shcol = sbuf(LS, "shcol", [128, 16])
                gb = sbuf(LS, "gb", [128, 2, 1024])
                if "p0" in phases or "p4" in phases:
                    with ExitStack() as PS:
                        modT = sbuf(PS, "modT", [128, 24, 2])
                        mw = [sbuf(PS, f"mw{i}", [128, 8, 512]) for i in range(2)]
                        modbcol = sbuf(PS, "modbcol", [128, 24]); nwcol = sbuf(PS, "nwcol", [128, 8])
                        gbias = sbuf(PS, "gbias", [128, 1024])
                        dma(modbcol[:], modbcol_d[l], "pp", (), ["modbcol"])
                        dma(nwcol[:], nwcol_d[l], "pp", (), ["nwcol"])
                        dma(gbias[:], modb_d[l:l + 1, 2048:3072].partition_broadcast(128), "pp", (), ["gbias"])
                        mwv = modw_d[l].rearrange("(k p) f -> p k f", p=128)
                        sc2 = sc[:].rearrange("p (v k) -> p k v", v=2)
                        for blk in range(6):
                            t = mw[blk % 2]; tk = f"mw{blk % 2}"
                            dma(t[:], mwv[:, :, blk * 512:(blk + 1) * 512], tk, (), [tk])
                            for j in range(4 if blk < 4 else 0):
                                fc = blk * 4 + j
                                for k in range(8):
                                    mm(banks[0][:, fc * 2:fc * 2 + 2], t[:, k, j * 128:(j + 1) * 128], sc2[:, k, :],
                                       [tk, "sc"], ["b0"], start=(k == 0), stop=(k == 7))
                            if blk >= 4:
                                for v in range(2):
                                    for k in range(8):
                                        mm(banks[1 + v][:, :], sc[:, v * 8 + k:v * 8 + k + 1].to_broadcast([128, 128]), t[:, k, :],
                                           [tk, "sc"], [BK[1 + v]], start=(k == 0), stop=(k == 7))
                                    tt("dve", gb[:, v, (blk - 4) * 512:(blk - 3) * 512], banks[1 + v][:, :],
                                       gbias[:, (blk - 4) * 512:(blk - 3) * 512], ALU.add, [BK[1 + v], "gbias"], ["gb"])
                        tt("dve", modT[:, 0:16, :], banks[0][:, 0:32].rearrange("p (f v) -> p f v", v=2),
                           modbcol[:, 0:16].unsqueeze(2).to_broadcast([128, 16, 2]), ALU.add, ["b0", "modbcol"], ["modT"])
                        for v in range(2):
                            tsc("dve", acol[:, v * 8:(v + 1) * 8], modT[:, 8:16, v], 1.0, None, ALU.add, None, ["modT"], ["acol"])
                            tt("dve", acol[:, v * 8:(v + 1) * 8], acol[:, v * 8:(v + 1) * 8], nwcol[:], ALU.mult, ["acol", "nwcol"], ["acol"])
                            cp("dve", shcol[:, v * 8:(v + 1) * 8], modT[:, 0:8, v], ["modT"], ["shcol"])
                        if dbg:
                            dm = nc.dram_tensor(f"dbg_mod{l}", [128, 48 + 32], F32, kind="ExternalOutput").ap()
                            dma(dm[:, 0:32], modT[:, 0:16, :].rearrange("p f v -> p (f v)"), "dbg", ["modT"], ["dbg_mod"])
                            dma(dm[:, 48:64], acol[:], "dbg", ["acol"], ["dbg_mod1"])
                            dma(dm[:, 64:80], shcol[:], "dbg", ["shcol"], ["dbg_mod2"])
                            dg = nc.dram_tensor(f"dbg_gb{l}", [128, 2048], F32, kind="ExternalOutput").ap()
                            dma(dg, gb[:].rearrange("p v n -> p (v n)"), "dbg", ["gb"], ["dbg_gb"])
                        if "p0" in phases:
                            ht = [sbuf(PS, f"ht{i}", [128, 1024]) for i in range(3)]
                            hn2 = [sbuf(PS, f"hn{i}", [128, 1024]) for i in range(2)]; sqj2 = [sbuf(PS, f"sqj{i}", [128, 1024]) for i in range(2)]
                            ssqp2 = [sbuf(PS, f"ssq{i}", [128, 2]) for i in range(2)]; uTt = [sbuf(PS, f"uTt{i}", [128, 8, 128], BF16) for i in range(2)]

                            def load_h(ci):
                                t_ = ht[ci % 3]; k_ = f"ht{ci % 3}"
                                for (ap, p0, p1) in hsrc(l, ci):
                                    dma(t_[p0:p1, :], ap, k_, [f"h{l}_{ci}"], [k_])

                            def p0s1(ci):
                                t_ = ht[ci % 3]; k_ = f"ht{ci % 3}"
                                p_ = ci % 2; hn = hn2[p_]; khn = f"hn{p_}"; sq = sqj2[p_]; ksq = f"sqj{p_}"; ssq = ssqp2[p_]; kss = f"ssq{p_}"
                                mset("dve", ssq[:, 0:1], 0.0, [kss])
                                act(sq[:], t_[:], AF.Square, [k_, kss], [ksq, kss], accum=ssq[:, 0:1])
                                rstd(ssq[:, 1:2], ssq[:, 0:1], 1.0 / D, [kss], [kss])
                                tsc("dve", hn[:], t_[:], ssq[:, 1:2], None, ALU.mult, None, [k_, kss], [khn])

                            def p0s2(ci):
                                v = 1 if ci < 2 else 0
                                p_ = ci % 2; hn = hn2[p_]; khn = f"hn{p_}"
                                for k in range(8):
                                    bk = 2 * p_ + k // 4
                                    mm(banks[bk][:, (k % 4) * 128:(k % 4 + 1) * 128], hn[:, k * 128:(k + 1) * 128], IDENT,
                                       [khn, "cst"], [BK[bk]])
                                u_ = uTt[ci % 2]; uk = f"uTt{ci % 2}"
                                for k in range(8):
                                    bk = 2 * p_ + k // 4
                                    src_ = banks[bk][:, (k % 4) * 128:(k % 4 + 1) * 128]
                                    a_ = acol[:, v * 8 + k:v * 8 + k + 1]; b_ = shcol[:, v * 8 + k:v * 8 + k + 1]
                                    if k // 4 == 0:
                                        act(u_[:, k, :], src_, AF.Identity, [BK[bk], "acol", "shcol"], [uk], bias=b_, scale=a_)
                                    else:
                                        tsc("dve", u_[:, k, :], src_, a_, b_, ALU.mult, ALU.add, [BK[bk], "acol", "shcol"], [uk])
                                dma(uT_v[:, :, ci * 128:(ci + 1) * 128], u_[:], "uTst", [uk], [f"uT{ci}"])
                            load_h(0); load_h(1)
                            p0s1(0)
                            for ci in range(NCH):
                                if ci + 2 < NCH:
                                    load_h(ci + 2)
                                if ci + 1 < NCH:
                                    p0s1(ci + 1)
                                p0s2(ci)
                        P.barrier()

                def phase_p4():
                    with ExitStack() as PS:
                        wout = sbuf(PS, "wout", [128, 16, 1024], BF16)
                        dma(wout[:], wout_d[l].rearrange("(j p) n -> p j n", p=128), "wout", (), ["wout"], eng="pool")
                        ct = [sbuf(PS, f"ct{i}", [128, 2048], BF16) for i in range(2)]
                        hr = [sbuf(PS, f"hr{i}", [128, 1024]) for i in range(2)]
                        catT2 = [sbuf(PS, f"catT{i}", [128, 16, 128], BF16) for i in range(2)]
                        hnew2 = [sbuf(PS, f"hnew{i}", [128, 1024]) for i in range(2)]; sq2 = [sbuf(PS, f"sq4{i}", [128, 1024]) for i in range(2)]
                        ssq2 = [sbuf(PS, f"ssq4{i}", [128, 2]) for i in range(2)]
                        fnwb = sbuf(PS, "fnwb", [128, 1024])
                        if last:
                            dma(fnwb[:], fnw_d.unsqueeze(0).partition_broadcast(128), "pp", (), ["fnwb"])
                        chunks = list(range(2, NCH)) if last else list(range(NCH))

                        def load(i):
                            ci = chunks[i]
                            dma(ct[i % 2][:], cat_d[ci * 128:(ci + 1) * 128, :], f"ct{i % 2}",
                                [f"cat_ssd{ci}", f"cat_gla{ci}", f"cat_s5{ci}"], [f"ct{i % 2}"])
                            for (ap, p0, p1) in hsrc(l, ci):
                                dma(hr[i % 2][p0:p1, :], ap, f"hr{i % 2}", [f"h{l}_{ci}"], [f"hr{i % 2}"])
                        load(0)
                        for i, ci in enumerate(chunks):
                            if i + 1 < len(chunks):
                                load(i + 1)
                            p_ = i % 2
                            c_ = ct[p_]; ck = f"ct{p_}"; h_ = hr[p_]; hk = f"hr{p_}"; v = 1 if ci < 2 else 0
                            catT = catT2[p_]; kc = f"catT{p_}"; hnew = hnew2[p_]; kh = f"hnew{p_}"; sq = sq2[p_]; ksq = f"sq4{p_}"; ssq = ssq2[p_]; kss = f"ssq4{p_}"
                            TB = (0, 1) if p_ == 0 else (2, 3); OB = (4, 5) if p_ == 0 else (6, 7)
                            for r in range(2):
                                for jj in range(8):
                                    j = r * 8 + jj; bk = TB[jj // 4]
                                    mm(banks[bk][:, (jj % 4) * 128:(jj % 4 + 1) * 128], c_[:, j * 128:(j + 1) * 128], identb[:], [ck, "identb"], [BK[bk]])
                                for q in range(2):
                                    cp("act" if q % 2 else "dve", catT[:, r * 8 + q * 4:r * 8 + (q + 1) * 4, :].rearrange("p a b -> p (a b)"), banks[TB[q]][:, :],
                                       [BK[TB[q]]], [kc])
                            for nb in range(2):
                                for j in range(16):
                                    mm(banks[OB[nb]][:, :], catT[:, j, :], wout[:, j, nb * 512:(nb + 1) * 512], [kc, "wout"],
                                       [BK[OB[nb]]], start=(j == 0), stop=(j == 15))
                            for nb in range(2):
                                tt("dve", hnew[:, nb * 512:(nb + 1) * 512], banks[OB[nb]][:, :], gb[:, v, nb * 512:(nb + 1) * 512],
                                   ALU.mult, [BK[OB[nb]], "gb"], [kh])
                            tt("pool", hnew[:], hnew[:], h_[:], ALU.add, [kh, hk], [kh])
                            if last:
                                mset("dve", ssq[:, 0:1], 0.0, [kss])
                                act(sq[:], hnew[:], AF.Square, [kh, kss], [ksq, kss], accum=ssq[:, 0:1])
                                rstd(ssq[:, 1:2], ssq[:, 0:1], 1.0 / D, [kss], [kss])
                                stt(hnew[:], hnew[:], ssq[:, 1:2], fnwb[:], ALU.mult, ALU.mult, [kh, kss, "fnwb"], [kh])
                            for (ap, p0, p1) in hdst(l, ci):
                                dma(ap, hnew[p0:p1, :], f"hst{p_}", [kh], [f"h{l + 1}_{ci}"])
                        P.barrier()


                def load_uc(tile, key, ci, halo):
                    s0 = ci * 128
                    lo, hi = (0, LC) if ci < 2 else (LC, LT)
                    a, b = max(s0 - halo, lo), min(s0 + 128 + halo, hi)
                    if halo and (a > s0 - halo):
                        mset("pool", tile[:, :, 0:halo], 0.0, [key])
                    if halo and (b < s0 + 128 + halo):
                        mset("pool", tile[:, :, 128 + halo:128 + 2 * halo], 0.0, [key])
                    c0 = a - (s0 - halo)
                    rd = [f"uT{c}" for c in range(max(ci - 1, 0), min(ci + 2, NCH))]
                    dma(tile[:, :, c0:c0 + (b - a)], uT_v[:, :, a:b], key, rd, [key])

                def phase_ssd():
                    wv = win_d[l].rearrange("(k p) n -> p k n", p=128)
                    bc_v = bc_d.rearrange("(j p) t -> p j t", p=128)
                    with ExitStack() as PA:
                        wxbc = sbuf(PA, "wxbc", [128, 8, 1536], BF16); wdt = sbuf(PA, "wdt", [128, 8, 32], BF16)
                        wz = sbuf(PA, "wz", [128, 8, 1024], BF16); szt = [sbuf(PA, f"szt{i}", [128, 1024]) for i in range(2)]
                        dma(wz[:], wv[:, :, OZ:OZ + 1024], "w1", (), ["wz"], eng="pool")
                        dma(wxbc[:], wv[:, :, OXBC:OXBC + 1536], "w2", (), ["wxbc"], eng="pool")
                        dma(wdt[:], wv[:, :, ODT:ODT + 32], "w3", (), ["wdt"], eng="pool")
                        convw = sbuf(PA, "convw", [128, 60]); convb = sbuf(PA, "convb", [128, 12])
                        dma(convw[:], convw_d[l], "pp", (), ["convw"]); dma(convb[:], convb_d[l], "pp", (), ["convb"])
                        cdiag = sbuf(PA, "cdiag", [128, 60, 128], BF16)
                        for i in range(60):
                            if i % 2:
                                act(cdiag[:, i, :], IDENT, AF.Copy, ["cst", "convw"], ["cdiag"], scale=convw[:, i:i + 1])
                            else:
                                tsc("dve", cdiag[:, i, :], IDENT, convw[:, i:i + 1], None, ALU.mult, None, ["cst", "convw"], ["cdiag"])
                        dtb = sbuf(PA, "dtb", [128, 32])
                        dma(dtb[:], dtb_d[l:l + 1, :].partition_broadcast(128), "pp", (), ["dtb"])
                        uts = [sbuf(PA, f"auts{i}", [128, 8, 260], BF16) for i in range(2)]
                        xpre = sbuf(PA, "xpre", [128, 12, 260], BF16); xT = sbuf(PA, "xT", [128, 12, 256], BF16)
                        xbt = [sbuf(PA, f"xbt{i}", [128, 1280], BF16) for i in range(2)]
                        dts = sbuf(PA, "dts", [128, 2, 32])
                        NSC = LT // 256

                        def load_sc(i):
                            s0 = i * 256; t_ = uts[i % 2]; k_ = f"auts{i % 2}"
                            lo, hi = (0, LC) if i == 0 else (LC, LT)
                            a_, b_ = max(s0 - 2, lo), min(s0 + 258, hi)
                            if a_ > s0 - 2:
                                mset("pool", t_[:, :, 0:2], 0.0, [k_])
                            if b_ < s0 + 258:
                                mset("pool", t_[:, :, 258:260], 0.0, [k_])
                            c0 = a_ - (s0 - 2)
                            rd = [f"uT{c}" for c in range(max(2 * i - 1, 0), min(2 * i + 3, NCH))]
                            dma(t_[:, :, c0:c0 + (b_ - a_)], uT_v[:, :, a_:b_], k_, rd, [k_])
                        load_sc(0)
                        for i in range(NSC):
                            if i + 1 < NSC:
                                load_sc(i + 1)
                            s0 = i * 256; u_ = uts[i % 2]; uk = f"auts{i % 2}"
                            for j in range(12):
                                bk = j % 2
                                for k in range(8):
                                    mm(banks[bk][:, 0:260], wxbc[:, k, j * 128:(j + 1) * 128], u_[:, k, :], ["wxbc", uk], [BK[bk]],
                                       start=(k == 0), stop=(k == 7))
                                cp("act" if j % 2 else "dve", xpre[:, j, :], banks[bk][:, 0:260], [BK[bk]], ["xpre"])
                            for j in range(12):
                                bk = 2 + j % 2
                                for k in range(5):
                                    mm(banks[bk][:, 0:256], cdiag[:, j * 5 + k, :], xpre[:, j, k:k + 256], ["cdiag", "xpre"], [BK[bk]],
                                       start=(k == 0), stop=(k == 4))
                                act(xT[:, j, :], banks[bk][:, 0:256], AF.Silu, [BK[bk], "convb"], ["xT"], bias=convb[:, j:j + 1])
                            dma(bc_v[:, :, s0:s0 + 256], xT[:, 8:12, :], "bcst", ["xT"], [f"bc{i}"])
                            for hf in range(2):
                                xb_ = xbt[hf]; xk = f"xbt{hf}"
                                for j in range(8):
                                    bk = 4 + j // 4
                                    mm(banks[bk][:, (j % 4) * 128:(j % 4 + 1) * 128], xT[:, j, hf * 128:(hf + 1) * 128], identb[:], ["xT", "identb"], [BK[bk]])
                                for g in range(2):
                                    mm(banks[6][:, g * 128:(g + 1) * 128], xT[:, 8 + g, hf * 128:(hf + 1) * 128], identb[:], ["xT", "identb"], ["b6"])
                                cp("act", xb_[:, 0:512], banks[4][:, :], ["b4"], [xk])
                                cp("dve", xb_[:, 512:1024], banks[5][:, :], ["b5"], [xk])
                                cp("dve", xb_[:, 1024:1280], banks[6][:, 0:256], ["b6"], [xk])
                                dma(xb_d[s0 + hf * 128:s0 + (hf + 1) * 128, :], xb_[:], f"xbst{hf}", [xk], [f"xb{2 * i + hf}"])
                                for k in range(8):
                                    mm(banks[7][:, hf * 32:(hf + 1) * 32], u_[:, k, 2 + hf * 128:2 + (hf + 1) * 128], wdt[:, k, :], [uk, "wdt"], ["b7"],
                                       start=(k == 0), stop=(k == 7))
                                for nb in range(2):
                                    for k in range(8):
                                        mm(banks[nb][:, :], u_[:, k, 2 + hf * 128:2 + (hf + 1) * 128], wz[:, k, nb * 512:(nb + 1) * 512], [uk, "wz"], [BK[nb]],
                                           start=(k == 0), stop=(k == 7))
                                    act(szt[hf][:, nb * 512:(nb + 1) * 512], banks[nb][:, :], AF.Silu, [BK[nb]], [f"szt{hf}"])
                                dma(sz_d[s0 + hf * 128:s0 + (hf + 1) * 128, :], szt[hf][:], f"szst{hf}", [f"szt{hf}"], [f"sz{2 * i + hf}"])
                            tt("dve", dts[:], banks[7][:, 0:64].rearrange("p (a b) -> p a b", a=2), dtb[:].unsqueeze(1).to_broadcast([128, 2, 32]), ALU.add,
                               ["b7", "dtb"], ["dts"])
                            act(dts[:], dts[:], AF.Exp, ["dts"], ["dts"])
                            act(dts[:], dts[:], AF.Ln, ["dts"], ["dts"], bias=1.0)
                            dma(dt_d[s0:s0 + 256, :].rearrange("(a p) c -> p a c", p=128), dts[:], "dtst", ["dts"], [f"dt{2 * i}", f"dt{2 * i + 1}"])
                        P.barrier()
                    with ExitStack() as PS:
                        aneg = sbuf(PS, "aneg", [128, 32]); dskb = sbuf(PS, "dskb", [128, 16]); nwb = sbuf(PS, "nwb", [128, 1024])
                        dma(aneg[:], alog_d[l:l + 1, :].partition_broadcast(128), "pp", (), ["aneg"])
                        dma(dskb[:], ssdd_d[l:l + 1, :].partition_broadcast(128), "pp", (), ["dskb"])
                        dma(nwb[:], ssdnw_d[l:l + 1, :].partition_broadcast(128), "pp", (), ["nwb"])
                        act(aneg[:], aneg[:], AF.Exp, ["aneg"], ["aneg"])
                        tsc("dve", aneg[:], aneg[:], -1.0, None, ALU.mult, None, ["aneg"], ["aneg"])
                        uc = [sbuf(PS, f"uc{i}", [128, 1024]) for i in range(3)]
                        bct = [sbuf(PS, f"bct{i}", [128, 4, 128], BF16) for i in range(3)]
                        xbl = [sbuf(PS, f"xbl{i}", [128, 1280], BF16) for i in range(3)]
                        dtl = [sbuf(PS, f"dtl{i}", [128, 32]) for i in range(3)]
                        xw2 = [sbuf(PS, f"xw{i}", [128, 1024], BF16) for i in range(2)]
                        xdt = sbuf(PS, "xdt", [128, 1024], BF16)
                        da = sbuf(PS, "da", [128, 16]); ex2 = [sbuf(PS, f"ex{i}", [128, 48]) for i in range(2)]; wsc = sbuf(PS, "wsc", [128, 16])
                        cbm = sbuf(PS, "cbm", [128, 256])
                        lh = [sbuf(PS, f"lh{i}", [128, 128]) for i in range(8)]
                        E = [sbuf(PS, f"E{i}", [128, 512]) for i in range(2)]
                        ST = sbuf(PS, "ST", [128, 16, 128], BF16)
                        H = sbuf(PS, "H", [128, 1024]); Hb = sbuf(PS, "Hb", [128, 1024], BF16)
                        ydir = sbuf(PS, "ydir", [128, 1024]); ybl = sbuf(PS, "ybl", [128, 1024]); sz = sbuf(PS, "sz", [128, 1024])
                        tmp = sbuf(PS, "tmp", [128, 1024]); ss2 = sbuf(PS, "ss2", [128, 4]); cto = sbuf(PS, "cto", [128, 1024], BF16)
                        for d in (1, 0):
                            INC, STR = (UINC, USTR) if d == 0 else (LINC, LSTR)
                            order = sweep_order(d)
                            mset("dve", H[:], 0.0, ["H"]); mset("pool", Hb[:], 0.0, ["Hb"])

                            def loads(i):
                                ci = order[i]; s0 = ci * 128; b_ = i % 3
                                fin_ = (d == 0) and not (last and ci < 2)
                                dma(bct[b_][:], bc_v[:, :, s0:s0 + 128], f"bct{b_}", [f"bc{ci // 2}"], [f"bct{b_}"])
                                dma(xbl[b_][:], xb_d[s0:s0 + 128, :], f"xbl{b_}", [f"xb{ci}"], [f"xbl{b_}"])
                                dma(dtl[b_][:], dt_d[s0:s0 + 128, :], f"dtl{b_}", [f"dt{ci}"], [f"dtl{b_}"])
                                if fin_:
                                    dma(uc[b_][:], sz_d[s0:s0 + 128, :], f"uc{b_}", [f"sz{ci}"], [f"uc{b_}"])

                            def nm(i):
                                b_ = i % 3; p_ = i % 2
                                return dict(bc_=bct[b_], bck=f"bct{b_}", xb_=xbl[b_], xk=f"xbl{b_}", dt=dtl[b_], dk=f"dtl{b_}", u_=uc[b_], uk=f"uc{b_}",
                                            ex=ex2[p_], kex=f"ex{p_}", xw=xw2[p_], kxw=f"xw{p_}", YB=((2, 3) if p_ == 0 else (4, 5)))

                            def ss1(i):
                                n = nm(i); bc_, bck, xb_, xk, dt, dk, ex, kex, xw, kxw, YB = (n[k_] for k_ in ("bc_", "bck", "xb_", "xk", "dt", "dk", "ex", "kex", "xw", "kxw", "YB"))
                                xtm = xb_[:, 0:1024]
                                tt("dve", da[:], dt[:, d * 16:(d + 1) * 16], aneg[:, d * 16:(d + 1) * 16], ALU.mult, [dk, "aneg"], ["da"])
                                mm(banks[7][:, 32:48], INC, da[:], ["cst", "da"], ["b7"])
                                mm(banks[7][:, 48:64], STR, da[:], ["cst", "da"], ["b7"])
                                mm(banks[7][:, 64:80], ONES, da[:], ["cst", "da"], ["b7"])
                                act(ex[:], banks[7][:, 32:80], AF.Exp, ["b7"], [kex])
                                tt("dve", wsc[:], ex[:, 16:32], dt[:, d * 16:(d + 1) * 16], ALU.mult, [kex, dk], ["wsc"])
                                tt("pool", xw[:].rearrange("p (h q) -> p h q", h=16), xtm.rearrange("p (h q) -> p h q", h=16),
                                   wsc[:].unsqueeze(2).to_broadcast([128, 16, 64]), ALU.mult, [xk, "wsc"], [kxw])
                                tt("pool", xdt[:].rearrange("p (h q) -> p h q", h=16), xtm.rearrange("p (h q) -> p h q", h=16),
                                   dt[:, d * 16:(d + 1) * 16].unsqueeze(2).to_broadcast([128, 16, 64]), ALU.mult, [xk, dk], ["xdt"])
                                for g in range(2):
                                    mm(banks[6][:, 256 + g * 128:256 + (g + 1) * 128], bc_[:, g, :], bc_[:, 2 + g, :], [bck], ["b6"])
                                tt("dve", cbm[:].rearrange("p (g q) -> p g q", g=2), banks[6][:, 256:512].rearrange("p (g q) -> p g q", g=2),
                                   INC.unsqueeze(1).to_broadcast([128, 2, 128]), ALU.mult, ["b6", "cst"], ["cbm"])

                                def grp_lh(gq):
                                    for h in range(gq * 4, gq * 4 + 4):
                                        li = h % 8; bk = gq % 2
                                        act(lh[li][:], STR, AF.Copy, ["cst", "da"], [f"lh{li}"], scale=da[:, h:h + 1])
                                        mm(banks[bk][:, (h % 4) * 128:(h % 4 + 1) * 128], lh[li][:], INC, [f"lh{li}", "cst"], [BK[bk]])

                                def grp_E(gq):
                                    bk = gq % 2; g = gq // 2
                                    act(E[bk][:], banks[bk][:, :], AF.Exp, [BK[bk]], [f"E{bk}"])
                                    tt("dve", ST[:, gq * 4:(gq + 1) * 4, :], E[bk][:].rearrange("p (h q) -> p h q", h=4),
                                       cbm[:, g * 128:(g + 1) * 128].unsqueeze(1).to_broadcast([128, 4, 128]), ALU.mult, [f"E{bk}", "cbm"], ["ST"])
                                grp_lh(0); grp_lh(1); grp_E(0); grp_lh(2); grp_E(1); grp_lh(3); grp_E(2); grp_E(3)

                            def ss1b(i):
                                YB = nm(i)["YB"]
                                for h in range(16):
                                    bk = YB[h // 8]
                                    mm(banks[bk][:, (h % 8) * 64:(h % 8 + 1) * 64], ST[:, h, :], xdt[:, h * 64:(h + 1) * 64], ["ST", "xdt"], [BK[bk]])

                            def ss2f(i):
                                ci = order[i]; s0 = ci * 128
                                n = nm(i); bc_, bck, xb_, xk, u_, uk, ex, kex, xw, kxw, YB = (n[k_] for k_ in ("bc_", "bck", "xb_", "xk", "u_", "uk", "ex", "kex", "xw", "kxw", "YB"))
                                xtm = xb_[:, 0:1024]; btm = xb_[:, 1024:1280]
                                fin = (d == 0) and not (last and ci < 2)
                                for g in range(2):
                                    mm(banks[6 + g][:, :], bc_[:, 2 + g, :], Hb[:, g * 512:(g + 1) * 512], [bck, "Hb"], [BK[6 + g]])
                                for g in range(2):
                                    tt("dve", ydir[:, g * 512:(g + 1) * 512].rearrange("p (h q) -> p h q", h=8),
                                       banks[6 + g][:, :].rearrange("p (h q) -> p h q", h=8),
                                       ex[:, g * 8:(g + 1) * 8].unsqueeze(2).to_broadcast([128, 8, 64]), ALU.mult, [BK[6 + g], kex], ["ydir"])
                                    tt("dve", ydir[:, g * 512:(g + 1) * 512], ydir[:, g * 512:(g + 1) * 512], banks[YB[g]][:, :], ALU.add,
                                       ["ydir", BK[YB[g]]], ["ydir"])
                                for g in range(2):
                                    mm(banks[6 + g][:, :], btm[:, g * 128:(g + 1) * 128], xw[:, g * 512:(g + 1) * 512], [xk, kxw], [BK[6 + g]])
                                tt("pool", H[:].rearrange("p (h q) -> p h q", h=16), H[:].rearrange("p (h q) -> p h q", h=16),
                                   ex[:, 32:48].unsqueeze(2).to_broadcast([128, 16, 64]), ALU.mult, ["H", kex], ["H"])
                                for g in range(2):
                                    tt("dve", H[:, g * 512:(g + 1) * 512], H[:, g * 512:(g + 1) * 512], banks[6 + g][:, :], ALU.add,
                                       ["H", BK[6 + g]], ["H"])
                                cp("act", Hb[:], H[:], ["H"], ["Hb"])
                                if d == 1:
                                    dma(yb_ssd_d[s0:s0 + 128, :], ydir[:], "ybst", ["ydir"], [f"ybs{ci}"])
                                elif fin:
                                    dma(ybl[:], yb_ssd_d[s0:s0 + 128, :], "ybld", [f"ybs{ci}"], ["ybl"])
                                    tt("dve", ydir[:], ydir[:], ybl[:], ALU.add, ["ydir", "ybl"], ["ydir"])
                                    tt("pool", tmp[:].rearrange("p (h q) -> p h q", h=16), xtm.rearrange("p (h q) -> p h q", h=16),
                                       dskb[:].unsqueeze(2).to_broadcast([128, 16, 64]), ALU.mult, [xk, "dskb"], ["tmp"])
                                    tt("dve", ydir[:], ydir[:], tmp[:], ALU.add, ["ydir", "tmp"], ["ydir"])
                                    tt("dve", ydir[:], ydir[:], u_[:], ALU.mult, ["ydir", uk], ["ydir"])
                                    mset("dve", ss2[:], 0.0, ["ss2"])
                                    for g in range(2):
                                        act(tmp[:, g * 512:(g + 1) * 512], ydir[:, g * 512:(g + 1) * 512], AF.Square, ["ydir", "ss2"], ["tmp", "ss2"],
                                            accum=ss2[:, g:g + 1])
                                    rstd(ss2[:, 2:4], ss2[:, 0:2], 1.0 / 512, ["ss2"], ["ss2"])
                                    for g in range(2):
                                        stt(cto[:, g * 512:(g + 1) * 512], ydir[:, g * 512:(g + 1) * 512], ss2[:, 2 + g:3 + g],
                                            nwb[:, g * 512:(g + 1) * 512], ALU.mult, ALU.mult, ["ydir", "ss2", "nwb"], ["cto"])
                                    dma(cat_d[s0:s0 + 128, 0:1024], cto[:], "catst", ["cto"], [f"cat_ssd{ci}"])
                            n_ = len(order)
                            loads(0); loads(1)
                            ss1(0); ss1b(0)
                            for i in range(n_):
                                if i + 2 < n_:
                                    loads(i + 2)
                                if i + 1 < n_:
                                    ss1(i + 1)
                                ss2f(i)
                                if i + 1 < n_:
                                    ss1b(i + 1)
                        P.barrier()

                def phase_gla():
                    wv = win_d[l].rearrange("(k p) n -> p k n", p=128)
                    qk_v = qk_d.rearrange("(j p) t -> p j t", p=128)
                    with ExitStack() as PA:
                        wq = sbuf(PA, "wq", [128, 8, 256], BF16); wk = sbuf(PA, "wk", [128, 8, 256], BF16)
                        wvv = sbuf(PA, "wvv", [128, 8, 512], BF16); wlr = sbuf(PA, "wlr", [128, 8, 32], BF16)
                        wgg = sbuf(PA, "wgg", [128, 8, 512], BF16); sgt2 = [sbuf(PA, f"sgt{i}", [128, 512]) for i in range(2)]
                        dma(wgg[:], wv[:, :, OGG:OGG + 512], "w3", (), ["wgg"], eng="pool")
                        dma(wq[:], wv[:, :, OQ:OQ + 256], "w1", (), ["wq"], eng="pool")
                        dma(wk[:], wv[:, :, OK_:OK_ + 256], "w2", (), ["wk"], eng="pool")
                        dma(wvv[:], wv[:, :, OV:OV + 512], "w3", (), ["wvv"], eng="pool")
                        dma(wlr[:], wv[:, :, OLR:OLR + 32], "w2", (), ["wlr"], eng="pool")
                        wlrp = sbuf(PA, "wlrp", [16, 512]); blrb = sbuf(PA, "blrb", [128, 512])
                        dma(wlrp[:], wlr_d[l], "pp", (), ["wlrp"])
                        dma(blrb[:], blr_d[l:l + 1, :].partition_broadcast(128), "pp", (), ["blrb"])
                        uts = [sbuf(PA, f"guts{i}", [128, 8, 256], BF16) for i in range(2)]
                        qkT = [sbuf(PA, f"qkT{i}", [128, 4, 256], BF16) for i in range(2)]
                        kvt = [sbuf(PA, f"kvt{i}", [128, 768], BF16) for i in range(2)]
                        lrT = sbuf(PA, "lrT", [16, 4, 128]); gsb = [sbuf(PA, f"gsb{i}", [128, 512]) for i in range(2)]
                        NSC = LT // 256

                        def load_sc(i):
                            s0 = i * 256
                            rd = [f"uT{2 * i}", f"uT{2 * i + 1}"]
                            dma(uts[i % 2][:], uT_v[:, :, s0:s0 + 256], f"guts{i % 2}", rd, [f"guts{i % 2}"])
                        load_sc(0)
                        for i in range(NSC):
                            if i + 1 < NSC:
                                load_sc(i + 1)
                            s0 = i * 256; u_ = uts[i % 2]; uk = f"guts{i % 2}"; qk_ = qkT[i % 2]; qkk = f"qkT{i % 2}"
                            for j in range(4):
                                w_ = wq if j < 2 else wk; c2 = j % 2; bk = j % 2
                                for k in range(8):
                                    mm(banks[bk][:, 0:256], w_[:, k, c2 * 128:(c2 + 1) * 128], u_[:, k, :], ["wq", "wk", uk], [BK[bk]], start=(k == 0), stop=(k == 7))
                                cp("act" if j % 2 else "dve", qk_[:, j, :], banks[bk][:, 0:256], [BK[bk]], [qkk])
                            dma(qk_v[:, :, s0:s0 + 256], qk_[:], f"qkst{i % 2}", [qkk], [f"qk{i}"])
                            for hf in range(2):
                                kv_ = kvt[hf]; kvk = f"kvt{hf}"; gs_ = gsb[hf]; gk = f"gsb{hf}"
                                usl = slice(hf * 128, (hf + 1) * 128)
                                for k in range(8):
                                    mm(banks[2][:, 0:256], u_[:, k, usl], wk[:, k, :], [uk, "wk"], ["b2"], start=(k == 0), stop=(k == 7))
                                for k in range(8):
                                    mm(banks[3][:, :], u_[:, k, usl], wvv[:, k, :], [uk, "wvv"], ["b3"], start=(k == 0), stop=(k == 7))
                                cp("dve", kv_[:, 0:256], banks[2][:, 0:256], ["b2"], [kvk])
                                cp("act", kv_[:, 256:768], banks[3][:, :], ["b3"], [kvk])
                                for k in range(8):
                                    mm(banks[7][:, :], u_[:, k, usl], wgg[:, k, :], [uk, "wgg"], ["b7"], start=(k == 0), stop=(k == 7))
                                act(sgt2[hf][:], banks[7][:, :], AF.Silu, ["b7"], [f"sgt{hf}"])
                                dma(sgg_d[s0 + hf * 128:s0 + (hf + 1) * 128, :], sgt2[hf][:], f"sgst{hf}", [f"sgt{hf}"], [f"sgg{2 * i + hf}"])
                                dma(kv_d[s0 + hf * 128:s0 + (hf + 1) * 128, :], kv_[:], f"kvst{hf}", [kvk], [f"kv{2 * i + hf}"])
                                for dd in range(2):
                                    for k in range(8):
                                        mm(banks[4][0:16, (hf * 2 + dd) * 128:(hf * 2 + dd + 1) * 128], wlr[:, k, dd * 16:(dd + 1) * 16], u_[:, k, usl], ["wlr", uk], ["b4"],
                                           start=(k == 0), stop=(k == 7))
                            cp("dve", lrT[:].rearrange("p a b -> p (a b)"), banks[4][0:16, :], ["b4"], ["lrT"])
                            for hf in range(2):
                                gs_ = gsb[hf]; gk = f"gsb{hf}"
                                for dd in range(2):
                                    mm(banks[5 + hf][:, dd * 256:(dd + 1) * 256], lrT[:, hf * 2 + dd, :], wlrp[:, dd * 256:(dd + 1) * 256], ["lrT", "wlrp"], [BK[5 + hf]])
                                tt("dve", gs_[:], banks[5 + hf][:, :], blrb[:], ALU.add, [BK[5 + hf], "blrb"], [gk])
                                act(gs_[:], gs_[:], AF.Exp, [gk], [gk], scale=-1.0)
                                act(gs_[:], gs_[:], AF.Ln, [gk], [gk], bias=1.0)
                                tsc("dve", gs_[:], gs_[:], -1.0 / 16.0, None, ALU.mult, None, [gk], [gk])
                                dma(g_d[s0 + hf * 128:s0 + (hf + 1) * 128, :], gs_[:], f"gst{hf}", [gk], [f"gg{2 * i + hf}"])
                        P.barrier()
                    with ExitStack() as PS:
                        gnwb = sbuf(PS, "gnwb", [128, 128])
                        dma(gnwb[:], gnw_d[l:l + 1, :].partition_broadcast(128), "pp", (), ["gnwb"])
                        uc = [sbuf(PS, f"guc{i}", [128, 512]) for i in range(3)]
                        qkl = [sbuf(PS, f"qkl{i}", [128, 4, 128], BF16) for i in range(3)]
                        kvl = [sbuf(PS, f"kvl{i}", [128, 768], BF16) for i in range(3)]
                        gl = [sbuf(PS, f"gl{i}", [128, 256]) for i in range(3)]
                        eb2 = [sbuf(PS, f"eb{i}", [128, 256]) for i in range(2)]; enb = sbuf(PS, "enb", [128, 256]); er = sbuf(PS, "er", [128, 256])
                        kd2 = [sbuf(PS, f"kd{i}", [128, 256], BF16) for i in range(2)]
                        STg2 = [sbuf(PS, f"STg{i}", [128, 4, 128], BF16) for i in range(2)]
                        Sg = sbuf(PS, "Sg", [128, 2, 128]); Sgb = sbuf(PS, "Sgb", [128, 2, 128], BF16)
                        od = sbuf(PS, "od", [128, 512]); obl = sbuf(PS, "obl", [128, 512]); sgt = sbuf(PS, "sgt", [128, 512])
                        qeTz2 = [sbuf(PS, f"qeTz{i}", [128, 4, 128], BF16) for i in range(2)]; keTz = sbuf(PS, "keTz", [128, 4, 128], BF16)
                        for i in range(2):
                            mset("pool", qeTz2[i][:], 0.0, [f"qeTz{i}"])
                        mset("pool", keTz[:], 0.0, ["keTz"])
                        er2 = sbuf(PS, "er2", [128, 256]); lnq = sbuf(PS, "lnq", [128, 1])
                        mset("dve", lnq[:], math.log(0.125), ["lnq"])
                        tmp = sbuf(PS, "gtmp", [128, 512]); ss4 = sbuf(PS, "ss4", [128, 8]); cto = sbuf(PS, "gcto", [128, 512], BF16)
                        for d in (1, 0):
                            INC, STR = (UINC, USTR) if d == 0 else (LINC, LSTR)
                            lastcol = 127 if d == 0 else 0
                            order = sweep_order(d)
                            mset("dve", Sg[:], 0.0, ["Sg"]); mset("pool", Sgb[:], 0.0, ["Sgb"])

                            def gloads(i):
                                ci = order[i]; s0 = ci * 128; b_ = i % 3
                                fin_ = (d == 0) and not (last and ci < 2)
                                dma(qkl[b_][:], qk_v[:, :, s0:s0 + 128], f"qkl{b_}", [f"qk{ci // 2}"], [f"qkl{b_}"])
                                dma(kvl[b_][:], kv_d[s0:s0 + 128, :], f"kvl{b_}", [f"kv{ci}"], [f"kvl{b_}"])
                                dma(gl[b_][:], g_d[s0:s0 + 128, d * 256:(d + 1) * 256], f"gl{b_}", [f"gg{ci}"], [f"gl{b_}"])
                                if fin_:
                                    dma(uc[b_][:], sgg_d[s0:s0 + 128, :], f"guc{b_}", [f"sgg{ci}"], [f"guc{b_}"])

                            def gs1(i):
                                p_ = i % 2; b_ = i % 3
                                qk_ = qkl[b_]; qkk = f"qkl{b_}"; kv_ = kvl[b_]; kvk = f"kvl{b_}"; g_ = gl[b_]; gk = f"gl{b_}"
                                eb = eb2[p_]; keb = f"eb{p_}"; kd = kd2[p_]; kkd = f"kd{p_}"
                                STg = STg2[p_]; kst = f"STg{p_}"; qeTz = qeTz2[p_]; kq = f"qeTz{p_}"
                                for c2 in range(2):
                                    mm(banks[4][:, c2 * 128:(c2 + 1) * 128], g_[:, c2 * 128:(c2 + 1) * 128], INC, [gk, "cst"], ["b4"])
                                mm(banks[4][:, 256:512], STR, g_[:], ["cst", gk], ["b4"])
                                act(eb[:], banks[4][:, 0:256], AF.Exp, ["b4"], [keb])
                                act(enb[:], banks[4][:, 0:256], AF.Exp, ["b4"], ["enb"], scale=-1.0)
                                act(er[:], banks[4][:, 256:512], AF.Exp, ["b4"], ["er"])
                                act(er2[:], banks[4][:, 0:256], AF.Exp, ["b4", "lnq"], ["er2"], bias=lnq[:, 0:1])
                                for h in range(4):
                                    c2 = h // 2; hb = (h % 2) * 64
                                    tt("dve", qeTz[hb:hb + 64, h, :], qk_[hb:hb + 64, c2, 0:128], er2[hb:hb + 64, c2 * 128:(c2 + 1) * 128],
                                       ALU.mult, [qkk, "er2"], [kq])
                                    tt("dve", keTz[hb:hb + 64, h, :], qk_[hb:hb + 64, 2 + c2, 0:128], enb[hb:hb + 64, c2 * 128:(c2 + 1) * 128],
                                       ALU.mult, [qkk, "enb"], ["keTz"])
                                tt("dve", kd[:], kv_[:, 0:256], er[:], ALU.mult, [kvk, "er"], [kkd])
                                for h in range(4):
                                    mm(banks[5][:, h * 128:(h + 1) * 128], keTz[:, h, :], qeTz[:, h, :], ["keTz", kq], ["b5"])

                            def gs1b(i):
                                p_ = i % 2
                                STg = STg2[p_]; kst = f"STg{p_}"
                                tt("dve", STg[:], banks[5][:, :].rearrange("p (h q) -> p h q", h=4), INC.unsqueeze(1).to_broadcast([128, 4, 128]), ALU.mult,
                                   ["b5", "cst"], [kst])

                            def gs2(i):
                                ci = order[i]; p_ = i % 2; s0 = ci * 128; b_ = i % 3
                                u_ = uc[b_]; uk = f"guc{b_}"; kv_ = kvl[b_]; kvk = f"kvl{b_}"
                                vtm = kv_[:, 256:768]
                                eb = eb2[p_]; keb = f"eb{p_}"; kd = kd2[p_]; kkd = f"kd{p_}"
                                STg = STg2[p_]; kst = f"STg{p_}"; qeTz = qeTz2[p_]; kq = f"qeTz{p_}"
                                fin = (d == 0) and not (last and ci < 2)
                                for h in range(4):
                                    c2 = h // 2
                                    mm(banks[6][:, h * 128:(h + 1) * 128], STg[:, h, :], vtm[:, h * 128:(h + 1) * 128], [kst, kvk], ["b6"], start=True, stop=False)
                                    mm(banks[6][:, h * 128:(h + 1) * 128], qeTz[:, h, :], Sgb[:, c2, :], [kq, "Sgb"], ["b6"], start=False, stop=True)
                                if d == 1:
                                    cp("dve", od[:], banks[6][:, :], ["b6"], ["od"])
                                    dma(yb_gla_d[s0:s0 + 128, :], od[:], "ybst", ["od"], [f"ybg{ci}"])
                                elif fin:
                                    dma(obl[:], yb_gla_d[s0:s0 + 128, :], "ybld", [f"ybg{ci}"], ["obl"])
                                    tt("dve", od[:], banks[6][:, :], obl[:], ALU.add, ["b6", "obl"], ["od"])
                                    mset("dve", ss4[:], 0.0, ["ss4"])
                                    for h in range(4):
                                        act(tmp[:, h * 128:(h + 1) * 128], od[:, h * 128:(h + 1) * 128], AF.Square, ["od", "ss4"], ["gtmp", "ss4"],
                                            accum=ss4[:, h:h + 1])
                                    rstd(ss4[:, 4:8], ss4[:, 0:4], 1.0 / 128, ["ss4"], ["ss4"])
                                    tt("dve", od[:].rearrange("p (h q) -> p h q", h=4), od[:].rearrange("p (h q) -> p h q", h=4),
                                       ss4[:, 4:8].unsqueeze(2).to_broadcast([128, 4, 128]), ALU.mult, ["od", "ss4"], ["od"])
                                    tt("pool", od[:].rearrange("p (h q) -> p h q", h=4), od[:].rearrange("p (h q) -> p h q", h=4),
                                       gnwb[:].unsqueeze(1).to_broadcast([128, 4, 128]), ALU.mult, ["od", "gnwb"], ["od"])
                                    tt("dve", cto[:], od[:], u_[:], ALU.mult, ["od", uk], ["gcto"])
                                    dma(cat_d[s0:s0 + 128, 1024:1536], cto[:], "catst", ["gcto"], [f"cat_gla{ci}"])
                                for c2 in range(2):
                                    for var in range(2):
                                        mm(banks[7][:, (c2 * 2 + var) * 128:(c2 * 2 + var + 1) * 128], kd[:, c2 * 128:(c2 + 1) * 128],
                                           vtm[:, (2 * c2 + var) * 128:(2 * c2 + var + 1) * 128], [kkd, kvk], ["b7"])
                                for c2 in range(2):
                                    for var in range(2):
                                        hb = var * 64
                                        stt(Sg[hb:hb + 64, c2, :], Sg[hb:hb + 64, c2, :], eb[hb:hb + 64, c2 * 128 + lastcol:c2 * 128 + lastcol + 1],
                                            banks[7][hb:hb + 64, (c2 * 2 + var) * 128:(c2 * 2 + var + 1) * 128], ALU.mult, ALU.add, ["Sg", keb, "b7"], ["Sg"])
                                cp("act", Sgb[:], Sg[:], ["Sg"], ["Sgb"])
                            n_ = len(order)
                            gloads(0); gloads(1)
                            gs1(0); gs1b(0)
                            for i in range(n_):
                                if i + 2 < n_:
                                    gloads(i + 2)
                                if i + 1 < n_:
                                    gs1(i + 1)
                                gs2(i)
                                if i + 1 < n_:
                                    gs1b(i + 1)
                        P.barrier()

                def phase_s5():
                    with ExitStack() as PS:
                        wv = win_d[l].rearrange("(k p) n -> p k n", p=128)
                        Uall = sbuf(PS, "Uall", [128, 32, 544], BF16)
                        c2 = sbuf(PS, "c2", [128, 1392])
                        dma(c2[:], cst2_d, "cst2", (), ["c2"])
                        NBv = (c2[:, 48:592], c2[:, 592:1136]); M8 = (c2[:, 1136:1264], c2[:, 1264:1392])
                        SUPER = [(0, 256)] + [(256 + 1024 * i_, 1024) for i_ in range(4)]
                        I32 = mybir.dt.int32
                        with ExitStack() as PA:
                            wu5 = sbuf(PA, "wu5", [128, 8, 512], BF16)
                            dma(wu5[:], wv[:, :, OU5:OU5 + 512], "w1", (), ["wu5"], eng="pool")
                            uts = sbuf(PA, "uts", [128, 8, 1024], BF16); Xf = sbuf(PA, "Xf", [128, 8, 512])
                            Xb = sbuf(PA, "Xb", [128, 32, 8, 16], BF16)
                            for (s0, T) in SUPER:
                                nb = T // 8; blk0 = s0 // 8
                                rd = [f"uT{c}" for c in range(s0 // 128, (s0 + T) // 128)]
                                dma(uts[:, :, 0:T], uT_v[:, :, s0:s0 + T], "uts", rd, ["uts"])
                                utv = uts[:, :, 0:T].rearrange("p k (b j) -> p k j b", j=8)
                                for j in range(8):
                                    bk = j % 2
                                    for k in range(8):
                                        mm(banks[bk][0:nb, :], utv[:, k, j, :], wu5[:, k, :], ["uts", "wu5"], [BK[bk]], start=(k == 0), stop=(k == 7))
                                    if not os.environ.get("NOXF"):
                                        act(Xf[0:nb, j, :], banks[bk][0:nb, :], AF.Identity, [BK[bk]], ["Xf"])
                                    if not os.environ.get("NOXB"):
                                        cp("dve", Xb[0:nb, :, j, :], banks[bk][0:nb, :].rearrange("p (g h) -> p g h", g=32), [BK[bk]], ["Xb"])
                                if SSTOP < 0.2:
                                    continue
                                dma(u5_d[s0:s0 + T, :].rearrange("(b j) c -> b j c", j=8), Xf[0:nb, :, :], "u5st", ["Xf"], [f"u5s{s0}"])
                                if SSTOP < 0.3:
                                    continue
                                for g in range(32):
                                    bk = 2 + (g // 4) % 2
                                    mm(banks[bk][:, (g % 4) * 128:(g % 4) * 128 + nb], Xb[0:nb, g, :, :].rearrange("p a b -> p (a b)"), identb[0:nb, 0:nb],
                                       ["Xb", "identb"], [BK[bk]])
                                    if g % 4 == 3:
                                        cp("dve", Uall[:, g - 3:g + 1, blk0:blk0 + nb], banks[bk][:, :].rearrange("p (a b) -> p a b", a=4)[:, :, 0:nb],
                                           [BK[bk]], ["Uall"])
                            P.barrier()
                        if SSTOP < 1:
                            return
                        WstRe = sbuf(PS, "WstRe", [128, 32, 128], BF16); WstIm = sbuf(PS, "WstIm", [128, 32, 128], BF16)
                        Mg = sbuf(PS, "Mg", [128, 32, 128], BF16)
                        WoRe = sbuf(PS, "WoRe", [128, 32, 128], BF16); WoIm = sbuf(PS, "WoIm", [128, 32, 128], BF16)
                        mset("pool", WoRe[:], 0.0, ["WoRe"]); mset("pool", WoIm[:], 0.0, ["WoIm"])
                        sm = sbuf(PS, "sm", [128, 20, 16]); S = lambda i_: sm[:, i_, :]
                        ti_t = sbuf(PS, "ti_t", [128, 544], I32); tf_t = sbuf(PS, "tf_t", [128, 544]); tm_t = sbuf(PS, "tm_t", [128, 544])
                        tq_t = sbuf(PS, "tq_t", [128, 544])
                        IDm = sbuf(PS, "IDm", [128, 2, 128])
                        mset("dve", IDm[:], 0.0, ["IDm"])
                        cp("dve", IDm[:, 0, 0:64], IDENT[:, 0:64], ["cst", "IDm"], ["IDm"])
                        cp("dve", IDm[:, 1, 64:128], IDENT[:, 64:128], ["cst", "IDm"], ["IDm"])

                        def fturn(dst, src, n, mul, add_turn, rd, wr):
                            ti = ti_t[:, 0:n]; tf = tf_t[:, 0:n]; tm = tm_t[:, 0:n]
                            tsc("dve", dst, src, mul, add_turn, ALU.mult, ALU.add, list(rd), list(wr))
                            cp("dve", ti, dst, list(wr), ["trg"])
                            cp("dve", tf, ti, ["trg"], ["trg"])
                            tt("dve", dst, dst, tf, ALU.subtract, list(wr) + ["trg"], list(wr))

                        def sincos_turn(dsin, dcos, f, n, rd, wr):
                            t_ = tq_t[:, 0:n]
                            act(dsin, f, AF.Sin, list(rd), list(wr), scale=6.28318)
                            act(t_, f, AF.Sin, list(rd), ["tq"], scale=3.14159)
                            tt("dve", t_, t_, t_, ALU.mult, ["tq"], ["tq"])
                            tsc("dve", dcos, t_, -2.0, 1.0, ALU.mult, ALU.add, ["tq"], list(wr))

                        for d in (1, 0):
                            K1 = ["sm"]
                            with ExitStack() as PC:
                                prm = sbuf(PC, "prm", [128, 48]); bbp = sbuf(PC, "bbp", [128, 512]); ccp = sbuf(PC, "ccp", [128, 512])
                                bbr = sbuf(PC, "bbr", [128, 256]); bbi = sbuf(PC, "bbi", [128, 256]); t16 = sbuf(PC, "t16", [128, 256])
                                pw = sbuf(PC, "pw", [128, 6, 16, 24])
                                KBre = sbuf(PC, "KBre", [128, 16, 8, 16]); KBim = sbuf(PC, "KBim", [128, 16, 8, 16])
                                QCre = sbuf(PC, "QCre", [128, 16, 8, 16]); QCim = sbuf(PC, "QCim", [128, 16, 8, 16])
                                QZre = sbuf(PC, "QZre", [128, 16, 128]); QZim = sbuf(PC, "QZim", [128, 16, 128]); tK = sbuf(PC, "tK", [128, 16, 8, 16])
                                dma(prm[:], s5p_d[l, d], "pp", (), ["prm"]); dma(bbp[:], s5b_d[l], "pp", (), ["bbp"]); dma(ccp[:], s5c_d[l, d], "pp", (), ["ccp"])
                                lre = prm[:, 0:16]; lim = prm[:, 16:32]
                                act(S(0), prm[:, 32:48], AF.Exp, ["prm"], K1)
                                tt("dve", S(1), lre, S(0), ALU.mult, ["prm"] + K1, K1)
                                tt("dve", S(2), lim, S(0), ALU.mult, ["prm"] + K1, K1)
                                act(S(3), S(1), AF.Exp, K1, K1)
                                fturn(S(6), S(2), 16, 1.0 / (2 * PI), 0.0, K1, K1)
                                sincos_turn(S(4), S(5), S(6), 16, K1, K1)
                                tt("dve", S(7), S(3), S(5), ALU.mult, K1, K1)
                                tsc("dve", S(7), S(7), -1.0, None, ALU.add, None, K1, K1)
                                tt("dve", S(8), S(3), S(4), ALU.mult, K1, K1)
                                tt("dve", S(9), lre, lre, ALU.mult, ["prm"], K1)
                                tt("dve", S(10), lim, lim, ALU.mult, ["prm"], K1)
                                tt("dve", S(9), S(9), S(10), ALU.add, K1, K1)
                                P.op("dve", lambda e: e.reciprocal(S(9), S(9)), K1, K1)
                                tt("dve", S(10), S(7), lre, ALU.mult, K1 + ["prm"], K1)
                                tt("dve", S(11), S(8), lim, ALU.mult, K1 + ["prm"], K1)
                                tt("dve", S(10), S(10), S(11), ALU.add, K1, K1)
                                tt("dve", S(10), S(10), S(9), ALU.mult, K1, K1)
                                tt("dve", S(11), S(8), lre, ALU.mult, K1 + ["prm"], K1)
                                tt("dve", S(12), S(7), lim, ALU.mult, K1 + ["prm"], K1)
                                tt("dve", S(11), S(11), S(12), ALU.subtract, K1, K1)
                                tt("dve", S(11), S(11), S(9), ALU.mult, K1, K1)
                                v3 = lambda t_: t_.rearrange("p (q h) -> p q h", q=16)
                                kreb = S(10).unsqueeze(2).to_broadcast([128, 16, 16]); kimb = S(11).unsqueeze(2).to_broadcast([128, 16, 16])
                                tt("dve", v3(bbr[:]), v3(bbp[:, 0:256]), kreb, ALU.mult, ["bbp"] + K1, ["bbr"])
                                tt("dve", v3(t16[:]), v3(bbp[:, 256:512]), kimb, ALU.mult, ["bbp"] + K1, ["t16"])
                                tt("dve", bbr[:], bbr[:], t16[:], ALU.subtract, ["bbr", "t16"], ["bbr"])
                                tt("dve", v3(bbi[:]), v3(bbp[:, 256:512]), kreb, ALU.mult, ["bbp"] + K1, ["bbi"])
                                tt("dve", v3(t16[:]), v3(bbp[:, 0:256]), kimb, ALU.mult, ["bbp"] + K1, ["t16"])
                                tt("dve", bbi[:], bbi[:], t16[:], ALU.add, ["bbi", "t16"], ["bbi"])
                                act(S(13), S(1), AF.Exp, K1, K1, scale=8.0)
                                fturn(S(14), S(2), 16, 8.0 / (2 * PI), 0.0, K1, K1)
                                kr = c2[:, d * 24:(d + 1) * 24].unsqueeze(1).to_broadcast([128, 16, 24])
                                PK = ["pw"]
                                tt("dve", pw[:, 0], S(2).unsqueeze(2).to_broadcast([128, 16, 24]), kr, ALU.mult, K1 + ["c2"], PK)
                                tt("dve", pw[:, 2], S(1).unsqueeze(2).to_broadcast([128, 16, 24]), kr, ALU.mult, K1 + ["c2"], PK)
                                f2 = lambda i_: pw[:, i_].rearrange("p a b -> p (a b)")
                                act(f2(2), f2(2), AF.Exp, PK, PK)
                                fturn(f2(1), f2(0), 384, 1.0 / (2 * PI), 0.0, PK, PK)
                                sincos_turn(f2(3), f2(4), f2(1), 384, PK, PK)
                                tt("dve", f2(4), f2(4), f2(2), ALU.mult, PK, PK)
                                tt("dve", f2(5), f2(3), f2(2), ALU.mult, PK, PK)
                                PWre = pw[:, 4]; PWim = pw[:, 5]

                                def cprod(ore, oim, row, xr, xi, kx, neg_im=False):
                                    pr = PWre[:, :, row * 8:(row + 1) * 8].unsqueeze(3).to_broadcast([128, 16, 8, 16])
                                    pi_ = PWim[:, :, row * 8:(row + 1) * 8].unsqueeze(3).to_broadcast([128, 16, 8, 16])
                                    xr4 = v3(xr).unsqueeze(2).to_broadcast([128, 16, 8, 16]); xi4 = v3(xi).unsqueeze(2).to_broadcast([128, 16, 8, 16])
                                    tt("dve", ore[:], pr, xr4, ALU.mult, PK + [kx], ["cpo"])
                                    tt("dve", tK[:], pi_, xi4, ALU.mult, PK + [kx], ["tK"])
                                    tt("dve", ore[:], ore[:], tK[:], ALU.subtract, ["cpo", "tK"], ["cpo"])
                                    tt("dve", oim[:], pr, xi4, ALU.mult, PK + [kx], ["cpo"])
                                    tt("dve", tK[:], pi_, xr4, ALU.mult, PK + [kx], ["tK"])
                                    if neg_im:
                                        stt(oim[:], oim[:], -1.0, tK[:], ALU.mult, ALU.subtract, ["cpo", "tK"], ["cpo"])
                                    else:
                                        tt("dve", oim[:], oim[:], tK[:], ALU.add, ["cpo", "tK"], ["cpo"])
                                f3 = lambda t_: t_[:].rearrange("p q a b -> p q (a b)")
                                cprod(KBre, KBim, 0, bbr[:], bbi[:], "bbr")
                                cprod(QCre, QCim, 1, ccp[:, 0:256], ccp[:, 256:512], "ccp", neg_im=True)
                                for (src_, dst_) in ((KBre, WstRe), (KBim, WstIm)):
                                    dv = dst_[:].rearrange("p (q m) c -> p q m c", m=2)
                                    for m in range(2):
                                        for q in range(16):
                                            bk = 4 + (q // 4) % 2
                                            mm(banks[bk][:, (q % 4) * 128:(q % 4 + 1) * 128], f3(src_)[:, q, :], IDm[:, m, :], ["cpo", "IDm"], [BK[bk]])
                                            if q % 4 == 3:
                                                cp("dve", dv[:, q - 3:q + 1, m, :], banks[bk][:, :].rearrange("p (a b) -> p a b", a=4), [BK[bk]], ["Wst"])
                                mgv = Mg[:].rearrange("p (q m) c -> p q m c", m=2)
                                for m in range(2):
                                    o = 1 - m
                                    mset("pool", QZre[o * 64:(o + 1) * 64, :, :], 0.0, ["QZ"]); mset("pool", QZim[o * 64:(o + 1) * 64, :, :], 0.0, ["QZ"])
                                    cp("pool", QZre[m * 64:(m + 1) * 64, :, :], f3(QCre)[m * 64:(m + 1) * 64, :, :], ["cpo", "QZ"], ["QZ"])
                                    cp("pool", QZim[m * 64:(m + 1) * 64, :, :], f3(QCim)[m * 64:(m + 1) * 64, :, :], ["cpo", "QZ"], ["QZ"])
                                    for q in range(16):
                                        bk = 6 + (q // 4) % 2
                                        osl = banks[bk][:, (q % 4) * 128:(q % 4 + 1) * 128]
                                        mm(osl, f3(KBre)[:, q, :], QZre[:, q, :], ["cpo", "QZ"], [BK[bk]], start=True, stop=False)
                                        mm(osl, f3(KBim)[:, q, :], QZim[:, q, :], ["cpo", "QZ"], [BK[bk]], start=False, stop=True)
                                        if q % 4 == 3:
                                            tt("dve", mgv[:, q - 3:q + 1, m, :], banks[bk][:, :].rearrange("p (a b) -> p a b", a=4),
                                               M8[d].unsqueeze(1).to_broadcast([128, 4, 128]), ALU.mult, [BK[bk], "c2"], ["Mg"])
                                cprod(QCre, QCim, 2, ccp[:, 0:256], ccp[:, 256:512], "ccp", neg_im=True)
                                for (src_, dst_, kk) in ((QCre, WoRe, "WoRe"), (QCim, WoIm, "WoIm")):
                                    dv = dst_[:].rearrange("p (q m) c -> p q m c", m=2)
                                    for m in range(2):
                                        cp("dve", dv[m * 64:(m + 1) * 64, :, m, :], f3(src_)[m * 64:(m + 1) * 64, :, :], ["cpo"], [kk])
                                P.barrier()
                            if dbg:
                                dsm = nc.dram_tensor(f"dbg_sm{l}_{d}", [128, 320], F32, kind="ExternalOutput").ap()
                                dma(dsm, sm[:].rearrange("p a b -> p (a b)"), "dbg", ["sm"], [f"dbg_sm{d}"])
                            if SSTOP < 2:
                                return
                            with ExitStack() as PY:
                              Yall = sbuf(PY, "Yall", [128, 32, 544])
                              with ExitStack() as PR:
                                cosB = sbuf(PR, "cosB", [128, 544]); sinB = sbuf(PR, "sinB", [128, 544]); fq = sbuf(PR, "fq", [128, 544])
                                ta = sbuf(PR, "ta", [128, 544]); tb = sbuf(PR, "tb", [128, 544]); tc_ = sbuf(PR, "tc", [128, 544]); td = sbuf(PR, "td", [128, 544])
                                Sre = sbuf(PR, "Sre", [128, 544]); Sim = sbuf(PR, "Sim", [128, 544])
                                Wre = sbuf(PR, "Wre", [128, 544]); Wim = sbuf(PR, "Wim", [128, 544])
                                Hre = sbuf(PR, "Hre", [128, 544], BF16); Him = sbuf(PR, "Him", [128, 544], BF16)
                                car = sbuf(PR, "car", [128, 4])
                                NB = NBv[d]
                                bc = 31 if d == 0 else 0
                                SEG = ((0, 32), (32, 544))
                                PIECES = ((0, 512), (512, 32))
                                cosB2 = [cosB, sbuf(PR, "cosB1", [128, 544])]; sinB2 = [sinB, sbuf(PR, "sinB1", [128, 544])]

                                def s_mm_tab(q):
                                    for (c0, n) in PIECES:
                                        for m in range(2):
                                            mm(big[0][:, c0:c0 + n], WstRe[:, 2 * q + m, :], Uall[:, 2 * q + m, c0:c0 + n], ["Wst", "Uall"], ["b0", "b1"],
                                               start=(m == 0), stop=(m == 1))
                                        for m in range(2):
                                            mm(big[1][:, c0:c0 + n], WstIm[:, 2 * q + m, :], Uall[:, 2 * q + m, c0:c0 + n], ["Wst", "Uall"], ["b2", "b3"],
                                               start=(m == 0), stop=(m == 1))
                                    fturn(fq[:], NB, 544, S(14)[:, q:q + 1], 0.0, ["c2"] + K1, ["fq"])
                                    sincos_turn(sinB2[q % 2][:], cosB2[q % 2][:], fq[:], 544, ["fq"], [f"trig{q % 2}"])
                                s_mm_tab(0)
                                for q in range(16):
                                    cosB = cosB2[q % 2]; sinB = sinB2[q % 2]; ktr = f"trig{q % 2}"
                                    pR = big[0][:, 0:544]; pI = big[1][:, 0:544]
                                    tt("dve", ta[:], pR, cosB[:], ALU.mult, ["b0", "b1", ktr], ["ta"])
                                    tt("dve", tb[:], pI, sinB[:], ALU.mult, ["b2", "b3", ktr], ["tb"])
                                    tt("pool", Sre[:], ta[:], tb[:], ALU.add, ["ta", "tb"], ["Sre"])
                                    tt("dve", tc_[:], pI, cosB[:], ALU.mult, ["b2", "b3", ktr], ["tc"])
                                    tt("dve", td[:], pR, sinB[:], ALU.mult, ["b0", "b1", ktr], ["td"])
                                    tt("pool", Sim[:], tc_[:], td[:], ALU.subtract, ["tc", "td"], ["Sim"])
                                    if q + 1 < 16:
                                        s_mm_tab(q + 1)
                                    r8b = S(13)[:, q:q + 1]

                                    def scan(w_, s_, a0, a1, init, wk, sk, extra, r8b=r8b):
                                        o_ap, d1 = w_[:, a0:a1], s_[:, a0:a1]
                                        if d == 1:
                                            o_ap, d1 = o_ap[:, ::-1], d1[:, ::-1]
                                        P.op("dve", lambda e: e.tensor_tensor_scan(o_ap, r8b.to_broadcast([128, a1 - a0]), d1, init, ALU.mult, ALU.add),
                                             [sk] + K1 + extra, [wk])
                                    scan(Wre, Sre, 0, 32, 0.0, "Wre", "Sre", [])
                                    scan(Wim, Sim, 0, 32, 0.0, "Wim", "Sim", [])
                                    c_ = cosB[:, bc:bc + 1]; s_ = sinB[:, bc:bc + 1]
                                    tt("dve", car[:, 2:3], Wim[:, bc:bc + 1], s_, ALU.mult, ["Wim", ktr], ["car"])
                                    stt(car[:, 0:1], Wre[:, bc:bc + 1], c_, car[:, 2:3], ALU.mult, ALU.subtract, ["Wre", ktr, "car"], ["car"])
                                    tt("dve", car[:, 3:4], Wre[:, bc:bc + 1], s_, ALU.mult, ["Wre", ktr], ["car"])
                                    stt(car[:, 1:2], Wim[:, bc:bc + 1], c_, car[:, 3:4], ALU.mult, ALU.add, ["Wim", ktr, "car"], ["car"])
                                    scan(Wre, Sre, 32, 544, car[:, 0:1], "Wre", "Sre", ["car"])
                                    scan(Wim, Sim, 32, 544, car[:, 1:2], "Wim", "Sim", ["car"])
                                    tt("pool", ta[:], Wre[:], cosB[:], ALU.mult, ["Wre", ktr], ["ta"])
                                    tt("pool", tb[:], Wim[:], sinB[:], ALU.mult, ["Wim", ktr], ["tb"])
                                    tt("pool", tc_[:], Wim[:], cosB[:], ALU.mult, ["Wim", ktr], ["tc"])
                                    tt("pool", td[:], Wre[:], sinB[:], ALU.mult, ["Wre", ktr], ["td"])
                                    for (a0, a1) in SEG:
                                        if d == 0:
                                            so, si = slice(a0 + 1, a1), slice(a0, a1 - 1); ic = a0
                                        else:
                                            so, si = slice(a0, a1 - 1), slice(a0 + 1, a1); ic = a1 - 1
                                        tt("dve", Hre[:, so], ta[:, si], tb[:, si], ALU.subtract, ["ta", "tb"], ["Hre"])
                                        tt("dve", Him[:, so], tc_[:, si], td[:, si], ALU.add, ["tc", "td"], ["Him"])
                                        if a0 == 0:
                                            mset("pool", Hre[:, ic:ic + 1], 0.0, ["Hre"]); mset("pool", Him[:, ic:ic + 1], 0.0, ["Him"])
                                        else:
                                            cp("dve", Hre[:, ic:ic + 1], car[:, 0:1], ["car", "Hre"], ["Hre"])
                                            cp("dve", Him[:, ic:ic + 1], car[:, 1:2], ["car", "Him"], ["Him"])
                                    for m in range(2):
                                        g = 2 * q + m; bY = big[2 + m]; kY = [BK[4 + 2 * m], BK[5 + 2 * m]]
                                        for (c0, n) in PIECES:
                                            mm(bY[:, c0:c0 + n], Mg[:, g, :], Uall[:, g, c0:c0 + n], ["Mg", "Uall"], kY, start=True, stop=False)
                                            mm(bY[:, c0:c0 + n], WoRe[:, g, :], Hre[:, c0:c0 + n], ["WoRe", "Hre"], kY, start=False, stop=False)
                                            mm(bY[:, c0:c0 + n], WoIm[:, g, :], Him[:, c0:c0 + n], ["WoIm", "Him"], kY, start=False, stop=True)
                                        if m == 0:
                                            act(Yall[:, g, :], bY[:, 0:544], AF.Identity, kY, ["Yall"])
                                        else:
                                            cp("dve", Yall[:, g, :], bY[:, 0:544], kY, ["Yall"])
                                P.barrier()
                              if SSTOP < 3:
                                return
                              if True:
                                Ytm = sbuf(PY, "Ytm", [128, 8, 512])
                                for (s0, T) in SUPER:
                                    nb = T // 8; blk0 = s0 // 8
                                    for g in range(32):
                                        bk = (g // 4) % 2
                                        mm(banks[bk][0:nb, (g % 4) * 128:(g % 4 + 1) * 128], Yall[:, g, blk0:blk0 + nb], IDENT, ["Yall", "cst"], [BK[bk]])
                                        if g % 4 == 3:
                                            g0 = g - 3
                                            cp("dve" if (g // 4) % 2 else "act_id", Ytm[0:nb, :, g0 * 16:(g0 + 4) * 16].rearrange("p j (g h) -> p j g h", g=4),
                                               banks[bk][0:nb, :].rearrange("p (g j h) -> p j g h", g=4, j=8), [BK[bk]], ["Ytm"])
                                    dma(y5_d[d][s0:s0 + T, :].rearrange("(b j) c -> b j c", j=8), Ytm[0:nb, :, :], "y5st", ["Ytm"], [f"y5_{d}_{s0}"])
                                P.barrier()
                        P.barrier()
                    if SSTOP < 4:
                        return
                    with ExitStack() as PF:
                        wv = win_d[l].rearrange("(k p) n -> p k n", p=128)
                        wsg = sbuf(PF, "wsg", [128, 8, 512], BF16); wglu = sbuf(PF, "wglu", [128, 4, 1024], BF16)
                        dma(wsg[:], wv[:, :, OSG:OSG + 512], "w2", (), ["wsg"], eng="pool")
                        dma(wglu[:], gluw_d[l].rearrange("(c p) n -> p c n", p=128), "w3", (), ["wglu"], eng="pool")
                        glubb = sbuf(PF, "glubb", [128, 1024]); dskb = sbuf(PF, "dskb5", [128, 512])
                        dma(glubb[:], glub_d[l:l + 1, :].partition_broadcast(128), "pp", (), ["glubb"])
                        dma(dskb[:], s5dr_d[l:l + 1, :].partition_broadcast(128), "pp", (), ["dskb5"])
                        uc = [sbuf(PF, f"suc{i_}", [128, 8, 128], BF16) for i_ in range(2)]
                        yf = [sbuf(PF, f"yf{i_}", [128, 512]) for i_ in range(2)]; yb = [sbuf(PF, f"yb{i_}", [128, 512]) for i_ in range(2)]
                        u5 = [sbuf(PF, f"u5t{i_}", [128, 512]) for i_ in range(2)]
                        ge2 = [sbuf(PF, f"ge{i_}", [128, 512], BF16) for i_ in range(2)]; geT2 = [sbuf(PF, f"geT{i_}", [128, 4, 128], BF16) for i_ in range(2)]
                        pa2 = [sbuf(PF, f"pa{i_}", [128, 512]) for i_ in range(2)]; pg2 = [sbuf(PF, f"pg{i_}", [128, 512]) for i_ in range(2)]
                        ssg2 = [sbuf(PF, f"ssg{i_}", [128, 512]) for i_ in range(2)]; cto2 = [sbuf(PF, f"scto{i_}", [128, 512], BF16) for i_ in range(2)]
                        chunks = list(range(2, NCH)) if last else list(range(NCH))
                        sck = lambda ci: 0 if ci < 2 else 256 + 1024 * ((ci - 2) // 8)

                        def loadf(i_):
                            ci = chunks[i_]; s0 = ci * 128; b_ = i_ % 2
                            load_uc(uc[b_], f"suc{b_}", ci, 0)
                            dma(yf[b_][:], y5_d[0][s0:s0 + 128, :], f"yf{b_}", [f"y5_0_{sck(ci)}"], [f"yf{b_}"])
                            dma(yb[b_][:], y5_d[1][s0:s0 + 128, :], f"yb{b_}", [f"y5_1_{sck(ci)}"], [f"yb{b_}"])
                            dma(u5[b_][:], u5_d[s0:s0 + 128, :], f"u5t{b_}", [f"u5s{sck(ci)}"], [f"u5t{b_}"])
                        def names(i_):
                            b_ = i_ % 2
                            return (b_, ge2[b_], geT2[b_], pa2[b_], pg2[b_], ssg2[b_], cto2[b_],
                                    f"ge{b_}", f"geT{b_}", f"pa{b_}", f"pg{b_}", f"ssg{b_}", f"scto{b_}",
                                    ((0, 2, 3, 4) if b_ == 0 else (1, 5, 6, 7)))

                        def s1(i_):
                            ci = chunks[i_]
                            b_, ge, geT, pa, pg, ssg, cto, kge, kgeT, kpa, kpg, kssg, kcto, (B0, B2, B3, B4) = names(i_)
                            u_ = uc[b_]; uk = f"suc{b_}"
                            tt("pool", yf[b_][:], yf[b_][:], yb[b_][:], ALU.add, [f"yf{b_}", f"yb{b_}"], [f"yf{b_}"])
                            tt("dve", u5[b_][:], u5[b_][:], dskb[:], ALU.mult, [f"u5t{b_}", "dskb5"], [f"u5t{b_}"])
                            tt("dve", yf[b_][:], yf[b_][:], u5[b_][:], ALU.add, [f"yf{b_}", f"u5t{b_}"], [f"yf{b_}"])
                            act(ge[:], yf[b_][:], AF.Gelu, [f"yf{b_}"], [kge])
                            for c4 in range(4):
                                mm(banks[B0][:, c4 * 128:(c4 + 1) * 128], ge[:, c4 * 128:(c4 + 1) * 128], identb[:], [kge, "identb"], [BK[B0]])
                            cp("dve", geT[:].rearrange("p a b -> p (a b)"), banks[B0][:, :], [BK[B0]], [kgeT])
                            for nb_, BB in ((0, B2), (1, B3)):
                                for c4 in range(4):
                                    mm(banks[BB][:, :], geT[:, c4, :], wglu[:, c4, nb_ * 512:(nb_ + 1) * 512], [kgeT, "wglu"], [BK[BB]],
                                       start=(c4 == 0), stop=(c4 == 3))
                            for k in range(8):
                                mm(banks[B4][:, :], u_[:, k, :], wsg[:, k, :], [uk, "wsg"], [BK[B4]], start=(k == 0), stop=(k == 7))

                        def s2(i_):
                            ci = chunks[i_]; s0 = ci * 128
                            b_, ge, geT, pa, pg, ssg, cto, kge, kgeT, kpa, kpg, kssg, kcto, (B0, B2, B3, B4) = names(i_)
                            tt("dve", pa[:], banks[B2][:, :], glubb[:, 0:512], ALU.add, [BK[B2], "glubb"], [kpa])
                            tt("dve", pg[:], banks[B3][:, :], glubb[:, 512:1024], ALU.add, [BK[B3], "glubb"], [kpg])
                            act(pg[:], pg[:], AF.Sigmoid, [kpg], [kpg])
                            act(ssg[:], banks[B4][:, :], AF.Silu, [BK[B4]], [kssg])
                            tt("pool", pa[:], pa[:], pg[:], ALU.mult, [kpa, kpg], [kpa])
                            tt("dve", cto[:], pa[:], ssg[:], ALU.mult, [kpa, kssg], [kcto])
                            dma(cat_d[s0:s0 + 128, 1536:2048], cto[:], f"catst{b_}", [kcto], [f"cat_s5{ci}"])
                        loadf(0)
                        if len(chunks) > 1:
                            loadf(1)
                        s1(0)
                        for i_ in range(len(chunks)):
                            if i_ + 2 < len(chunks):
                                loadf(i_ + 2)
                            if i_ + 1 < len(chunks):
                                s1(i_ + 1)
                            s2(i_)
                        P.barrier()

                if "ssd" in phases:
                    phase_ssd()
                if "gla" in phases:
                    phase_gla()
                if "s5" in phases:
                    phase_s5()
                if "p4" in phases:
                    phase_p4()
        finals = [k for k in P.last_write if k.startswith(f"h{DEPTH}_") or (dbg and not k.startswith("b"))]
        P.wait_all("sp", finals)
        P.emit()
    return nc


def sweep_order(d):
    if d == 0:
        return list(range(NCH))
    return [1, 0] + list(range(NCH - 1, 1, -1))


def _consts():
    k = np.arange(128)[:, None]; j = np.arange(128)[None, :]
    mats = [np.eye(128), (k <= j), (k >= j), (k > j), (k < j), np.ones((128, 128)),
            np.broadcast_to(np.arange(1, 129)[None, :], (128, 128)), np.broadcast_to(np.arange(128, 0, -1)[None, :], (128, 128))]
    return np.concatenate([m.astype(np.float32) for m in mats], axis=1)


def _consts2():
    kr = np.zeros((2, 3, 8), np.float32)
    i = np.arange(8)
    kr[0, 0] = 7 - i; kr[0, 1] = i - 7; kr[0, 2] = i + 1
    kr[1, 0] = i; kr[1, 1] = -i; kr[1, 2] = 8 - i
    nbf = np.concatenate([np.arange(1, 33), np.arange(1, 513)]).astype(np.float32)
    nbb = np.concatenate([np.arange(32, 0, -1), np.arange(512, 0, -1)]).astype(np.float32)
    r = np.arange(128)[:, None] // 16; c = np.arange(128)[None, :] // 16
    m8f = (c >= r).astype(np.float32); m8b = (r >= c).astype(np.float32)
    row = np.concatenate([kr.reshape(-1), nbf, nbb])
    return np.ascontiguousarray(np.concatenate([np.broadcast_to(row[None, :], (128, row.size)), m8f, m8b], axis=1).astype(np.float32))


def _pairlay(a):
    a = np.asarray(a)
    rest = a.shape[2:]
    a = a.reshape(16, 2, 64, *rest)
    a = np.moveaxis(a, 0, 2)
    return np.ascontiguousarray(a.reshape(128, 16, *rest))


def prep_inputs(inp, b):
    f = lambda a: np.ascontiguousarray(np.asarray(a, dtype=np.float32))
    col = lambda v, n: f(np.asarray(v).reshape(n, 128).T)
    m = {}
    m["x"] = f(inp["x"][b]); m["ctx"] = f(inp["ctx"][b])
    m["ccol"] = f(np.concatenate([col(inp["c"][b], 8), col(inp["c_ctx"], 8)], axis=1))
    m["mod_w"] = f(inp["mod_w"]); m["mod_b"] = f(inp["mod_b"])
    m["modb_col"] = f(np.stack([col(inp["mod_b"][l], 24) for l in range(DEPTH)]))
    m["nw_col"] = f(np.stack([col(inp["norm_w"][l], 8) for l in range(DEPTH)]))
    m["w_in"] = f(inp["w_in"]); m["w_out"] = f(inp["w_out"])
    cw = np.asarray(inp["ssd_conv_w"])
    m["convw_col"] = f(np.stack([cw[l].reshape(5, 12, 128).transpose(2, 1, 0).reshape(128, 60) for l in range(DEPTH)]))
    m["convb_col"] = f(np.stack([col(inp["ssd_conv_b"][l], 12) for l in range(DEPTH)]))
    m["a_log"] = f(np.asarray(inp["ssd_a_log"]).reshape(DEPTH, 32)); m["dt_bias"] = f(np.asarray(inp["ssd_dt_bias"]).reshape(DEPTH, 32))
    m["ssd_d"] = f(inp["ssd_d"]); m["ssd_nw"] = f(inp["ssd_norm_w"])
    wl = np.asarray(inp["gla_w_lr"])
    m["wlr"] = f(wl.transpose(0, 2, 1, 3).reshape(DEPTH, 16, 512)); m["blr"] = f(np.asarray(inp["gla_b_lr"]).reshape(DEPTH, 512))
    m["gla_nw"] = f(inp["gla_norm_w"])
    s5p = np.zeros((DEPTH, 2, 128, 48), np.float32)
    s5c = np.zeros((DEPTH, 2, 128, 512), np.float32)
    s5b = np.zeros((DEPTH, 128, 512), np.float32)
    for l in range(DEPTH):
        s5b[l, :, 0:256] = _pairlay(inp["s5_b_re"][l]).reshape(128, 256)
        s5b[l, :, 256:512] = _pairlay(inp["s5_b_im"][l]).reshape(128, 256)
        for d in range(2):
            s5p[l, d, :, 0:16] = _pairlay(inp["s5_lam_re"][l, d])
            s5p[l, d, :, 16:32] = _pairlay(inp["s5_lam_im"][l, d])
            s5p[l, d, :, 32:48] = _pairlay(np.broadcast_to(np.asarray(inp["s5_log_step"][l, d])[:, None], (32, 64)))
            s5c[l, d, :, 0:256] = _pairlay(np.asarray(inp["s5_c_re"][l, d]).transpose(0, 2, 1)).reshape(128, 256)
            s5c[l, d, :, 256:512] = _pairlay(np.asarray(inp["s5_c_im"][l, d]).transpose(0, 2, 1)).reshape(128, 256)
    m["s5p"] = s5p; m["s5b"] = s5b; m["s5c"] = s5c
    m["s5d_col"] = f(np.stack([col(inp["s5_d"][l], 4) for l in range(DEPTH)]))
    m["glu_w"] = f(inp["s5_glu_w"]); m["glu_b"] = f(inp["s5_glu_b"]); m["fnw"] = f(inp["final_norm_w"])
    m["consts"] = _consts(); m["consts2"] = _consts2(); m["s5d_row"] = f(inp["s5_d"])
    return m


def kernel(**inputs):
    nc = build()
    in_maps = [prep_inputs(inputs, b) for b in range(8)]
    res = run_bass_kernel_spmd(nc, in_maps, core_ids=list(range(8)))
    return np.stack([np.asarray(r["out"], dtype=np.float32) for r in res.results], axis=0)
```

```python
import math, os
GSTOP = float(os.environ.get('GSTOP', '99'))
SSTOP = float(os.environ.get('SSTOP', '99'))
import numpy as np
from contextlib import ExitStack
import concourse.bass as bass
import concourse.mybir as mybir
from concourse.bass_utils import run_bass_kernel_spmd

F32 = mybir.dt.float32
BF16 = mybir.dt.bfloat16
AF = mybir.ActivationFunctionType
ALU = mybir.AluOpType

ENGS = ("pe", "act", "dve", "pool", "sp")


class Prog:
    def __init__(self, nc, stack):
        self.nc = nc
        self.stack = stack
        self.lists = {e: [] for e in ENGS}
        self.sems = {}
        self.semval = {}
        self.waited = {e: {} for e in ENGS}
        self.last_write = {}
        self.readers = {}
        for e in ("pe", "act", "dve", "pool"):
            self._sem("E_" + e)

    def _sem(self, name):
        if name not in self.sems:
            self.sems[name] = self.stack.enter_context(self.nc.semaphore(name))
            self.semval[name] = 0
        return self.sems[name]

    def _deps(self, eng, reads, writes):
        toks = []
        for k in reads:
            t = self.last_write.get(k)
            if t is not None:
                toks.append(t)
        for k in writes:
            t = self.last_write.get(k)
            if t is not None:
                toks.append(t)
            toks.extend(self.readers.get(k, ()))
        need = {}
        for (s, v) in toks:
            if s.startswith("D_"):
                v = self.semval[s]
            if eng == "pe" and s == "E_pe":
                continue
            if v > need.get(s, 0):
                need[s] = v
        waits = []
        for s, v in need.items():
            if self.waited[eng].get(s, 0) < v:
                self.waited[eng][s] = v
                waits.append((s, v))
        return waits

    def _commit(self, tok, reads, writes):
        for k in writes:
            self.last_write[k] = tok
            self.readers[k] = []
        for k in reads:
            if k in writes:
                continue
            self.readers.setdefault(k, []).append(tok)

    def op(self, eng, fn, reads=(), writes=()):
        bank_reads = [k for k in reads if len(k) == 2 and k[0] == "b" and k[1].isdigit()]
        if bank_reads:
            writes = list(writes) + [k for k in bank_reads if k not in writes]
        waits = self._deps(eng, reads, writes)
        s = "E_" + eng
        self.semval[s] += 1
        tok = (s, self.semval[s])
        self.lists[eng].append((waits, fn, (s, 1)))
        self._commit(tok, reads, writes)

    def dma(self, fn, stream, reads=(), writes=(), eng="sp"):
        s = "D_" + stream
        self._sem(s)
        waits = self._deps(eng, reads, writes)
        if self.semval[s] > 0 and self.waited[eng].get(s, 0) < self.semval[s]:
            self.waited[eng][s] = self.semval[s]
            waits = [w for w in waits if w[0] != s] + [(s, self.semval[s])]
        self.semval[s] += 16
        tok = (s, self.semval[s])
        self.lists[eng].append((waits, fn, (s, 16)))
        self._commit(tok, reads, writes)

    def barrier(self):
        for eng in ENGS:
            waits = []
            for s, v in self.semval.items():
                if v > 0 and self.waited[eng].get(s, 0) < v and not (eng != "pe" and False):
                    self.waited[eng][s] = v
                    waits.append((s, v))
            self.lists[eng].append((waits, None, None))

    def wait_all(self, eng, keys):
        waits = self._deps(eng, keys, ())
        self.lists[eng].append((waits, None, None))

    def emit(self):
        lists, sems = self.lists, self.sems

        def run(engh, items):
            for waits, fn, inc in items:
                for (s, v) in waits:
                    engh.wait_ge(sems[s], v)
                if fn is not None:
                    fn(engh).then_inc(sems[inc[0]], inc[1])

        with self.nc.Block() as block:
            @block.tensor
            def _(e):
                run(e, lists["pe"])

            @block.scalar
            def _(e):
                run(e, lists["act"])

            @block.vector
            def _(e):
                run(e, lists["dve"])

            @block.gpsimd
            def _(e):
                run(e, lists["pool"])

            @block.sync
            def _(e):
                run(e, lists["sp"])


D = 1024
LQ = 4096
LC = 256
LT = LQ + LC
NCH = LT // 128
DEPTH = 2
EPS = 1e-6
PI = math.pi
OZ, OXBC, ODT, OQ, OK_, OV, OGG, OLR, OU5, OSG = 0, 1024, 2560, 2592, 2848, 3104, 3616, 4128, 4160, 4672


def build(dbg=False, layers=(0, 1), phases=("p0", "ssd", "gla", "s5", "p4")):
    nc = bass.Bass("TRN2", target_bir_lowering=False)
    I = lambda name, shape, dt=F32: nc.dram_tensor(name, shape, dt, kind="ExternalInput").ap()
    skind = "ExternalOutput" if dbg else "Internal"
    S = lambda name, shape, dt=F32: nc.dram_tensor(name, shape, dt, kind=skind).ap()
    x_d = I("x", [LQ, D]); ctx_d = I("ctx", [LC, D])
    ccol_d = I("ccol", [128, 16])
    modw_d = I("mod_w", [DEPTH, D, 3 * D]); modbcol_d = I("modb_col", [DEPTH, 128, 24]); modb_d = I("mod_b", [DEPTH, 3 * D])
    nwcol_d = I("nw_col", [DEPTH, 128, 8])
    win_d = I("w_in", [DEPTH, D, 5184]); wout_d = I("w_out", [DEPTH, 2048, D])
    convw_d = I("convw_col", [DEPTH, 128, 60]); convb_d = I("convb_col", [DEPTH, 128, 12])
    alog_d = I("a_log", [DEPTH, 32]); dtb_d = I("dt_bias", [DEPTH, 32]); ssdd_d = I("ssd_d", [DEPTH, 16])
    ssdnw_d = I("ssd_nw", [DEPTH, 1024])
    wlr_d = I("wlr", [DEPTH, 16, 512]); blr_d = I("blr", [DEPTH, 512]); gnw_d = I("gla_nw", [DEPTH, 128])
    s5p_d = I("s5p", [DEPTH, 2, 128, 48])
    s5b_d = I("s5b", [DEPTH, 128, 512])
    s5c_d = I("s5c", [DEPTH, 2, 128, 512])
    s5d_d = I("s5d_col", [DEPTH, 128, 4]); gluw_d = I("glu_w", [DEPTH, 512, 1024]); glub_d = I("glu_b", [DEPTH, 1024])
    fnw_d = I("fnw", [1024]); s5dr_d = I("s5d_row", [DEPTH, 512]); cst2_d = I("consts2", [128, 1392])
    cst_d = I("consts", [128, 8 * 128])
    out_d = nc.dram_tensor("out", [LQ, D], F32, kind="ExternalOutput").ap()

    uT_d = S("uT", [D, LT], BF16)
    cat_d = (I("cat", [LT, 2048], BF16) if (dbg and not any(p in phases for p in ("ssd", "gla", "s5"))) else S("cat", [LT, 2048], BF16))
    yb_ssd_d = S("yb_ssd", [LT, 1024]); yb_gla_d = S("yb_gla", [LT, 512]); yb_s5_d = S("yb_s5T", [512, LT])
    h1_d = S("h1", [LQ, D]); hc1_d = S("hc1", [LC, D])
    u5_d = S("u5s", [LT, 512]); y5_d = [S("y5f", [LT, 512]), S("y5b", [LT, 512])]
    bc_d = S("bcT", [512, LT], BF16); xb_d = S("xbtm", [LT, 1280], BF16); dt_d = S("dtsp", [LT, 32])
    qk_d = S("qkT", [512, LT], BF16); kv_d = S("kvtm", [LT, 768], BF16); g_d = S("glog", [LT, 512])
    sz_d = S("szc", [LT, 1024]); sgg_d = S("sggc", [LT, 512])

    with ExitStack() as st:
        P = Prog(nc, st)
        uid = [0]

        def sbuf(stack, name, shape, dt=F32):
            uid[0] += 1
            return stack.enter_context(nc.sbuf_tensor(f"{name}_{uid[0]}", shape, dt))

        big = [st.enter_context(nc.psum_tensor(f"pbig{i}", [128, 1024], F32)) for i in range(4)]
        banks = [big[i // 2][:, (i % 2) * 512:(i % 2 + 1) * 512] for i in range(8)]
        BK = [f"b{i}" for i in range(8)]

        def mm(out, lhsT, rhs, rd, wr, start=True, stop=True):
            P.op("pe", lambda e: e.matmul(out, lhsT, rhs, start=start, stop=stop), rd, wr)

        def act(out, in_, func, rd, wr, bias=None, scale=None, accum=None):
            kw = {}
            if bias is not None:
                kw["bias"] = bias
            if scale is not None:
                kw["scale"] = scale
            if accum is not None:
                kw["accum_out"] = accum
            P.op("act", lambda e: e.activation(out, in_, func, **kw), rd, wr)

        def tt(eng, out, a, b, op, rd, wr):
            P.op(eng, lambda e: e.tensor_tensor(out, a, b, op), rd, wr)

        def tsc(eng, out, a, s1, s2, op0, op1, rd, wr):
            if s2 is None:
                P.op(eng, lambda e: e.tensor_scalar(out, a, s1, None, op0), rd, wr)
            else:
                P.op(eng, lambda e: e.tensor_scalar(out, a, s1, s2, op0, op1), rd, wr)

        def stt(out, in0, scalar, in1, op0, op1, rd, wr):
            P.op("dve", lambda e: e.scalar_tensor_tensor(out, in0, scalar, in1, op0, op1), rd, wr)

        def cp(eng, out, in_, rd, wr):
            if eng == "act_id":
                P.op("act", lambda e: e.activation(out, in_, AF.Identity), rd, wr)
            elif eng == "act":
                P.op("act", lambda e: e.copy(out, in_), rd, wr)
            else:
                P.op(eng, lambda e: e.tensor_copy(out, in_), rd, wr)

        def mset(eng, ap, val, wr):
            P.op(eng, lambda e: e.memset(ap, val), (), wr)

        epsc = sbuf(st, "epsc", [128, 1])
        P.op("dve", lambda e: e.memset(epsc[:], EPS), (), ["epsc"])

        def rstd(out, in_, scale, rd, wr):
            act(out, in_, AF.Ln, list(rd) + ["epsc"], wr, bias=epsc[0:out.shape[0], :], scale=scale)
            act(out, out, AF.Exp, wr, wr, scale=-0.5)

        ppc = [0]

        def dma(out, in_, stream, rd, wr, eng="sp"):
            if stream == "pp":
                ppc[0] += 1
                stream = f"pp{ppc[0] % 12}"
            P.dma(lambda e: e.dma_start(out=out, in_=in_), stream, rd, wr, eng=eng)

        cst = sbuf(st, "cst", [128, 8 * 128])
        dma(cst[:], cst_d, "cst", (), ["cst"])
        IDENT = cst[:, 0:128]; UINC = cst[:, 128:256]; LINC = cst[:, 256:384]
        USTR = cst[:, 384:512]; LSTR = cst[:, 512:640]; ONES = cst[:, 640:768]; NVEC = cst[:, 768:896]; NVECR = cst[:, 896:1024]
        identb = sbuf(st, "identb", [128, 128], BF16)
        dma(identb[:], cst_d[:, 0:128], "cstb", (), ["identb"], eng="pool")
        ccol = sbuf(st, "ccol", [128, 16])
        dma(ccol[:], ccol_d, "ccol", (), ["ccol"])
        sc = sbuf(st, "sc", [128, 16])
        act(sc[:], ccol[:], AF.Silu, ["ccol"], ["sc"])

        def hsrc(l, ci):
            if ci < 2:
                src = ctx_d if l == 0 else hc1_d
                return [(src[ci * 128:(ci + 1) * 128, :], 0, 128)]
            lc = ci - 2
            src = x_d if l == 0 else h1_d
            if l % 2 == 0:
                return [(src[lc * 128:(lc + 1) * 128, :], 0, 128)]
            v = src.rearrange("(r c) d -> c r d", c=64)
            return [(v[2 * lc], 0, 64), (v[2 * lc + 1], 64, 128)]

        def hdst(l, ci):
            if ci < 2:
                return [(hc1_d[ci * 128:(ci + 1) * 128, :], 0, 128)]
            lc = ci - 2
            dst = h1_d if l == 0 else out_d
            if l % 2 == 0:
                return [(dst[lc * 128:(lc + 1) * 128, :], 0, 128)]
            v = dst.rearrange("(r c) d -> c r d", c=64)
            return [(v[2 * lc], 0, 64), (v[2 * lc + 1], 64, 128)]

        uT_v = uT_d.rearrange("(k p) t -> p k t", p=128)

        for l in layers:
            last = (l == DEPTH - 1)
            with ExitStack() as LS:
                acol = sbuf(LS, "acol", [128, 16]); shcol = sbuf(LS, "shcol", [128, 16])
                gb = sbuf(LS, "gb", [128, 2, 1024])
                if "p0" in phases or "p4" in phases:
                    with ExitStack() as PS:
                        modT = sbuf(PS, "modT", [128, 24, 2])
                        mw = [sbuf(PS, f"mw{i}", [128, 8, 512]) for i in range(2)]
                        modbcol = sbuf(PS, "modbcol", [128, 24]); nwcol = sbuf(PS, "nwcol", [128, 8])
                        gbias = sbuf(PS, "gbias", [128, 1024])
                        dma(modbcol[:], modbcol_d[l], "pp", (), ["modbcol"])
                        dma(nwcol[:], nwcol_d[l], "pp", (), ["nwcol"])
                        dma(gbias[:], modb_d[l:l + 1, 2048:3072].partition_broadcast(128), "pp", (), ["gbias"])
                        mwv = modw_d[l].rearrange("(k p) f -> p k f", p=128)
                        sc2 = sc[:].rearrange("p (v k) -> p k v", v=2)
                        for blk in range(6):
                            t = mw[blk % 2]; tk = f"mw{blk % 2}"
                            dma(t[:], mwv[:, :, blk * 512:(blk + 1) * 512], tk, (), [tk])
                            for j in range(4 if blk < 4 else 0):
                                fc = blk * 4 + j
                                for k in range(8):
                                    mm(banks[0][:, fc * 2:fc * 2 + 2], t[:, k, j * 128:(j + 1) * 128], sc2[:, k, :],
                                       [tk, "sc"], ["b0"], start=(k == 0), stop=(k == 7))
                            if blk >= 4:
                                for v in range(2):
                                    for k in range(8):
                                        mm(banks[1 + v][:, :], sc[:, v * 8 + k:v * 8 + k + 1].to_broadcast([128, 128]), t[:, k, :],
                                           [tk, "sc"], [BK[1 + v]], start=(k == 0), stop=(k == 7))
                                    tt("dve", gb[:, v, (blk - 4) * 512:(blk - 3) * 512], banks[1 + v][:, :],
                                       gbias[:, (blk - 4) * 512:(blk - 3) * 512], ALU.add, [BK[1 + v], "gbias"], ["gb"])
                        tt("dve", modT[:, 0:16, :], banks[0][:, 0:32].rearrange("p (f v) -> p f v", v=2),
                           modbcol[:, 0:16].unsqueeze(2).to_broadcast([128, 16, 2]), ALU.add, ["b0", "modbcol"], ["modT"])
                        for v in range(2):
                            tsc("dve", acol[:, v * 8:(v + 1) * 8], modT[:, 8:16, v], 1.0, None, ALU.add, None, ["modT"], ["acol"])
                            tt("dve", acol[:, v * 8:(v + 1) * 8], acol[:, v * 8:(v + 1) * 8], nwcol[:], ALU.mult, ["acol", "nwcol"], ["acol"])
                            cp("dve", shcol[:, v * 8:(v + 1) * 8], modT[:, 0:8, v], ["modT"], ["shcol"])
                        if dbg:
                            dm = nc.dram_tensor(f"dbg_mod{l}", [128, 48 + 32], F32, kind="ExternalOutput").ap()
                            dma(dm[:, 0:32], modT[:, 0:16, :].rearrange("p f v -> p (f v)"), "dbg", ["modT"], ["dbg_mod"])
                            dma(dm[:, 48:64], acol[:], "dbg", ["acol"], ["dbg_mod1"])
                            dma(dm[:, 64:80], shcol[:], "dbg", ["shcol"], ["dbg_mod2"])
                            dg = nc.dram_tensor(f"dbg_gb{l}", [128, 2048], F32, kind="ExternalOutput").ap()
                            dma(dg, gb[:].rearrange("p v n -> p (v n)"), "dbg", ["gb"], ["dbg_gb"])
                        if "p0" in phases:
                            ht = [sbuf(PS, f"ht{i}", [128, 1024]) for i in range(3)]
                            hn2 = [sbuf(PS, f"hn{i}", [128, 1024]) for i in range(2)]; sqj2 = [sbuf(PS, f"sqj{i}", [128, 1024]) for i in range(2)]
                            ssqp2 = [sbuf(PS, f"ssq{i}", [128, 2]) for i in range(2)]; uTt = [sbuf(PS, f"uTt{i}", [128, 8, 128], BF16) for i in range(2)]

                            def load_h(ci):
                                t_ = ht[ci % 3]; k_ = f"ht{ci % 3}"
                                for (ap, p0, p1) in hsrc(l, ci):
                                    dma(t_[p0:p1, :], ap, k_, [f"h{l}_{ci}"], [k_])

                            def p0s1(ci):
                                t_ = ht[ci % 3]; k_ = f"ht{ci % 3}"
                                p_ = ci % 2; hn = hn2[p_]; khn = f"hn{p_}"; sq = sqj2[p_]; ksq = f"sqj{p_}"; ssq = ssqp2[p_]; kss = f"ssq{p_}"
                                mset("dve", ssq[:, 0:1], 0.0, [kss])
                                act(sq[:], t_[:], AF.Square, [k_, kss], [ksq, kss], accum=ssq[:, 0:1])
                                rstd(ssq[:, 1:2], ssq[:, 0:1], 1.0 / D, [kss], [kss])
                                tsc("dve", hn[:], t_[:], ssq[:, 1:2], None, ALU.mult, None, [k_, kss], [khn])

                            def p0s2(ci):
                                v = 1 if ci < 2 else 0
                                p_ = ci % 2; hn = hn2[p_]; khn = f"hn{p_}"
                                for k in range(8):
                                    bk = 2 * p_ + k // 4
                                    mm(banks[bk][:, (k % 4) * 128:(k % 4 + 1) * 128], hn[:, k * 128:(k + 1) * 128], IDENT,
                                       [khn, "cst"], [BK[bk]])
                                u_ = uTt[ci % 2]; uk = f"uTt{ci % 2}"
                                for k in range(8):
                                    bk = 2 * p_ + k // 4
                                    src_ = banks[bk][:, (k % 4) * 128:(k % 4 + 1) * 128]
                                    a_ = acol[:, v * 8 + k:v * 8 + k + 1]; b_ = shcol[:, v * 8 + k:v * 8 + k + 1]
                                    if k // 4 == 0:
                                        act(u_[:, k, :], src_, AF.Identity, [BK[bk], "acol", "shcol"], [uk], bias=b_, scale=a_)
                                    else:
                                        tsc("dve", u_[:, k, :], src_, a_, b_, ALU.mult, ALU.add, [BK[bk], "acol", "shcol"], [uk])
                                dma(uT_v[:, :, ci * 128:(ci + 1) * 128], u_[:], "uTst", [uk], [f"uT{ci}"])
                            load_h(0); load_h(1)
                            p0s1(0)
                            for ci in range(NCH):
                                if ci + 2 < NCH:
                                    load_h(ci + 2)
                                if ci + 1 < NCH:
                                    p0s1(ci + 1)
                                p0s2(ci)
                        P.barrier()

                def phase_p4():
                    with ExitStack() as PS:
                        wout = sbuf(PS, "wout", [128, 16, 1024], BF16)
                        dma(wout[:], wout_d[l].rearrange("(j p) n -> p j n", p=128), "wout", (), ["wout"], eng="pool")
                        ct = [sbuf(PS, f"ct{i}", [128, 2048], BF16) for i in range(2)]
                        hr = [sbuf(PS, f"hr{i}", [128, 1024]) for i in range(2)]
                        catT2 = [sbuf(PS, f"catT{i}", [128, 16, 128], BF16) for i in range(2)]
                        hnew2 = [sbuf(PS, f"hnew{i}", [128, 1024]) for i in range(2)]; sq2 = [sbuf(PS, f"sq4{i}", [128, 1024]) for i in range(2)]
                        ssq2 = [sbuf(PS, f"ssq4{i}", [128, 2]) for i in range(2)]
                        fnwb = sbuf(PS, "fnwb", [128, 1024])
                        if last:
                            dma(fnwb[:], fnw_d.unsqueeze(0).partition_broadcast(128), "pp", (), ["fnwb"])
                        chunks = list(range(2, NCH)) if last else list(range(NCH))

                        def load(i):
                            ci = chunks[i]
                            dma(ct[i % 2][:], cat_d[ci * 128:(ci + 1) * 128, :], f"ct{i % 2}",
                                [f"cat_ssd{ci}", f"cat_gla{ci}", f"cat_s5{ci}"], [f"ct{i % 2}"])
                            for (ap, p0, p1) in hsrc(l, ci):
                                dma(hr[i % 2][p0:p1, :], ap, f"hr{i % 2}", [f"h{l}_{ci}"], [f"hr{i % 2}"])
                        load(0)
                        for i, ci in enumerate(chunks):
                            if i + 1 < len(chunks):
                                load(i + 1)
                            p_ = i % 2
                            c_ = ct[p_]; ck = f"ct{p_}"; h_ = hr[p_]; hk = f"hr{p_}"; v = 1 if ci < 2 else 0
                            catT = catT2[p_]; kc = f"catT{p_}"; hnew = hnew2[p_]; kh = f"hnew{p_}"; sq = sq2[p_]; ksq = f"sq4{p_}"; ssq = ssq2[p_]; kss = f"ssq4{p_}"
                            TB = (0, 1) if p_ == 0 else (2, 3); OB = (4, 5) if p_ == 0 else (6, 7)
                            for r in range(2):
                                for jj in range(8):
                                    j = r * 8 + jj; bk = TB[jj // 4]
                                    mm(banks[bk][:, (jj % 4) * 128:(jj % 4 + 1) * 128], c_[:, j * 128:(j + 1) * 128], identb[:], [ck, "identb"], [BK[bk]])
                                for q in range(2):
                                    cp("act" if q % 2 else "dve", catT[:, r * 8 + q * 4:r * 8 + (q + 1) * 4, :].rearrange("p a b -> p (a b)"), banks[TB[q]][:, :],
                                       [BK[TB[q]]], [kc])
                            for nb in range(2):
                                for j in range(16):
                                    mm(banks[OB[nb]][:, :], catT[:, j, :], wout[:, j, nb * 512:(nb + 1) * 512], [kc, "wout"],
                                       [BK[OB[nb]]], start=(j == 0), stop=(j == 15))
                            for nb in range(2):
                                tt("dve", hnew[:, nb * 512:(nb + 1) * 512], banks[OB[nb]][:, :], gb[:, v, nb * 512:(nb + 1) * 512],
                                   ALU.mult, [BK[OB[nb]], "gb"], [kh])
                            tt("pool", hnew[:], hnew[:], h_[:], ALU.add, [kh, hk], [kh])
                            if last:
                                mset("dve", ssq[:, 0:1], 0.0, [kss])
                                act(sq[:], hnew[:], AF.Square, [kh, kss], [ksq, kss], accum=ssq[:, 0:1])
                                rstd(ssq[:, 1:2], ssq[:, 0:1], 1.0 / D, [kss], [kss])
                                stt(hnew[:], hnew[:], ssq[:, 1:2], fnwb[:], ALU.mult, ALU.mult, [kh, kss, "fnwb"], [kh])
                            for (ap, p0, p1) in hdst(l, ci):
                                dma(ap, hnew[p0:p1, :], f"hst{p_}", [kh], [f"h{l + 1}_{ci}"])
                        P.barrier()


                def load_uc(tile, key, ci, halo):
                    s0 = ci * 128
                    lo, hi = (0, LC) if ci < 2 else (LC, LT)
                    a, b = max(s0 - halo, lo), min(s0 + 128 + halo, hi)
                    if halo and (a > s0 - halo):
                        mset("pool", tile[:, :, 0:halo], 0.0, [key])
                    if halo and (b < s0 + 128 + halo):
                        mset("pool", tile[:, :, 128 + halo:128 + 2 * halo], 0.0, [key])
                    c0 = a - (s0 - halo)
                    rd = [f"uT{c}" for c in range(max(ci - 1, 0), min(ci + 2, NCH))]
                    dma(tile[:, :, c0:c0 + (b - a)], uT_v[:, :, a:b], key, rd, [key])

                def phase_ssd():
                    wv = win_d[l].rearrange("(k p) n -> p k n", p=128)
                    bc_v = bc_d.rearrange("(j p) t -> p j t", p=128)
                    with ExitStack() as PA:
                        wxbc = sbuf(PA, "wxbc", [128, 8, 1536], BF16); wdt = sbuf(PA, "wdt", [128, 8, 32], BF16)
                        wz = sbuf(PA, "wz", [128, 8, 1024], BF16); szt = [sbuf(PA, f"szt{i}", [128, 1024]) for i in range(2)]
                        dma(wz[:], wv[:, :, OZ:OZ + 1024], "w1", (), ["wz"], eng="pool")
                        dma(wxbc[:], wv[:, :, OXBC:OXBC + 1536], "w2", (), ["wxbc"], eng="pool")
                        dma(wdt[:], wv[:, :, ODT:ODT + 32], "w3", (), ["wdt"], eng="pool")
                        convw = sbuf(PA, "convw", [128, 60]); convb = sbuf(PA, "convb", [128, 12])
                        dma(convw[:], convw_d[l], "pp", (), ["convw"]); dma(convb[:], convb_d[l], "pp", (), ["convb"])
                        cdiag = sbuf(PA, "cdiag", [128, 60, 128], BF16)
                        for i in range(60):
                            if i % 2:
                                act(cdiag[:, i, :], IDENT, AF.Copy, ["cst", "convw"], ["cdiag"], scale=convw[:, i:i + 1])
                            else:
                                tsc("dve", cdiag[:, i, :], IDENT, convw[:, i:i + 1], None, ALU.mult, None, ["cst", "convw"], ["cdiag"])
                        dtb = sbuf(PA, "dtb", [128, 32])
                        dma(dtb[:], dtb_d[l:l + 1, :].partition_broadcast(128), "pp", (), ["dtb"])
                        uts = [sbuf(PA, f"auts{i}", [128, 8, 260], BF16) for i in range(2)]
                        xpre = sbuf(PA, "xpre", [128, 12, 260], BF16); xT = sbuf(PA, "xT", [128, 12, 256], BF16)
                        xbt = [sbuf(PA, f"xbt{i}", [128, 1280], BF16) for i in range(2)]
                        dts = sbuf(PA, "dts", [128, 2, 32])
                        NSC = LT // 256

                        def load_sc(i):
                            s0 = i * 256; t_ = uts[i % 2]; k_ = f"auts{i % 2}"
                            lo, hi = (0, LC) if i == 0 else (LC, LT)
                            a_, b_ = max(s0 - 2, lo), min(s0 + 258, hi)
                            if a_ > s0 - 2:
                                mset("pool", t_[:, :, 0:2], 0.0, [k_])
                            if b_ < s0 + 258:
                                mset("pool", t_[:, :, 258:260], 0.0, [k_])
                            c0 = a_ - (s0 - 2)
                            rd = [f"uT{c}" for c in range(max(2 * i - 1, 0), min(2 * i + 3, NCH))]
                            dma(t_[:, :, c0:c0 + (b_ - a_)], uT_v[:, :, a_:b_], k_, rd, [k_])
                        load_sc(0)
                        for i in range(NSC):
                            if i + 1 < NSC:
                                load_sc(i + 1)
                            s0 = i * 256; u_ = uts[i % 2]; uk = f"auts{i % 2}"
                            for j in range(12):
                                bk = j % 2
                                for k in range(8):
                                    mm(banks[bk][:, 0:260], wxbc[:, k, j * 128:(j + 1) * 128], u_[:, k, :], ["wxbc", uk], [BK[bk]],
                                       start=(k == 0), stop=(k == 7))
                                cp("act" if j % 2 else "dve", xpre[:, j, :], banks[bk][:, 0:260], [BK[bk]], ["xpre"])
                            for j in range(12):
                                bk = 2 + j % 2
                                for k in range(5):
                                    mm(banks[bk][:, 0:256], cdiag[:, j * 5 + k, :], xpre[:, j, k:k + 256], ["cdiag", "xpre"], [BK[bk]],
                                       start=(k == 0), stop=(k == 4))
                                act(xT[:, j, :], banks[bk][:, 0:256], AF.Silu, [BK[bk], "convb"], ["xT"], bias=convb[:, j:j + 1])
                            dma(bc_v[:, :, s0:s0 + 256], xT[:, 8:12, :], "bcst", ["xT"], [f"bc{i}"])
                            for hf in range(2):
                                xb_ = xbt[hf]; xk = f"xbt{hf}"
                                for j in range(8):
                                    bk = 4 + j // 4
                                    mm(banks[bk][:, (j % 4) * 128:(j % 4 + 1) * 128], xT[:, j, hf * 128:(hf + 1) * 128], identb[:], ["xT", "identb"], [BK[bk]])
                                for g in range(2):
                                    mm(banks[6][:, g * 128:(g + 1) * 128], xT[:, 8 + g, hf * 128:(hf + 1) * 128], identb[:], ["xT", "identb"], ["b6"])
                                cp("act", xb_[:, 0:512], banks[4][:, :], ["b4"], [xk])
                                cp("dve", xb_[:, 512:1024], banks[5][:, :], ["b5"], [xk])
                                cp("dve", xb_[:, 1024:1280], banks[6][:, 0:256], ["b6"], [xk])
                                dma(xb_d[s0 + hf * 128:s0 + (hf + 1) * 128, :], xb_[:], f"xbst{hf}", [xk], [f"xb{2 * i + hf}"])
                                for k in range(8):
                                    mm(banks[7][:, hf * 32:(hf + 1) * 32], u_[:, k, 2 + hf * 128:2 + (hf + 1) * 128], wdt[:, k, :], [uk, "wdt"], ["b7"],
                                       start=(k == 0), stop=(k == 7))
                                for nb in range(2):
                                    for k in range(8):
                                        mm(banks[nb][:, :], u_[:, k, 2 + hf * 128:2 + (hf + 1) * 128], wz[:, k, nb * 512:(nb + 1) * 512], [uk, "wz"], [BK[nb]],
                                           start=(k == 0), stop=(k == 7))
                                    act(szt[hf][:, nb * 512:(nb + 1) * 512], banks[nb][:, :], AF.Silu, [BK[nb]], [f"szt{hf}"])
                                dma(sz_d[s0 + hf * 128:s0 + (hf + 1) * 128, :], szt[hf][:], f"szst{hf}", [f"szt{hf}"], [f"sz{2 * i + hf}"])
                            tt("dve", dts[:], banks[7][:, 0:64].rearrange("p (a b) -> p a b", a=2), dtb[:].unsqueeze(1).to_broadcast([128, 2, 32]), ALU.add,
                               ["b7", "dtb"], ["dts"])
                            act(dts[:], dts[:], AF.Exp, ["dts"], ["dts"])
                            act(dts[:], dts[:], AF.Ln, ["dts"], ["dts"], bias=1.0)
                            dma(dt_d[s0:s0 + 256, :].rearrange("(a p) c -> p a c", p=128), dts[:], "dtst", ["dts"], [f"dt{2 * i}", f"dt{2 * i + 1}"])
                        P.barrier()
                    with ExitStack() as PS:
                        aneg = sbuf(PS, "aneg", [128, 32]); dskb = sbuf(PS, "dskb", [128, 16]); nwb = sbuf(PS, "nwb", [128, 1024])
                        dma(aneg[:], alog_d[l:l + 1, :].partition_broadcast(128), "pp", (), ["aneg"])
                        dma(dskb[:], ssdd_d[l:l + 1, :].partition_broadcast(128), "pp", (), ["dskb"])
                        dma(nwb[:], ssdnw_d[l:l + 1, :].partition_broadcast(128), "pp", (), ["nwb"])
                        act(aneg[:], aneg[:], AF.Exp, ["aneg"], ["aneg"])
                        tsc("dve", aneg[:], aneg[:], -1.0, None, ALU.mult, None, ["aneg"], ["aneg"])
                        uc = [sbuf(PS, f"uc{i}", [128, 1024]) for i in range(3)]
                        bct = [sbuf(PS, f"bct{i}", [128, 4, 128], BF16) for i in range(3)]
                        xbl = [sbuf(PS, f"xbl{i}", [128, 1280], BF16) for i in range(3)]
                        dtl = [sbuf(PS, f"dtl{i}", [128, 32]) for i in range(3)]
                        xw2 = [sbuf(PS, f"xw{i}", [128, 1024], BF16) for i in range(2)]
                        xdt = sbuf(PS, "xdt", [128, 1024], BF16)
                        da = sbuf(PS, "da", [128, 16]); ex2 = [sbuf(PS, f"ex{i}", [128, 48]) for i in range(2)]; wsc = sbuf(PS, "wsc", [128, 16])
                        cbm = sbuf(PS, "cbm", [128, 256])
                        lh = [sbuf(PS, f"lh{i}", [128, 128]) for i in range(8)]
                        E = [sbuf(PS, f"E{i}", [128, 512]) for i in range(2)]
                        ST = sbuf(PS, "ST", [128, 16, 128], BF16)
                        H = sbuf(PS, "H", [128, 1024]); Hb = sbuf(PS, "Hb", [128, 1024], BF16)
                        ydir = sbuf(PS, "ydir", [128, 1024]); ybl = sbuf(PS, "ybl", [128, 1024]); sz = sbuf(PS, "sz", [128, 1024])
                        tmp = sbuf(PS, "tmp", [128, 1024]); ss2 = sbuf(PS, "ss2", [128, 4]); cto = sbuf(PS, "cto", [128, 1024], BF16)
                        for d in (1, 0):
                            INC, STR = (UINC, USTR) if d == 0 else (LINC, LSTR)
                            order = sweep_order(d)
                            mset("dve", H[:], 0.0, ["H"]); mset("pool", Hb[:], 0.0, ["Hb"])

                            def loads(i):
                                ci = order[i]; s0 = ci * 128; b_ = i % 3
                                fin_ = (d == 0) and not (last and ci < 2)
                                dma(bct[b_][:], bc_v[:, :, s0:s0 + 128], f"bct{b_}", [f"bc{ci // 2}"], [f"bct{b_}"])
                                dma(xbl[b_][:], xb_d[s0:s0 + 128, :], f"xbl{b_}", [f"xb{ci}"], [f"xbl{b_}"])
                                dma(dtl[b_][:], dt_d[s0:s0 + 128, :], f"dtl{b_}", [f"dt{ci}"], [f"dtl{b_}"])
                                if fin_:
                                    dma(uc[b_][:], sz_d[s0:s0 + 128, :], f"uc{b_}", [f"sz{ci}"], [f"uc{b_}"])

                            def nm(i):
                                b_ = i % 3; p_ = i % 2
                                return dict(bc_=bct[b_], bck=f"bct{b_}", xb_=xbl[b_], xk=f"xbl{b_}", dt=dtl[b_], dk=f"dtl{b_}", u_=uc[b_], uk=f"uc{b_}",
                                            ex=ex2[p_], kex=f"ex{p_}", xw=xw2[p_], kxw=f"xw{p_}", YB=((2, 3) if p_ == 0 else (4, 5)))

                            def ss1(i):
                                n = nm(i); bc_, bck, xb_, xk, dt, dk, ex, kex, xw, kxw, YB = (n[k_] for k_ in ("bc_", "bck", "xb_", "xk", "dt", "dk", "ex", "kex", "xw", "kxw", "YB"))
                                xtm = xb_[:, 0:1024]
                                tt("pool", xdt[:].rearrange("p (h q) -> p h q", h=16), xtm.rearrange("p (h q) -> p h q", h=16),
                                   dt[:, d * 16:(d + 1) * 16].unsqueeze(2).to_broadcast([128, 16, 64]), ALU.mult, [xk, dk], ["xdt"])
                                tt("dve", da[:], dt[:, d * 16:(d + 1) * 16], aneg[:, d * 16:(d + 1) * 16], ALU.mult, [dk, "aneg"], ["da"])
                                mm(banks[7][:, 32:48], INC, da[:], ["cst", "da"], ["b7"])
                                mm(banks[7][:, 48:64], STR, da[:], ["cst", "da"], ["b7"])
                                mm(banks[7][:, 64:80], ONES, da[:], ["cst", "da"], ["b7"])
                                act(ex[:], banks[7][:, 32:80], AF.Exp, ["b7"], [kex])
                                tt("dve", wsc[:], ex[:, 16:32], dt[:, d * 16:(d + 1) * 16], ALU.mult, [kex, dk], ["wsc"])
                                tt("pool", xw[:].rearrange("p (h q) -> p h q", h=16), xtm.rearrange("p (h q) -> p h q", h=16),
                                   wsc[:].unsqueeze(2).to_broadcast([128, 16, 64]), ALU.mult, [xk, "wsc"], [kxw])
                                for g in range(2):
                                    mm(banks[6][:, 256 + g * 128:256 + (g + 1) * 128], bc_[:, g, :], bc_[:, 2 + g, :], [bck], ["b6"])
                                tt("dve", cbm[:].rearrange("p (g q) -> p g q", g=2), banks[6][:, 256:512].rearrange("p (g q) -> p g q", g=2),
                                   INC.unsqueeze(1).to_broadcast([128, 2, 128]), ALU.mult, ["b6", "cst"], ["cbm"])

                                def grp_lh(gq):
                                    for h in range(gq * 4, gq * 4 + 4):
                                        li = h % 8; bk = gq % 2
                                        act(lh[li][:], STR, AF.Copy, ["cst", "da"], [f"lh{li}"], scale=da[:, h:h + 1])
                                        mm(banks[bk][:, (h % 4) * 128:(h % 4 + 1) * 128], lh[li][:], INC, [f"lh{li}", "cst"], [BK[bk]])

                                def grp_E(gq):
                                    bk = gq % 2; g = gq // 2
                                    act(E[bk][:], banks[bk][:, :], AF.Exp, [BK[bk]], [f"E{bk}"])
                                    tt("dve", ST[:, gq * 4:(gq + 1) * 4, :], E[bk][:].rearrange("p (h q) -> p h q", h=4),
                                       cbm[:, g * 128:(g + 1) * 128].unsqueeze(1).to_broadcast([128, 4, 128]), ALU.mult, [f"E{bk}", "cbm"], ["ST"])
                                grp_lh(0); grp_lh(1); grp_E(0); grp_lh(2); grp_E(1); grp_lh(3); grp_E(2); grp_E(3)

                            def ss1b(i):
                                YB = nm(i)["YB"]
                                for h in range(16):
                                    bk = YB[h // 8]
                                    mm(banks[bk][:, (h % 8) * 64:(h % 8 + 1) * 64], ST[:, h, :], xdt[:, h * 64:(h + 1) * 64], ["ST", "xdt"], [BK[bk]])

                            def ss2f(i):
                                ci = order[i]; s0 = ci * 128
                                n = nm(i); bc_, bck, xb_, xk, u_, uk, ex, kex, xw, kxw, YB = (n[k_] for k_ in ("bc_", "bck", "xb_", "xk", "u_", "uk", "ex", "kex", "xw", "kxw", "YB"))
                                xtm = xb_[:, 0:1024]; btm = xb_[:, 1024:1280]
                                fin = (d == 0) and not (last and ci < 2)
                                for g in range(2):
                                    mm(banks[6 + g][:, :], bc_[:, 2 + g, :], Hb[:, g * 512:(g + 1) * 512], [bck, "Hb"], [BK[6 + g]])
                                for g in range(2):
                                    tt("dve", ydir[:, g * 512:(g + 1) * 512].rearrange("p (h q) -> p h q", h=8),
                                       banks[6 + g][:, :].rearrange("p (h q) -> p h q", h=8),
                                       ex[:, g * 8:(g + 1) * 8].unsqueeze(2).to_broadcast([128, 8, 64]), ALU.mult, [BK[6 + g], kex], ["ydir"])
                                    tt("dve", ydir[:, g * 512:(g + 1) * 512], ydir[:, g * 512:(g + 1) * 512], banks[YB[g]][:, :], ALU.add,
                                       ["ydir", BK[YB[g]]], ["ydir"])
                                for g in range(2):
                                    mm(banks[6 + g][:, :], btm[:, g * 128:(g + 1) * 128], xw[:, g * 512:(g + 1) * 512], [xk, kxw], [BK[6 + g]])
                                for g in range(2):
                                    tt("dve", H[:, g * 512:(g + 1) * 512], H[:, g * 512:(g + 1) * 512], banks[6 + g][:, :], ALU.add,
                                       ["H", BK[6 + g]], ["H"])
                                cp("act", Hb[:], H[:], ["H"], ["Hb"])
                                if d == 1:
                                    dma(yb_ssd_d[s0:s0 + 128, :], ydir[:], "ybst", ["ydir"], [f"ybs{ci}"])
                                elif fin:
                                    dma(ybl[:], yb_ssd_d[s0:s0 + 128, :], "ybld", [f"ybs{ci}"], ["ybl"])
                                    tt("dve", ydir[:], ydir[:], ybl[:], ALU.add, ["ydir", "ybl"], ["ydir"])
                                    tt("pool", tmp[:].rearrange("p (h q) -> p h q", h=16), xtm.rearrange("p (h q) -> p h q", h=16),
                                       dskb[:].unsqueeze(2).to_broadcast([128, 16, 64]), ALU.mult, [xk, "dskb"], ["tmp"])
                                    tt("dve", ydir[:], ydir[:], tmp[:], ALU.add, ["ydir", "tmp"], ["ydir"])
                                    tt("dve", ydir[:], ydir[:], u_[:], ALU.mult, ["ydir", uk], ["ydir"])
                                    mset("dve", ss2[:], 0.0, ["ss2"])
                                    for g in range(2):
                                        act(tmp[:, g * 512:(g + 1) * 512], ydir[:, g * 512:(g + 1) * 512], AF.Square, ["ydir", "ss2"], ["tmp", "ss2"],
                                            accum=ss2[:, g:g + 1])
                                    rstd(ss2[:, 2:4], ss2[:, 0:2], 1.0 / 512, ["ss2"], ["ss2"])
                                    for g in range(2):
                                        stt(cto[:, g * 512:(g + 1) * 512], ydir[:, g * 512:(g + 1) * 512], ss2[:, 2 + g:3 + g],
                                            nwb[:, g * 512:(g + 1) * 512], ALU.mult, ALU.mult, ["ydir", "ss2", "nwb"], ["cto"])
                                    dma(cat_d[s0:s0 + 128, 0:1024], cto[:], "catst", ["cto"], [f"cat_ssd{ci}"])
                            def sdec(i):
                                ex = ex2[i % 2]; kex = f"ex{i % 2}"
                                tt("pool", H[:].rearrange("p (h q) -> p h q", h=16), H[:].rearrange("p (h q) -> p h q", h=16),
                                   ex[:, 32:48].unsqueeze(2).to_broadcast([128, 16, 64]), ALU.mult, ["H", kex], ["H"])
                            n_ = len(order)
                            loads(0); loads(1)
                            ss1(0); ss1b(0)
                            for i in range(n_):
                                if i + 2 < n_:
                                    loads(i + 2)
                                if i > 0:
                                    sdec(i)
                                if i + 1 < n_:
                                    ss1(i + 1)
                                ss2f(i)
                                if i + 1 < n_:
                                    ss1b(i + 1)
                        P.barrier()

                def phase_gla():
                    wv = win_d[l].rearrange("(k p) n -> p k n", p=128)
                    qk_v = qk_d.rearrange("(j p) t -> p j t", p=128)
                    with ExitStack() as PA:
                        wq = sbuf(PA, "wq", [128, 8, 256], BF16); wk = sbuf(PA, "wk", [128, 8, 256], BF16)
                        wvv = sbuf(PA, "wvv", [128, 8, 512], BF16); wlr = sbuf(PA, "wlr", [128, 8, 32], BF16)
                        wgg = sbuf(PA, "wgg", [128, 8, 512], BF16); sgt2 = [sbuf(PA, f"sgt{i}", [128, 512]) for i in range(2)]
                        dma(wgg[:], wv[:, :, OGG:OGG + 512], "w3", (), ["wgg"], eng="pool")
                        dma(wq[:], wv[:, :, OQ:OQ + 256], "w1", (), ["wq"], eng="pool")
                        dma(wk[:], wv[:, :, OK_:OK_ + 256], "w2", (), ["wk"], eng="pool")
                        dma(wvv[:], wv[:, :, OV:OV + 512], "w3", (), ["wvv"], eng="pool")
                        dma(wlr[:], wv[:, :, OLR:OLR + 32], "w2", (), ["wlr"], eng="pool")
                        wlrp = sbuf(PA, "wlrp", [16, 512]); blrb = sbuf(PA, "blrb", [128, 512])
                        dma(wlrp[:], wlr_d[l], "pp", (), ["wlrp"])
                        dma(blrb[:], blr_d[l:l + 1, :].partition_broadcast(128), "pp", (), ["blrb"])
                        uts = [sbuf(PA, f"guts{i}", [128, 8, 256], BF16) for i in range(2)]
                        qkT = [sbuf(PA, f"qkT{i}", [128, 4, 256], BF16) for i in range(2)]
                        kvt = [sbuf(PA, f"kvt{i}", [128, 768], BF16) for i in range(2)]
                        lrT = sbuf(PA, "lrT", [16, 4, 128]); gsb = [sbuf(PA, f"gsb{i}", [128, 512]) for i in range(2)]
                        NSC = LT // 256

                        def load_sc(i):
                            s0 = i * 256
                            rd = [f"uT{2 * i}", f"uT{2 * i + 1}"]
                            dma(uts[i % 2][:], uT_v[:, :, s0:s0 + 256], f"guts{i % 2}", rd, [f"guts{i % 2}"])
                        load_sc(0)
                        for i in range(NSC):
                            if i + 1 < NSC:
                                load_sc(i + 1)
                            s0 = i * 256; u_ = uts[i % 2]; uk = f"guts{i % 2}"; qk_ = qkT[i % 2]; qkk = f"qkT{i % 2}"
                            for j in range(4):
                                w_ = wq if j < 2 else wk; c2 = j % 2; bk = j % 2
                                for k in range(8):
                                    mm(banks[bk][:, 0:256], w_[:, k, c2 * 128:(c2 + 1) * 128], u_[:, k, :], ["wq", "wk", uk], [BK[bk]], start=(k == 0), stop=(k == 7))
                                cp("act" if j % 2 else "dve", qk_[:, j, :], banks[bk][:, 0:256], [BK[bk]], [qkk])
                            dma(qk_v[:, :, s0:s0 + 256], qk_[:], f"qkst{i % 2}", [qkk], [f"qk{i}"])
                            for hf in range(2):
                                kv_ = kvt[hf]; kvk = f"kvt{hf}"; gs_ = gsb[hf]; gk = f"gsb{hf}"
                                usl = slice(hf * 128, (hf + 1) * 128)
                                for k in range(8):
                                    mm(banks[2][:, 0:256], u_[:, k, usl], wk[:, k, :], [uk, "wk"], ["b2"], start=(k == 0), stop=(k == 7))
                                for k in range(8):
                                    mm(banks[3][:, :], u_[:, k, usl], wvv[:, k, :], [uk, "wvv"], ["b3"], start=(k == 0), stop=(k == 7))
                                cp("dve", kv_[:, 0:256], banks[2][:, 0:256], ["b2"], [kvk])
                                cp("act", kv_[:, 256:768], banks[3][:, :], ["b3"], [kvk])
                                for k in range(8):
                                    mm(banks[7][:, :], u_[:, k, usl], wgg[:, k, :], [uk, "wgg"], ["b7"], start=(k == 0), stop=(k == 7))
                                act(sgt2[hf][:], banks[7][:, :], AF.Silu, ["b7"], [f"sgt{hf}"])
                                dma(sgg_d[s0 + hf * 128:s0 + (hf + 1) * 128, :], sgt2[hf][:], f"sgst{hf}", [f"sgt{hf}"], [f"sgg{2 * i + hf}"])
                                dma(kv_d[s0 + hf * 128:s0 + (hf + 1) * 128, :], kv_[:], f"kvst{hf}", [kvk], [f"kv{2 * i + hf}"])
                                for dd in range(2):
                                    for k in range(8):
                                        mm(banks[4][0:16, (hf * 2 + dd) * 128:(hf * 2 + dd + 1) * 128], wlr[:, k, dd * 16:(dd + 1) * 16], u_[:, k, usl], ["wlr", uk], ["b4"],
                                           start=(k == 0), stop=(k == 7))
                            cp("dve", lrT[:].rearrange("p a b -> p (a b)"), banks[4][0:16, :], ["b4"], ["lrT"])
                            for hf in range(2):
                                gs_ = gsb[hf]; gk = f"gsb{hf}"
                                for dd in range(2):
                                    mm(banks[5 + hf][:, dd * 256:(dd + 1) * 256], lrT[:, hf * 2 + dd, :], wlrp[:, dd * 256:(dd + 1) * 256], ["lrT", "wlrp"], [BK[5 + hf]])
                                tt("dve", gs_[:], banks[5 + hf][:, :], blrb[:], ALU.add, [BK[5 + hf], "blrb"], [gk])
                                act(gs_[:], gs_[:], AF.Exp, [gk], [gk], scale=-1.0)
                                act(gs_[:], gs_[:], AF.Ln, [gk], [gk], bias=1.0)
                                tsc("dve", gs_[:], gs_[:], -1.0 / 16.0, None, ALU.mult, None, [gk], [gk])
                                dma(g_d[s0 + hf * 128:s0 + (hf + 1) * 128, :], gs_[:], f"gst{hf}", [gk], [f"gg{2 * i + hf}"])
                        P.barrier()
                    with ExitStack() as PS:
                        gnwb = sbuf(PS, "gnwb", [128, 128])
                        dma(gnwb[:], gnw_d[l:l + 1, :].partition_broadcast(128), "pp", (), ["gnwb"])
                        uc = [sbuf(PS, f"guc{i}", [128, 512]) for i in range(3)]
                        qkl = [sbuf(PS, f"qkl{i}", [128, 4, 128], BF16) for i in range(3)]
                        kvl = [sbuf(PS, f"kvl{i}", [128, 768], BF16) for i in range(3)]
                        gl = [sbuf(PS, f"gl{i}", [128, 256]) for i in range(3)]
                        eb2 = [sbuf(PS, f"eb{i}", [128, 256]) for i in range(2)]; enb = sbuf(PS, "enb", [128, 256]); er = sbuf(PS, "er", [128, 256])
                        kd2 = [sbuf(PS, f"kd{i}", [128, 256], BF16) for i in range(2)]
                        STg2 = [sbuf(PS, f"STg{i}", [128, 4, 128], BF16) for i in range(2)]
                        Sg = sbuf(PS, "Sg", [128, 2, 128]); Sgb = sbuf(PS, "Sgb", [128, 2, 128], BF16)
                        od = sbuf(PS, "od", [128, 512]); obl = sbuf(PS, "obl", [128, 512]); sgt = sbuf(PS, "sgt", [128, 512])
                        qeTz2 = [sbuf(PS, f"qeTz{i}", [128, 4, 128], BF16) for i in range(2)]; keTz = sbuf(PS, "keTz", [128, 4, 128], BF16)
                        for i in range(2):
                            mset("pool", qeTz2[i][:], 0.0, [f"qeTz{i}"])
                        mset("pool", keTz[:], 0.0, ["keTz"])
                        er2 = sbuf(PS, "er2", [128, 256]); lnq = sbuf(PS, "lnq", [128, 1])
                        mset("dve", lnq[:], math.log(0.125), ["lnq"])
                        tmp = sbuf(PS, "gtmp", [128, 512]); ss4 = sbuf(PS, "ss4", [128, 8]); cto = sbuf(PS, "gcto", [128, 512], BF16)
                        for d in (1, 0):
                            INC, STR = (UINC, USTR) if d == 0 else (LINC, LSTR)
                            lastcol = 127 if d == 0 else 0
                            order = sweep_order(d)
                            mset("dve", Sg[:], 0.0, ["Sg"]); mset("pool", Sgb[:], 0.0, ["Sgb"])

                            def gloads(i):
                                ci = order[i]; s0 = ci * 128; b_ = i % 3
                                fin_ = (d == 0) and not (last and ci < 2)
                                dma(qkl[b_][:], qk_v[:, :, s0:s0 + 128], f"qkl{b_}", [f"qk{ci // 2}"], [f"qkl{b_}"])
                                dma(kvl[b_][:], kv_d[s0:s0 + 128, :], f"kvl{b_}", [f"kv{ci}"], [f"kvl{b_}"])
                                dma(gl[b_][:], g_d[s0:s0 + 128, d * 256:(d + 1) * 256], f"gl{b_}", [f"gg{ci}"], [f"gl{b_}"])
                                if fin_:
                                    dma(uc[b_][:], sgg_d[s0:s0 + 128, :], f"guc{b_}", [f"sgg{ci}"], [f"guc{b_}"])

                            def gs1(i):
                                p_ = i % 2; b_ = i % 3
                                qk_ = qkl[b_]; qkk = f"qkl{b_}"; kv_ = kvl[b_]; kvk = f"kvl{b_}"; g_ = gl[b_]; gk = f"gl{b_}"
                                eb = eb2[p_]; keb = f"eb{p_}"; kd = kd2[p_]; kkd = f"kd{p_}"
                                STg = STg2[p_]; kst = f"STg{p_}"; qeTz = qeTz2[p_]; kq = f"qeTz{p_}"
                                for c2 in range(2):
                                    mm(banks[4][:, c2 * 128:(c2 + 1) * 128], g_[:, c2 * 128:(c2 + 1) * 128], INC, [gk, "cst"], ["b4"])
                                mm(banks[4][:, 256:512], STR, g_[:], ["cst", gk], ["b4"])
                                act(eb[:], banks[4][:, 0:256], AF.Exp, ["b4"], [keb])
                                act(enb[:], banks[4][:, 0:256], AF.Exp, ["b4"], ["enb"], scale=-1.0)
                                act(er[:], banks[4][:, 256:512], AF.Exp, ["b4"], ["er"])
                                act(er2[:], banks[4][:, 0:256], AF.Exp, ["b4", "lnq"], ["er2"], bias=lnq[:, 0:1])
                                for h in range(4):
                                    c2 = h // 2; hb = (h % 2) * 64
                                    tt("dve", qeTz[hb:hb + 64, h, :], qk_[hb:hb + 64, c2, 0:128], er2[hb:hb + 64, c2 * 128:(c2 + 1) * 128],
                                       ALU.mult, [qkk, "er2"], [kq])
                                    tt("dve", keTz[hb:hb + 64, h, :], qk_[hb:hb + 64, 2 + c2, 0:128], enb[hb:hb + 64, c2 * 128:(c2 + 1) * 128],
                                       ALU.mult, [qkk, "enb"], ["keTz"])
                                tt("dve", kd[:], kv_[:, 0:256], er[:], ALU.mult, [kvk, "er"], [kkd])
                                for h in range(4):
                                    mm(banks[5][:, h * 128:(h + 1) * 128], keTz[:, h, :], qeTz[:, h, :], ["keTz", kq], ["b5"])

                            def gs1b(i):
                                p_ = i % 2
                                STg = STg2[p_]; kst = f"STg{p_}"
                                tt("dve", STg[:], banks[5][:, :].rearrange("p (h q) -> p h q", h=4), INC.unsqueeze(1).to_broadcast([128, 4, 128]), ALU.mult,
                                   ["b5", "cst"], [kst])

                            def gs2(i):
                                ci = order[i]; p_ = i % 2; s0 = ci * 128; b_ = i % 3
                                u_ = uc[b_]; uk = f"guc{b_}"; kv_ = kvl[b_]; kvk = f"kvl{b_}"
                                vtm = kv_[:, 256:768]
                                eb = eb2[p_]; keb = f"eb{p_}"; kd = kd2[p_]; kkd = f"kd{p_}"
                                STg = STg2[p_]; kst = f"STg{p_}"; qeTz = qeTz2[p_]; kq = f"qeTz{p_}"
                                fin = (d == 0) and not (last and ci < 2)
                                for h in range(4):
                                    c2 = h // 2
                                    mm(banks[6][:, h * 128:(h + 1) * 128], STg[:, h, :], vtm[:, h * 128:(h + 1) * 128], [kst, kvk], ["b6"], start=True, stop=False)
                                    mm(banks[6][:, h * 128:(h + 1) * 128], qeTz[:, h, :], Sgb[:, c2, :], [kq, "Sgb"], ["b6"], start=False, stop=True)
                                if d == 1:
                                    cp("dve", od[:], banks[6][:, :], ["b6"], ["od"])
                                    dma(yb_gla_d[s0:s0 + 128, :], od[:], "ybst", ["od"], [f"ybg{ci}"])
                                elif fin:
                                    dma(obl[:], yb_gla_d[s0:s0 + 128, :], "ybld", [f"ybg{ci}"], ["obl"])
                                    tt("dve", od[:], banks[6][:, :], obl[:], ALU.add, ["b6", "obl"], ["od"])
                                    mset("dve", ss4[:], 0.0, ["ss4"])
                                    for h in range(4):
                                        act(tmp[:, h * 128:(h + 1) * 128], od[:, h * 128:(h + 1) * 128], AF.Square, ["od", "ss4"], ["gtmp", "ss4"],
                                            accum=ss4[:, h:h + 1])
                                    rstd(ss4[:, 4:8], ss4[:, 0:4], 1.0 / 128, ["ss4"], ["ss4"])
                                    tt("dve", od[:].rearrange("p (h q) -> p h q", h=4), od[:].rearrange("p (h q) -> p h q", h=4),
                                       ss4[:, 4:8].unsqueeze(2).to_broadcast([128, 4, 128]), ALU.mult, ["od", "ss4"], ["od"])
                                    tt("pool", od[:].rearrange("p (h q) -> p h q", h=4), od[:].rearrange("p (h q) -> p h q", h=4),
                                       gnwb[:].unsqueeze(1).to_broadcast([128, 4, 128]), ALU.mult, ["od", "gnwb"], ["od"])
                                    tt("dve", cto[:], od[:], u_[:], ALU.mult, ["od", uk], ["gcto"])
                                    dma(cat_d[s0:s0 + 128, 1024:1536], cto[:], "catst", ["gcto"], [f"cat_gla{ci}"])
                                for c2 in range(2):
                                    for var in range(2):
                                        mm(banks[7][:, (c2 * 2 + var) * 128:(c2 * 2 + var + 1) * 128], kd[:, c2 * 128:(c2 + 1) * 128],
                                           vtm[:, (2 * c2 + var) * 128:(2 * c2 + var + 1) * 128], [kkd, kvk], ["b7"])
                                for c2 in range(2):
                                    for var in range(2):
                                        hb = var * 64
                                        stt(Sg[hb:hb + 64, c2, :], Sg[hb:hb + 64, c2, :], eb[hb:hb + 64, c2 * 128 + lastcol:c2 * 128 + lastcol + 1],
                                            banks[7][hb:hb + 64, (c2 * 2 + var) * 128:(c2 * 2 + var + 1) * 128], ALU.mult, ALU.add, ["Sg", keb, "b7"], ["Sg"])
                                cp("act", Sgb[:], Sg[:], ["Sg"], ["Sgb"])
                            n_ = len(order)
                            gloads(0); gloads(1)
                            gs1(0); gs1b(0)
                            for i in range(n_):
                                if i + 2 < n_:
                                    gloads(i + 2)
                                if i + 1 < n_:
                                    gs1(i + 1)
                                gs2(i)
                                if i + 1 < n_:
                                    gs1b(i + 1)
                        P.barrier()

                def phase_s5():
                    with ExitStack() as PS:
                        wv = win_d[l].rearrange("(k p) n -> p k n", p=128)
                        Uall = sbuf(PS, "Uall", [128, 32, 544], BF16)
                        c2 = sbuf(PS, "c2", [128, 1392])
                        dma(c2[:], cst2_d, "cst2", (), ["c2"])
                        NBv = (c2[:, 48:592], c2[:, 592:1136]); M8 = (c2[:, 1136:1264], c2[:, 1264:1392])
                        SUPER = [(0, 256)] + [(256 + 1024 * i_, 1024) for i_ in range(4)]
                        I32 = mybir.dt.int32
                        with ExitStack() as PA:
                            wu5 = sbuf(PA, "wu5", [128, 8, 512], BF16)
                            dma(wu5[:], wv[:, :, OU5:OU5 + 512], "w1", (), ["wu5"], eng="pool")
                            uts = sbuf(PA, "uts", [128, 8, 1024], BF16); Xf = sbuf(PA, "Xf", [128, 8, 512])
                            Xb = sbuf(PA, "Xb", [128, 32, 8, 16], BF16)
                            for (s0, T) in SUPER:
                                nb = T // 8; blk0 = s0 // 8
                                rd = [f"uT{c}" for c in range(s0 // 128, (s0 + T) // 128)]
                                dma(uts[:, :, 0:T], uT_v[:, :, s0:s0 + T], "uts", rd, ["uts"])
                                utv = uts[:, :, 0:T].rearrange("p k (b j) -> p k j b", j=8)
                                for j in range(8):
                                    bk = j % 2
                                    for k in range(8):
                                        mm(banks[bk][0:nb, :], utv[:, k, j, :], wu5[:, k, :], ["uts", "wu5"], [BK[bk]], start=(k == 0), stop=(k == 7))
                                    if not os.environ.get("NOXF"):
                                        act(Xf[0:nb, j, :], banks[bk][0:nb, :], AF.Identity, [BK[bk]], ["Xf"])
                                    if not os.environ.get("NOXB"):
                                        cp("dve", Xb[0:nb, :, j, :], banks[bk][0:nb, :].rearrange("p (g h) -> p g h", g=32), [BK[bk]], ["Xb"])
                                if SSTOP < 0.2:
                                    continue
                                dma(u5_d[s0:s0 + T, :].rearrange("(b j) c -> b j c", j=8), Xf[0:nb, :, :], "u5st", ["Xf"], [f"u5s{s0}"])
                                if SSTOP < 0.3:
                                    continue
                                for g in range(32):
                                    bk = 2 + (g // 4) % 2
                                    mm(banks[bk][:, (g % 4) * 128:(g % 4) * 128 + nb], Xb[0:nb, g, :, :].rearrange("p a b -> p (a b)"), identb[0:nb, 0:nb],
                                       ["Xb", "identb"], [BK[bk]])
                                    if g % 4 == 3:
                                        cp("dve", Uall[:, g - 3:g + 1, blk0:blk0 + nb], banks[bk][:, :].rearrange("p (a b) -> p a b", a=4)[:, :, 0:nb],
                                           [BK[bk]], ["Uall"])
                            P.barrier()
                        if SSTOP < 1:
                            return
                        WstRe = sbuf(PS, "WstRe", [128, 32, 128], BF16); WstIm = sbuf(PS, "WstIm", [128, 32, 128], BF16)
                        Mg = sbuf(PS, "Mg", [128, 32, 128], BF16)
                        WoRe = sbuf(PS, "WoRe", [128, 32, 128], BF16); WoIm = sbuf(PS, "WoIm", [128, 32, 128], BF16)
                        mset("pool", WoRe[:], 0.0, ["WoRe"]); mset("pool", WoIm[:], 0.0, ["WoIm"])
                        sm = sbuf(PS, "sm", [128, 20, 16]); S = lambda i_: sm[:, i_, :]
                        ti_t = sbuf(PS, "ti_t", [128, 544], I32); tf_t = sbuf(PS, "tf_t", [128, 544]); tm_t = sbuf(PS, "tm_t", [128, 544])
                        tq_t = sbuf(PS, "tq_t", [128, 544])
                        IDm = sbuf(PS, "IDm", [128, 2, 128])
                        mset("dve", IDm[:], 0.0, ["IDm"])
                        cp("dve", IDm[:, 0, 0:64], IDENT[:, 0:64], ["cst", "IDm"], ["IDm"])
                        cp("dve", IDm[:, 1, 64:128], IDENT[:, 64:128], ["cst", "IDm"], ["IDm"])

                        def fturn(dst, src, n, mul, add_turn, rd, wr):
                            ti = ti_t[:, 0:n]; tf = tf_t[:, 0:n]; tm = tm_t[:, 0:n]
                            tsc("dve", dst, src, mul, add_turn, ALU.mult, ALU.add, list(rd), list(wr))
                            cp("dve", ti, dst, list(wr), ["trg"])
                            cp("dve", tf, ti, ["trg"], ["trg"])
                            tt("dve", dst, dst, tf, ALU.subtract, list(wr) + ["trg"], list(wr))

                        def sincos_turn(dsin, dcos, f, n, rd, wr):
                            t_ = tq_t[:, 0:n]
                            act(dsin, f, AF.Sin, list(rd), list(wr), scale=6.28318)
                            act(t_, f, AF.Sin, list(rd), ["tq"], scale=3.14159)
                            tt("dve", t_, t_, t_, ALU.mult, ["tq"], ["tq"])
                            tsc("dve", dcos, t_, -2.0, 1.0, ALU.mult, ALU.add, ["tq"], list(wr))

                        for d in (1, 0):
                            K1 = ["sm"]
                            with ExitStack() as PC:
                                prm = sbuf(PC, "prm", [128, 48]); bbp = sbuf(PC, "bbp", [128, 512]); ccp = sbuf(PC, "ccp", [128, 512])
                                bbr = sbuf(PC, "bbr", [128, 256]); bbi = sbuf(PC, "bbi", [128, 256]); t16 = sbuf(PC, "t16", [128, 256])
                                pw = sbuf(PC, "pw", [128, 6, 16, 24])
                                KBre = sbuf(PC, "KBre", [128, 16, 8, 16]); KBim = sbuf(PC, "KBim", [128, 16, 8, 16])
                                QCre = sbuf(PC, "QCre", [128, 16, 8, 16]); QCim = sbuf(PC, "QCim", [128, 16, 8, 16])
                                QZre = sbuf(PC, "QZre", [128, 16, 128]); QZim = sbuf(PC, "QZim", [128, 16, 128]); tK = sbuf(PC, "tK", [128, 16, 8, 16])
                                dma(prm[:], s5p_d[l, d], "pp", (), ["prm"]); dma(bbp[:], s5b_d[l], "pp", (), ["bbp"]); dma(ccp[:], s5c_d[l, d], "pp", (), ["ccp"])
                                lre = prm[:, 0:16]; lim = prm[:, 16:32]
                                act(S(0), prm[:, 32:48], AF.Exp, ["prm"], K1)
                                tt("dve", S(1), lre, S(0), ALU.mult, ["prm"] + K1, K1)
                                tt("dve", S(2), lim, S(0), ALU.mult, ["prm"] + K1, K1)
                                act(S(3), S(1), AF.Exp, K1, K1)
                                fturn(S(6), S(2), 16, 1.0 / (2 * PI), 0.0, K1, K1)
                                sincos_turn(S(4), S(5), S(6), 16, K1, K1)
                                tt("dve", S(7), S(3), S(5), ALU.mult, K1, K1)
                                tsc("dve", S(7), S(7), -1.0, None, ALU.add, None, K1, K1)
                                tt("dve", S(8), S(3), S(4), ALU.mult, K1, K1)
                                tt("dve", S(9), lre, lre, ALU.mult, ["prm"], K1)
                                tt("dve", S(10), lim, lim, ALU.mult, ["prm"], K1)
                                tt("dve", S(9), S(9), S(10), ALU.add, K1, K1)
                                P.op("dve", lambda e: e.reciprocal(S(9), S(9)), K1, K1)
                                tt("dve", S(10), S(7), lre, ALU.mult, K1 + ["prm"], K1)
                                tt("dve", S(11), S(8), lim, ALU.mult, K1 + ["prm"], K1)
                                tt("dve", S(10), S(10), S(11), ALU.add, K1, K1)
                                tt("dve", S(10), S(10), S(9), ALU.mult, K1, K1)
                                tt("dve", S(11), S(8), lre, ALU.mult, K1 + ["prm"], K1)
                                tt("dve", S(12), S(7), lim, ALU.mult, K1 + ["prm"], K1)
                                tt("dve", S(11), S(11), S(12), ALU.subtract, K1, K1)
                                tt("dve", S(11), S(11), S(9), ALU.mult, K1, K1)
                                v3 = lambda t_: t_.rearrange("p (q h) -> p q h", q=16)
                                kreb = S(10).unsqueeze(2).to_broadcast([128, 16, 16]); kimb = S(11).unsqueeze(2).to_broadcast([128, 16, 16])
                                tt("dve", v3(bbr[:]), v3(bbp[:, 0:256]), kreb, ALU.mult, ["bbp"] + K1, ["bbr"])
                                tt("dve", v3(t16[:]), v3(bbp[:, 256:512]), kimb, ALU.mult, ["bbp"] + K1, ["t16"])
                                tt("dve", bbr[:], bbr[:], t16[:], ALU.subtract, ["bbr", "t16"], ["bbr"])
                                tt("dve", v3(bbi[:]), v3(bbp[:, 256:512]), kreb, ALU.mult, ["bbp"] + K1, ["bbi"])
                                tt("dve", v3(t16[:]), v3(bbp[:, 0:256]), kimb, ALU.mult, ["bbp"] + K1, ["t16"])
                                tt("dve", bbi[:], bbi[:], t16[:], ALU.add, ["bbi", "t16"], ["bbi"])
                                act(S(13), S(1), AF.Exp, K1, K1, scale=8.0)
                                fturn(S(14), S(2), 16, 8.0 / (2 * PI), 0.0, K1, K1)
                                kr = c2[:, d * 24:(d + 1) * 24].unsqueeze(1).to_broadcast([128, 16, 24])
                                PK = ["pw"]
                                tt("dve", pw[:, 0], S(2).unsqueeze(2).to_broadcast([128, 16, 24]), kr, ALU.mult, K1 + ["c2"], PK)
                                tt("dve", pw[:, 2], S(1).unsqueeze(2).to_broadcast([128, 16, 24]), kr, ALU.mult, K1 + ["c2"], PK)
                                f2 = lambda i_: pw[:, i_].rearrange("p a b -> p (a b)")
                                act(f2(2), f2(2), AF.Exp, PK, PK)
                                fturn(f2(1), f2(0), 384, 1.0 / (2 * PI), 0.0, PK, PK)
                                sincos_turn(f2(3), f2(4), f2(1), 384, PK, PK)
                                tt("dve", f2(4), f2(4), f2(2), ALU.mult, PK, PK)
                                tt("dve", f2(5), f2(3), f2(2), ALU.mult, PK, PK)
                                PWre = pw[:, 4]; PWim = pw[:, 5]

                                def cprod(ore, oim, row, xr, xi, kx, neg_im=False):
                                    pr = PWre[:, :, row * 8:(row + 1) * 8].unsqueeze(3).to_broadcast([128, 16, 8, 16])
                                    pi_ = PWim[:, :, row * 8:(row + 1) * 8].unsqueeze(3).to_broadcast([128, 16, 8, 16])
                                    xr4 = v3(xr).unsqueeze(2).to_broadcast([128, 16, 8, 16]); xi4 = v3(xi).unsqueeze(2).to_broadcast([128, 16, 8, 16])
                                    tt("dve", ore[:], pr, xr4, ALU.mult, PK + [kx], ["cpo"])
                                    tt("dve", tK[:], pi_, xi4, ALU.mult, PK + [kx], ["tK"])
                                    tt("dve", ore[:], ore[:], tK[:], ALU.subtract, ["cpo", "tK"], ["cpo"])
                                    tt("dve", oim[:], pr, xi4, ALU.mult, PK + [kx], ["cpo"])
                                    tt("dve", tK[:], pi_, xr4, ALU.mult, PK + [kx], ["tK"])
                                    if neg_im:
                                        stt(oim[:], oim[:], -1.0, tK[:], ALU.mult, ALU.subtract, ["cpo", "tK"], ["cpo"])
                                    else:
                                        tt("dve", oim[:], oim[:], tK[:], ALU.add, ["cpo", "tK"], ["cpo"])
                                f3 = lambda t_: t_[:].rearrange("p q a b -> p q (a b)")
                                cprod(KBre, KBim, 0, bbr[:], bbi[:], "bbr")
                                cprod(QCre, QCim, 1, ccp[:, 0:256], ccp[:, 256:512], "ccp", neg_im=True)
                                for (src_, dst_) in ((KBre, WstRe), (KBim, WstIm)):
                                    dv = dst_[:].rearrange("p (q m) c -> p q m c", m=2)
                                    for m in range(2):
                                        for q in range(16):
                                            bk = 4 + (q // 4) % 2
                                            mm(banks[bk][:, (q % 4) * 128:(q % 4 + 1) * 128], f3(src_)[:, q, :], IDm[:, m, :], ["cpo", "IDm"], [BK[bk]])
                                            if q % 4 == 3:
                                                cp("dve", dv[:, q - 3:q + 1, m, :], banks[bk][:, :].rearrange("p (a b) -> p a b", a=4), [BK[bk]], ["Wst"])
                                mgv = Mg[:].rearrange("p (q m) c -> p q m c", m=2)
                                for m in range(2):
                                    o = 1 - m
                                    mset("pool", QZre[o * 64:(o + 1) * 64, :, :], 0.0, ["QZ"]); mset("pool", QZim[o * 64:(o + 1) * 64, :, :], 0.0, ["QZ"])
                                    cp("pool", QZre[m * 64:(m + 1) * 64, :, :], f3(QCre)[m * 64:(m + 1) * 64, :, :], ["cpo", "QZ"], ["QZ"])
                                    cp("pool", QZim[m * 64:(m + 1) * 64, :, :], f3(QCim)[m * 64:(m + 1) * 64, :, :], ["cpo", "QZ"], ["QZ"])
                                    for q in range(16):
                                        bk = 6 + (q // 4) % 2
                                        osl = banks[bk][:, (q % 4) * 128:(q % 4 + 1) * 128]
                                        mm(osl, f3(KBre)[:, q, :], QZre[:, q, :], ["cpo", "QZ"], [BK[bk]], start=True, stop=False)
                                        mm(osl, f3(KBim)[:, q, :], QZim[:, q, :], ["cpo", "QZ"], [BK[bk]], start=False, stop=True)
                                        if q % 4 == 3:
                                            tt("dve", mgv[:, q - 3:q + 1, m, :], banks[bk][:, :].rearrange("p (a b) -> p a b", a=4),
                                               M8[d].unsqueeze(1).to_broadcast([128, 4, 128]), ALU.mult, [BK[bk], "c2"], ["Mg"])
                                cprod(QCre, QCim, 2, ccp[:, 0:256], ccp[:, 256:512], "ccp", neg_im=True)
                                for (src_, dst_, kk) in ((QCre, WoRe, "WoRe"), (QCim, WoIm, "WoIm")):
                                    dv = dst_[:].rearrange("p (q m) c -> p q m c", m=2)
                                    for m in range(2):
                                        cp("dve", dv[m * 64:(m + 1) * 64, :, m, :], f3(src_)[m * 64:(m + 1) * 64, :, :], ["cpo"], [kk])
                                P.barrier()
                            if dbg:
                                dsm = nc.dram_tensor(f"dbg_sm{l}_{d}", [128, 320], F32, kind="ExternalOutput").ap()
                                dma(dsm, sm[:].rearrange("p a b -> p (a b)"), "dbg", ["sm"], [f"dbg_sm{d}"])
                            if SSTOP < 2:
                                return
                            with ExitStack() as PY:
                              Yall = sbuf(PY, "Yall", [128, 32, 544])
                              with ExitStack() as PR:
                                cosB = sbuf(PR, "cosB", [128, 544]); sinB = sbuf(PR, "sinB", [128, 544]); fq = sbuf(PR, "fq", [128, 544])
                                ta = sbuf(PR, "ta", [128, 544]); tb = sbuf(PR, "tb", [128, 544]); tc_ = sbuf(PR, "tc", [128, 544]); td = sbuf(PR, "td", [128, 544])
                                Sre = sbuf(PR, "Sre", [128, 544]); Sim = sbuf(PR, "Sim", [128, 544])
                                Wre = sbuf(PR, "Wre", [128, 544]); Wim = sbuf(PR, "Wim", [128, 544])
                                Hre = sbuf(PR, "Hre", [128, 544], BF16); Him = sbuf(PR, "Him", [128, 544], BF16)
                                car = sbuf(PR, "car", [128, 4])
                                NB = NBv[d]
                                bc = 31 if d == 0 else 0
                                SEG = ((0, 32), (32, 544))
                                PIECES = ((0, 512), (512, 32))
                                cosB2 = [cosB, sbuf(PR, "cosB1", [128, 544])]; sinB2 = [sinB, sbuf(PR, "sinB1", [128, 544])]

                                def s_mm_tab(q):
                                    for (c0, n) in PIECES:
                                        for m in range(2):
                                            mm(big[0][:, c0:c0 + n], WstRe[:, 2 * q + m, :], Uall[:, 2 * q + m, c0:c0 + n], ["Wst", "Uall"], ["b0", "b1"],
                                               start=(m == 0), stop=(m == 1))
                                        for m in range(2):
                                            mm(big[1][:, c0:c0 + n], WstIm[:, 2 * q + m, :], Uall[:, 2 * q + m, c0:c0 + n], ["Wst", "Uall"], ["b2", "b3"],
                                               start=(m == 0), stop=(m == 1))
                                    fturn(fq[:], NB, 544, S(14)[:, q:q + 1], 0.0, ["c2"] + K1, ["fq"])
                                    sincos_turn(sinB2[q % 2][:], cosB2[q % 2][:], fq[:], 544, ["fq"], [f"trig{q % 2}"])
                                s_mm_tab(0)
                                for q in range(16):
                                    cosB = cosB2[q % 2]; sinB = sinB2[q % 2]; ktr = f"trig{q % 2}"
                                    pR = big[0][:, 0:544]; pI = big[1][:, 0:544]
                                    tt("dve", ta[:], pR, cosB[:], ALU.mult, ["b0", "b1", ktr], ["ta"])
                                    tt("dve", tb[:], pI, sinB[:], ALU.mult, ["b2", "b3", ktr], ["tb"])
                                    tt("pool", Sre[:], ta[:], tb[:], ALU.add, ["ta", "tb"], ["Sre"])
                                    tt("dve", tc_[:], pI, cosB[:], ALU.mult, ["b2", "b3", ktr], ["tc"])
                                    tt("dve", td[:], pR, sinB[:], ALU.mult, ["b0", "b1", ktr], ["td"])
                                    tt("pool", Sim[:], tc_[:], td[:], ALU.subtract, ["tc", "td"], ["Sim"])
                                    if q + 1 < 16:
                                        s_mm_tab(q + 1)
                                    r8b = S(13)[:, q:q + 1]

                                    def scan(w_, s_, a0, a1, init, wk, sk, extra, r8b=r8b):
                                        o_ap, d1 = w_[:, a0:a1], s_[:, a0:a1]
                                        if d == 1:
                                            o_ap, d1 = o_ap[:, ::-1], d1[:, ::-1]
                                        P.op("dve", lambda e: e.tensor_tensor_scan(o_ap, r8b.to_broadcast([128, a1 - a0]), d1, init, ALU.mult, ALU.add),
                                             [sk] + K1 + extra, [wk])
                                    scan(Wre, Sre, 0, 32, 0.0, "Wre", "Sre", [])
                                    scan(Wim, Sim, 0, 32, 0.0, "Wim", "Sim", [])
                                    c_ = cosB[:, bc:bc + 1]; s_ = sinB[:, bc:bc + 1]
                                    tt("dve", car[:, 2:3], Wim[:, bc:bc + 1], s_, ALU.mult, ["Wim", ktr], ["car"])
                                    stt(car[:, 0:1], Wre[:, bc:bc + 1], c_, car[:, 2:3], ALU.mult, ALU.subtract, ["Wre", ktr, "car"], ["car"])
                                    tt("dve", car[:, 3:4], Wre[:, bc:bc + 1], s_, ALU.mult, ["Wre", ktr], ["car"])
                                    stt(car[:, 1:2], Wim[:, bc:bc + 1], c_, car[:, 3:4], ALU.mult, ALU.add, ["Wim", ktr, "car"], ["car"])
                                    scan(Wre, Sre, 32, 544, car[:, 0:1], "Wre", "Sre", ["car"])
                                    scan(Wim, Sim, 32, 544, car[:, 1:2], "Wim", "Sim", ["car"])
                                    tt("pool", ta[:], Wre[:], cosB[:], ALU.mult, ["Wre", ktr], ["ta"])
                                    tt("pool", tb[:], Wim[:], sinB[:], ALU.mult, ["Wim", ktr], ["tb"])
                                    tt("pool", tc_[:], Wim[:], cosB[:], ALU.mult, ["Wim", ktr], ["tc"])
                                    tt("pool", td[:], Wre[:], sinB[:], ALU.mult, ["Wre", ktr], ["td"])
                                    for (a0, a1) in SEG:
                                        if d == 0:
                                            so, si = slice(a0 + 1, a1), slice(a0, a1 - 1); ic = a0
                                        else:
                                            so, si = slice(a0, a1 - 1), slice(a0 + 1, a1); ic = a1 - 1
                                        tt("dve", Hre[:, so], ta[:, si], tb[:, si], ALU.subtract, ["ta", "tb"], ["Hre"])
                                        tt("dve", Him[:, so], tc_[:, si], td[:, si], ALU.add, ["tc", "td"], ["Him"])
                                        if a0 == 0:
                                            mset("pool", Hre[:, ic:ic + 1], 0.0, ["Hre"]); mset("pool", Him[:, ic:ic + 1], 0.0, ["Him"])
                                        else:
                                            cp("dve", Hre[:, ic:ic + 1], car[:, 0:1], ["car", "Hre"], ["Hre"])
                                            cp("dve", Him[:, ic:ic + 1], car[:, 1:2], ["car", "Him"], ["Him"])
                                    for m in range(2):
                                        g = 2 * q + m; bY = big[2 + m]; kY = [BK[4 + 2 * m], BK[5 + 2 * m]]
                                        for (c0, n) in PIECES:
                                            mm(bY[:, c0:c0 + n], Mg[:, g, :], Uall[:, g, c0:c0 + n], ["Mg", "Uall"], kY, start=True, stop=False)
                                            mm(bY[:, c0:c0 + n], WoRe[:, g, :], Hre[:, c0:c0 + n], ["WoRe", "Hre"], kY, start=False, stop=False)
                                            mm(bY[:, c0:c0 + n], WoIm[:, g, :], Him[:, c0:c0 + n], ["WoIm", "Him"], kY, start=False, stop=True)
                                        if m == 0:
                                            act(Yall[:, g, :], bY[:, 0:544], AF.Identity, kY, ["Yall"])
                                        else:
                                            cp("dve", Yall[:, g, :], bY[:, 0:544], kY, ["Yall"])
                                P.barrier()
                              if SSTOP < 3:
                                return
                              if True:
                                Ytm = sbuf(PY, "Ytm", [128, 8, 512])
                                for (s0, T) in SUPER:
                                    nb = T // 8; blk0 = s0 // 8
                                    for g in range(32):
                                        bk = (g // 4) % 2
                                        mm(banks[bk][0:nb, (g % 4) * 128:(g % 4 + 1) * 128], Yall[:, g, blk0:blk0 + nb], IDENT, ["Yall", "cst"], [BK[bk]])
                                        if g % 4 == 3:
                                            g0 = g - 3
                                            cp("dve" if (g // 4) % 2 else "act_id", Ytm[0:nb, :, g0 * 16:(g0 + 4) * 16].rearrange("p j (g h) -> p j g h", g=4),
                                               banks[bk][0:nb, :].rearrange("p (g j h) -> p j g h", g=4, j=8), [BK[bk]], ["Ytm"])
                                    dma(y5_d[d][s0:s0 + T, :].rearrange("(b j) c -> b j c", j=8), Ytm[0:nb, :, :], "y5st", ["Ytm"], [f"y5_{d}_{s0}"])
                                P.barrier()
                        P.barrier()
                    if SSTOP < 4:
                        return
                    with ExitStack() as PF:
                        wv = win_d[l].rearrange("(k p) n -> p k n", p=128)
                        wsg = sbuf(PF, "wsg", [128, 8, 512], BF16); wglu = sbuf(PF, "wglu", [128, 4, 1024], BF16)
                        dma(wsg[:], wv[:, :, OSG:OSG + 512], "w2", (), ["wsg"], eng="pool")
                        dma(wglu[:], gluw_d[l].rearrange("(c p) n -> p c n", p=128), "w3", (), ["wglu"], eng="pool")
                        glubb = sbuf(PF, "glubb", [128, 1024]); dskb = sbuf(PF, "dskb5", [128, 512])
                        dma(glubb[:], glub_d[l:l + 1, :].partition_broadcast(128), "pp", (), ["glubb"])
                        dma(dskb[:], s5dr_d[l:l + 1, :].partition_broadcast(128), "pp", (), ["dskb5"])
                        uc = [sbuf(PF, f"suc{i_}", [128, 8, 128], BF16) for i_ in range(2)]
                        yf = [sbuf(PF, f"yf{i_}", [128, 512]) for i_ in range(2)]; yb = [sbuf(PF, f"yb{i_}", [128, 512]) for i_ in range(2)]
                        u5 = [sbuf(PF, f"u5t{i_}", [128, 512]) for i_ in range(2)]
                        ge2 = [sbuf(PF, f"ge{i_}", [128, 512], BF16) for i_ in range(2)]; geT2 = [sbuf(PF, f"geT{i_}", [128, 4, 128], BF16) for i_ in range(2)]
                        pa2 = [sbuf(PF, f"pa{i_}", [128, 512]) for i_ in range(2)]; pg2 = [sbuf(PF, f"pg{i_}", [128, 512]) for i_ in range(2)]
                        ssg2 = [sbuf(PF, f"ssg{i_}", [128, 512]) for i_ in range(2)]; cto2 = [sbuf(PF, f"scto{i_}", [128, 512], BF16) for i_ in range(2)]
                        chunks = list(range(2, NCH)) if last else list(range(NCH))
                        sck = lambda ci: 0 if ci < 2 else 256 + 1024 * ((ci - 2) // 8)

                        def loadf(i_):
                            ci = chunks[i_]; s0 = ci * 128; b_ = i_ % 2
                            load_uc(uc[b_], f"suc{b_}", ci, 0)
                            dma(yf[b_][:], y5_d[0][s0:s0 + 128, :], f"yf{b_}", [f"y5_0_{sck(ci)}"], [f"yf{b_}"])
                            dma(yb[b_][:], y5_d[1][s0:s0 + 128, :], f"yb{b_}", [f"y5_1_{sck(ci)}"], [f"yb{b_}"])
                            dma(u5[b_][:], u5_d[s0:s0 + 128, :], f"u5t{b_}", [f"u5s{sck(ci)}"], [f"u5t{b_}"])
                        def names(i_):
                            b_ = i_ % 2
                            return (b_, ge2[b_], geT2[b_], pa2[b_], pg2[b_], ssg2[b_], cto2[b_],
                                    f"ge{b_}", f"geT{b_}", f"pa{b_}", f"pg{b_}", f"ssg{b_}", f"scto{b_}",
                                    ((0, 2, 3, 4) if b_ == 0 else (1, 5, 6, 7)))

                        def s1(i_):
                            ci = chunks[i_]
                            b_, ge, geT, pa, pg, ssg, cto, kge, kgeT, kpa, kpg, kssg, kcto, (B0, B2, B3, B4) = names(i_)
                            u_ = uc[b_]; uk = f"suc{b_}"
                            tt("pool", yf[b_][:], yf[b_][:], yb[b_][:], ALU.add, [f"yf{b_}", f"yb{b_}"], [f"yf{b_}"])
                            tt("dve", u5[b_][:], u5[b_][:], dskb[:], ALU.mult, [f"u5t{b_}", "dskb5"], [f"u5t{b_}"])
                            tt("dve", yf[b_][:], yf[b_][:], u5[b_][:], ALU.add, [f"yf{b_}", f"u5t{b_}"], [f"yf{b_}"])
                            act(ge[:], yf[b_][:], AF.Gelu, [f"yf{b_}"], [kge])
                            for c4 in range(4):
                                mm(banks[B0][:, c4 * 128:(c4 + 1) * 128], ge[:, c4 * 128:(c4 + 1) * 128], identb[:], [kge, "identb"], [BK[B0]])
                            cp("dve", geT[:].rearrange("p a b -> p (a b)"), banks[B0][:, :], [BK[B0]], [kgeT])
                            for nb_, BB in ((0, B2), (1, B3)):
                                for c4 in range(4):
                                    mm(banks[BB][:, :], geT[:, c4, :], wglu[:, c4, nb_ * 512:(nb_ + 1) * 512], [kgeT, "wglu"], [BK[BB]],
                                       start=(c4 == 0), stop=(c4 == 3))
                            for k in range(8):
                                mm(banks[B4][:, :], u_[:, k, :], wsg[:, k, :], [uk, "wsg"], [BK[B4]], start=(k == 0), stop=(k == 7))

                        def s2(i_):
                            ci = chunks[i_]; s0 = ci * 128
                            b_, ge, geT, pa, pg, ssg, cto, kge, kgeT, kpa, kpg, kssg, kcto, (B0, B2, B3, B4) = names(i_)
                            tt("dve", pa[:], banks[B2][:, :], glubb[:, 0:512], ALU.add, [BK[B2], "glubb"], [kpa])
                            tt("dve", pg[:], banks[B3][:, :], glubb[:, 512:1024], ALU.add, [BK[B3], "glubb"], [kpg])
                            act(pg[:], pg[:], AF.Sigmoid, [kpg], [kpg])
                            act(ssg[:], banks[B4][:, :], AF.Silu, [BK[B4]], [kssg])
                            tt("pool", pa[:], pa[:], pg[:], ALU.mult, [kpa, kpg], [kpa])
                            tt("dve", cto[:], pa[:], ssg[:], ALU.mult, [kpa, kssg], [kcto])
                            dma(cat_d[s0:s0 + 128, 1536:2048], cto[:], f"catst{b_}", [kcto], [f"cat_s5{ci}"])
                        loadf(0)
                        if len(chunks) > 1:
                            loadf(1)
                        s1(0)
                        for i_ in range(len(chunks)):
                            if i_ + 2 < len(chunks):
                                loadf(i_ + 2)
                            if i_ + 1 < len(chunks):
                                s1(i_ + 1)
                            s2(i_)
                        P.barrier()

                if "ssd" in phases:
                    phase_ssd()
                if "gla" in phases:
                    phase_gla()
                if "s5" in phases:
                    phase_s5()
                if "p4" in phases:
                    phase_p4()
        finals = [k for k in P.last_write if k.startswith(f"h{DEPTH}_") or (dbg and not k.startswith("b"))]
        P.wait_all("sp", finals)
        P.emit()
    return nc


def sweep_order(d):
    if d == 0:
        return list(range(NCH))
    return [1, 0] + list(range(NCH - 1, 1, -1))


def _consts():
    k = np.arange(128)[:, None]; j = np.arange(128)[None, :]
    mats = [np.eye(128), (k <= j), (k >= j), (k > j), (k < j), np.ones((128, 128)),
            np.broadcast_to(np.arange(1, 129)[None, :], (128, 128)), np.broadcast_to(np.arange(128, 0, -1)[None, :], (128, 128))]
    return np.concatenate([m.astype(np.float32) for m in mats], axis=1)


def _consts2():
    kr = np.zeros((2, 3, 8), np.float32)
    i = np.arange(8)
    kr[0, 0] = 7 - i; kr[0, 1] = i - 7; kr[0, 2] = i + 1
    kr[1, 0] = i; kr[1, 1] = -i; kr[1, 2] = 8 - i
    nbf = np.concatenate([np.arange(1, 33), np.arange(1, 513)]).astype(np.float32)
    nbb = np.concatenate([np.arange(32, 0, -1), np.arange(512, 0, -1)]).astype(np.float32)
    r = np.arange(128)[:, None] // 16; c = np.arange(128)[None, :] // 16
    m8f = (c >= r).astype(np.float32); m8b = (r >= c).astype(np.float32)
    row = np.concatenate([kr.reshape(-1), nbf, nbb])
    return np.ascontiguousarray(np.concatenate([np.broadcast_to(row[None, :], (128, row.size)), m8f, m8b], axis=1).astype(np.float32))


def _pairlay(a):
    a = np.asarray(a)
    rest = a.shape[2:]
    a = a.reshape(16, 2, 64, *rest)
    a = np.moveaxis(a, 0, 2)
    return np.ascontiguousarray(a.reshape(128, 16, *rest))


def prep_inputs(inp, b):
    f = lambda a: np.ascontiguousarray(np.asarray(a, dtype=np.float32))
    col = lambda v, n: f(np.asarray(v).reshape(n, 128).T)
    m = {}
    m["x"] = f(inp["x"][b]); m["ctx"] = f(inp["ctx"][b])
    m["ccol"] = f(np.concatenate([col(inp["c"][b], 8), col(inp["c_ctx"], 8)], axis=1))
    m["mod_w"] = f(inp["mod_w"]); m["mod_b"] = f(inp["mod_b"])
    m["modb_col"] = f(np.stack([col(inp["mod_b"][l], 24) for l in range(DEPTH)]))
    m["nw_col"] = f(np.stack([col(inp["norm_w"][l], 8) for l in range(DEPTH)]))
    m["w_in"] = f(inp["w_in"]); m["w_out"] = f(inp["w_out"])
    cw = np.asarray(inp["ssd_conv_w"])
    m["convw_col"] = f(np.stack([cw[l].reshape(5, 12, 128).transpose(2, 1, 0).reshape(128, 60) for l in range(DEPTH)]))
    m["convb_col"] = f(np.stack([col(inp["ssd_conv_b"][l], 12) for l in range(DEPTH)]))
    m["a_log"] = f(np.asarray(inp["ssd_a_log"]).reshape(DEPTH, 32)); m["dt_bias"] = f(np.asarray(inp["ssd_dt_bias"]).reshape(DEPTH, 32))
    m["ssd_d"] = f(inp["ssd_d"]); m["ssd_nw"] = f(inp["ssd_norm_w"])
    wl = np.asarray(inp["gla_w_lr"])
    m["wlr"] = f(wl.transpose(0, 2, 1, 3).reshape(DEPTH, 16, 512)); m["blr"] = f(np.asarray(inp["gla_b_lr"]).reshape(DEPTH, 512))
    m["gla_nw"] = f(inp["gla_norm_w"])
    s5p = np.zeros((DEPTH, 2, 128, 48), np.float32)
    s5c = np.zeros((DEPTH, 2, 128, 512), np.float32)
    s5b = np.zeros((DEPTH, 128, 512), np.float32)
    for l in range(DEPTH):
        s5b[l, :, 0:256] = _pairlay(inp["s5_b_re"][l]).reshape(128, 256)
        s5b[l, :, 256:512] = _pairlay(inp["s5_b_im"][l]).reshape(128, 256)
        for d in range(2):
            s5p[l, d, :, 0:16] = _pairlay(inp["s5_lam_re"][l, d])
            s5p[l, d, :, 16:32] = _pairlay(inp["s5_lam_im"][l, d])
            s5p[l, d, :, 32:48] = _pairlay(np.broadcast_to(np.asarray(inp["s5_log_step"][l, d])[:, None], (32, 64)))
            s5c[l, d, :, 0:256] = _pairlay(np.asarray(inp["s5_c_re"][l, d]).transpose(0, 2, 1)).reshape(128, 256)
            s5c[l, d, :, 256:512] = _pairlay(np.asarray(inp["s5_c_im"][l, d]).transpose(0, 2, 1)).reshape(128, 256)
    m["s5p"] = s5p; m["s5b"] = s5b; m["s5c"] = s5c
    m["s5d_col"] = f(np.stack([col(inp["s5_d"][l], 4) for l in range(DEPTH)]))
    m["glu_w"] = f(inp["s5_glu_w"]); m["glu_b"] = f(inp["s5_glu_b"]); m["fnw"] = f(inp["final_norm_w"])
    m["consts"] = _consts(); m["consts2"] = _consts2(); m["s5d_row"] = f(inp["s5_d"])
    return m


def kernel(**inputs):
    nc = build()
    in_maps = [prep_inputs(inputs, b) for b in range(8)]
    res = run_bass_kernel_spmd(nc, in_maps, core_ids=list(range(8)))
    return np.stack([np.asarray(r["out"], dtype=np.float32) for r in res.results], axis=0)
```

```python
import math, os
GSTOP = float(os.environ.get('GSTOP', '99'))
SSTOP = float(os.environ.get('SSTOP', '99'))
import numpy as np
from contextlib import ExitStack
import concourse.bass as bass
import concourse.mybir as mybir
from concourse.bass_utils import run_bass_kernel_spmd

F32 = mybir.dt.float32
BF16 = mybir.dt.bfloat16
AF = mybir.ActivationFunctionType
ALU = mybir.AluOpType

ENGS = ("pe", "act", "dve", "pool", "sp")


class Prog:
    def __init__(self, nc, stack):
        self.nc = nc
        self.stack = stack
        self.lists = {e: [] for e in ENGS}
        self.sems = {}
        self.semval = {}
        self.waited = {e: {} for e in ENGS}
        self.last_write = {}
        self.readers = {}
        for e in ("pe", "act", "dve", "pool"):
            self._sem("E_" + e)

    def _sem(self, name):
        if name not in self.sems:
            self.sems[name] = self.stack.enter_context(self.nc.semaphore(name))
            self.semval[name] = 0
        return self.sems[name]

    def _deps(self, eng, reads, writes):
        toks = []
        for k in reads:
            t = self.last_write.get(k)
            if t is not None:
                toks.append(t)
        for k in writes:
            t = self.last_write.get(k)
            if t is not None:
                toks.append(t)
            toks.extend(self.readers.get(k, ()))
        need = {}
        for (s, v) in toks:
            if s.startswith("D_"):
                v = self.semval[s]
            if eng == "pe" and s == "E_pe":
                continue
            if v > need.get(s, 0):
                need[s] = v
        waits = []
        for s, v in need.items():
            if self.waited[eng].get(s, 0) < v:
                self.waited[eng][s] = v
                waits.append((s, v))
        return waits

    def _commit(self, tok, reads, writes):
        for k in writes:
            self.last_write[k] = tok
            self.readers[k] = []
        for k in reads:
            if k in writes:
                continue
            self.readers.setdefault(k, []).append(tok)

    def op(self, eng, fn, reads=(), writes=()):
        bank_reads = [k for k in reads if len(k) == 2 and k[0] == "b" and k[1].isdigit()]
        if bank_reads:
            writes = list(writes) + [k for k in bank_reads if k not in writes]
        waits = self._deps(eng, reads, writes)
        s = "E_" + eng
        self.semval[s] += 1
        tok = (s, self.semval[s])
        self.lists[eng].append((waits, fn, (s, 1)))
        self._commit(tok, reads, writes)

    def dma(self, fn, stream, reads=(), writes=(), eng="sp"):
        s = "D_" + stream
        self._sem(s)
        waits = self._deps(eng, reads, writes)
        if self.semval[s] > 0 and self.waited[eng].get(s, 0) < self.semval[s]:
            self.waited[eng][s] = self.semval[s]
            waits = [w for w in waits if w[0] != s] + [(s, self.semval[s])]
        self.semval[s] += 16
        tok = (s, self.semval[s])
        self.lists[eng].append((waits, fn, (s, 16)))
        self._commit(tok, reads, writes)

    def barrier(self):
        for eng in ENGS:
            waits = []
            for s, v in self.semval.items():
                if v > 0 and self.waited[eng].get(s, 0) < v and not (eng != "pe" and False):
                    self.waited[eng][s] = v
                    waits.append((s, v))
            self.lists[eng].append((waits, None, None))

    def wait_all(self, eng, keys):
        waits = self._deps(eng, keys, ())
        self.lists[eng].append((waits, None, None))

    def emit(self):
        lists, sems = self.lists, self.sems

        def run(engh, items):
            for waits, fn, inc in items:
                for (s, v) in waits:
                    engh.wait_ge(sems[s], v)
                if fn is not None:
                    fn(engh).then_inc(sems[inc[0]], inc[1])

        with self.nc.Block() as block:
            @block.tensor
            def _(e):
                run(e, lists["pe"])

            @block.scalar
            def _(e):
                run(e, lists["act"])

            @block.vector
            def _(e):
                run(e, lists["dve"])

            @block.gpsimd
            def _(e):
                run(e, lists["pool"])

            @block.sync
            def _(e):
                run(e, lists["sp"])


D = 1024
LQ = 4096
LC = 256
LT = LQ + LC
NCH = LT // 128
DEPTH = 2
EPS = 1e-6
PI = math.pi
OZ, OXBC, ODT, OQ, OK_, OV, OGG, OLR, OU5, OSG = 0, 1024, 2560, 2592, 2848, 3104, 3616, 4128, 4160, 4672


def build(dbg=False, layers=(0, 1), phases=("p0", "ssd", "gla", "s5", "p4")):
    nc = bass.Bass("TRN2", target_bir_lowering=False)
    I = lambda name, shape, dt=F32: nc.dram_tensor(name, shape, dt, kind="ExternalInput").ap()
    skind = "ExternalOutput" if dbg else "Internal"
    S = lambda name, shape, dt=F32: nc.dram_tensor(name, shape, dt, kind=skind).ap()
    x_d = I("x", [LQ, D]); ctx_d = I("ctx", [LC, D])
    ccol_d = I("ccol", [128, 16])
    modw_d = I("mod_w", [DEPTH, D, 3 * D]); modbcol_d = I("modb_col", [DEPTH, 128, 24]); modb_d = I("mod_b", [DEPTH, 3 * D])
    nwcol_d = I("nw_col", [DEPTH, 128, 8])
    win_d = I("w_in", [DEPTH, D, 5184]); wout_d = I("w_out", [DEPTH, 2048, D])
    convw_d = I("convw_col", [DEPTH, 128, 60]); convb_d = I("convb_col", [DEPTH, 128, 12])
    alog_d = I("a_log", [DEPTH, 32]); dtb_d = I("dt_bias", [DEPTH, 32]); ssdd_d = I("ssd_d", [DEPTH, 16])
    ssdnw_d = I("ssd_nw", [DEPTH, 1024])
    wlr_d = I("wlr", [DEPTH, 16, 512]); blr_d = I("blr", [DEPTH, 512]); gnw_d = I("gla_nw", [DEPTH, 128])
    s5p_d = I("s5p", [DEPTH, 2, 128, 48])
    s5b_d = I("s5b", [DEPTH, 128, 512])
    s5c_d = I("s5c", [DEPTH, 2, 128, 512])
    s5d_d = I("s5d_col", [DEPTH, 128, 4]); gluw_d = I("glu_w", [DEPTH, 512, 1024]); glub_d = I("glu_b", [DEPTH, 1024])
    fnw_d = I("fnw", [1024]); s5dr_d = I("s5d_row", [DEPTH, 512]); cst2_d = I("consts2", [128, 1392])
    cst_d = I("consts", [128, 8 * 128])
    out_d = nc.dram_tensor("out", [LQ, D], F32, kind="ExternalOutput").ap()

    uT_d = S("uT", [D, LT], BF16)
    cat_d = (I("cat", [LT, 2048], BF16) if (dbg and not any(p in phases for p in ("ssd", "gla", "s5"))) else S("cat", [LT, 2048], BF16))
    yb_ssd_d = S("yb_ssd", [LT, 1024]); yb_gla_d = S("yb_gla", [LT, 512]); yb_s5_d = S("yb_s5T", [512, LT])
    h1_d = S("h1", [LQ, D]); hc1_d = S("hc1", [LC, D])
    u5_d = S("u5s", [LT, 512]); y5_d = [S("y5f", [LT, 512]), S("y5b", [LT, 512])]
    bc_d = S("bcT", [512, LT], BF16); xb_d = S("xbtm", [LT, 1280], BF16); dt_d = S("dtsp", [LT, 32])
    qk_d = S("qkT", [512, LT], BF16); kv_d = S("kvtm", [LT, 768], BF16); g_d = S("glog", [LT, 512])
    sz_d = S("szc", [LT, 1024]); sgg_d = S("sggc", [LT, 512])

    with ExitStack() as st:
        P = Prog(nc, st)
        uid = [0]

        def sbuf(stack, name, shape, dt=F32):
            uid[0] += 1
            return stack.enter_context(nc.sbuf_tensor(f"{name}_{uid[0]}", shape, dt))

        big = [st.enter_context(nc.psum_tensor(f"pbig{i}", [128, 1024], F32)) for i in range(4)]
        banks = [big[i // 2][:, (i % 2) * 512:(i % 2 + 1) * 512] for i in range(8)]
        BK = [f"b{i}" for i in range(8)]

        def mm(out, lhsT, rhs, rd, wr, start=True, stop=True):
            P.op("pe", lambda e: e.matmul(out, lhsT, rhs, start=start, stop=stop), rd, wr)

        def act(out, in_, func, rd, wr, bias=None, scale=None, accum=None):
            kw = {}
            if bias is not None:
                kw["bias"] = bias
            if scale is not None:
                kw["scale"] = scale
            if accum is not None:
                kw["accum_out"] = accum
            P.op("act", lambda e: e.activation(out, in_, func, **kw), rd, wr)

        def tt(eng, out, a, b, op, rd, wr):
            P.op(eng, lambda e: e.tensor_tensor(out, a, b, op), rd, wr)

        def tsc(eng, out, a, s1, s2, op0, op1, rd, wr):
            if s2 is None:
                P.op(eng, lambda e: e.tensor_scalar(out, a, s1, None, op0), rd, wr)
            else:
                P.op(eng, lambda e: e.tensor_scalar(out, a, s1, s2, op0, op1), rd, wr)

        def stt(out, in0, scalar, in1, op0, op1, rd, wr):
            P.op("dve", lambda e: e.scalar_tensor_tensor(out, in0, scalar, in1, op0, op1), rd, wr)

        def cp(eng, out, in_, rd, wr):
            if eng == "act_id":
                P.op("act", lambda e: e.activation(out, in_, AF.Identity), rd, wr)
            elif eng == "act":
                P.op("act", lambda e: e.copy(out, in_), rd, wr)
            else:
                P.op(eng, lambda e: e.tensor_copy(out, in_), rd, wr)

        def mset(eng, ap, val, wr):
            P.op(eng, lambda e: e.memset(ap, val), (), wr)

        epsc = sbuf(st, "epsc", [128, 1])
        P.op("dve", lambda e: e.memset(epsc[:], EPS), (), ["epsc"])

        def rstd(out, in_, scale, rd, wr):
            act(out, in_, AF.Ln, list(rd) + ["epsc"], wr, bias=epsc[0:out.shape[0], :], scale=scale)
            act(out, out, AF.Exp, wr, wr, scale=-0.5)

        ppc = [0]

        def dma(out, in_, stream, rd, wr, eng="sp"):
            if stream == "pp":
                ppc[0] += 1
                stream = f"pp{ppc[0] % 12}"
            P.dma(lambda e: e.dma_start(out=out, in_=in_), stream, rd, wr, eng=eng)

        cst = sbuf(st, "cst", [128, 8 * 128])
        dma(cst[:], cst_d, "cst", (), ["cst"])
        IDENT = cst[:, 0:128]; UINC = cst[:, 128:256]; LINC = cst[:, 256:384]
        USTR = cst[:, 384:512]; LSTR = cst[:, 512:640]; ONES = cst[:, 640:768]; NVEC = cst[:, 768:896]; NVECR = cst[:, 896:1024]
        identb = sbuf(st, "identb", [128, 128], BF16)
        dma(identb[:], cst_d[:, 0:128], "cstb", (), ["identb"], eng="pool")
        ccol = sbuf(st, "ccol", [128, 16])
        dma(ccol[:], ccol_d, "ccol", (), ["ccol"])
        sc = sbuf(st, "sc", [128, 16])
        act(sc[:], ccol[:], AF.Silu, ["ccol"], ["sc"])

        def hsrc(l, ci):
            if ci < 2:
                src = ctx_d if l == 0 else hc1_d
                return [(src[ci * 128:(ci + 1) * 128, :], 0, 128)]
            lc = ci - 2
            src = x_d if l == 0 else h1_d
            if l % 2 == 0:
                return [(src[lc * 128:(lc + 1) * 128, :], 0, 128)]
            v = src.rearrange("(r c) d -> c r d", c=64)
            return [(v[2 * lc], 0, 64), (v[2 * lc + 1], 64, 128)]

        def hdst(l, ci):
            if ci < 2:
                return [(hc1_d[ci * 128:(ci + 1) * 128, :], 0, 128)]
            lc = ci - 2
            dst = h1_d if l == 0 else out_d
            if l % 2 == 0:
                return [(dst[lc * 128:(lc + 1) * 128, :], 0, 128)]
            v = dst.rearrange("(r c) d -> c r d", c=64)
            return [(v[2 * lc], 0, 64), (v[2 * lc + 1], 64, 128)]

        uT_v = uT_d.rearrange("(k p) t -> p k t", p=128)

        for l in layers:
            last = (l == DEPTH - 1)
            with ExitStack() as LS:
                acol = sbuf(LS, "acol", [128, 16]); shcol = sbuf(LS, "shcol", [128, 16])
                gb = sbuf(LS, "gb", [128, 2, 1024])
                if "p0" in phases or "p4" in phases:
                    with ExitStack() as PS:
                        modT = sbuf(PS, "modT", [128, 24, 2])
                        mw = [sbuf(PS, f"mw{i}", [128, 8, 512]) for i in range(2)]
                        modbcol = sbuf(PS, "modbcol", [128, 24]); nwcol = sbuf(PS, "nwcol", [128, 8])
                        gbias = sbuf(PS, "gbias", [128, 1024])
                        dma(modbcol[:], modbcol_d[l], "pp", (), ["modbcol"])
                        dma(nwcol[:], nwcol_d[l], "pp", (), ["nwcol"])
                        dma(gbias[:], modb_d[l:l + 1, 2048:3072].partition_broadcast(128), "pp", (), ["gbias"])
                        mwv = modw_d[l].rearrange("(k p) f -> p k f", p=128)
                        sc2 = sc[:].rearrange("p (v k) -> p k v", v=2)
                        for blk in range(6):
                            t = mw[blk % 2]; tk = f"mw{blk % 2}"
                            dma(t[:], mwv[:, :, blk * 512:(blk + 1) * 512], tk, (), [tk])
                            for j in range(4 if blk < 4 else 0):
                                fc = blk * 4 + j
                                for k in range(8):
                                    mm(banks[0][:, fc * 2:fc * 2 + 2], t[:, k, j * 128:(j + 1) * 128], sc2[:, k, :],
                                       [tk, "sc"], ["b0"], start=(k == 0), stop=(k == 7))
                            if blk >= 4:
                                for v in range(2):
                                    for k in range(8):
                                        mm(banks[1 + v][:, :], sc[:, v * 8 + k:v * 8 + k + 1].to_broadcast([128, 128]), t[:, k, :],
                                           [tk, "sc"], [BK[1 + v]], start=(k == 0), stop=(k == 7))
                                    tt("dve", gb[:, v, (blk - 4) * 512:(blk - 3) * 512], banks[1 + v][:, :],
                                       gbias[:, (blk - 4) * 512:(blk - 3) * 512], ALU.add, [BK[1 + v], "gbias"], ["gb"])
                        tt("dve", modT[:, 0:16, :], banks[0][:, 0:32].rearrange("p (f v) -> p f v", v=2),
                           modbcol[:, 0:16].unsqueeze(2).to_broadcast([128, 16, 2]), ALU.add, ["b0", "modbcol"], ["modT"])
                        for v in range(2):
                            tsc("dve", acol[:, v * 8:(v + 1) * 8], modT[:, 8:16, v], 1.0, None, ALU.add, None, ["modT"], ["acol"])
                            tt("dve", acol[:, v * 8:(v + 1) * 8], acol[:, v * 8:(v + 1) * 8], nwcol[:], ALU.mult, ["acol", "nwcol"], ["acol"])
                            cp("dve", shcol[:, v * 8:(v + 1) * 8], modT[:, 0:8, v], ["modT"], ["shcol"])
                        if dbg:
                            dm = nc.dram_tensor(f"dbg_mod{l}", [128, 48 + 32], F32, kind="ExternalOutput").ap()
                            dma(dm[:, 0:32], modT[:, 0:16, :].rearrange("p f v -> p (f v)"), "dbg", ["modT"], ["dbg_mod"])
                            dma(dm[:, 48:64], acol[:], "dbg", ["acol"], ["dbg_mod1"])
                            dma(dm[:, 64:80], shcol[:], "dbg", ["shcol"], ["dbg_mod2"])
                            dg = nc.dram_tensor(f"dbg_gb{l}", [128, 2048], F32, kind="ExternalOutput").ap()
                            dma(dg, gb[:].rearrange("p v n -> p (v n)"), "dbg", ["gb"], ["dbg_gb"])
                        if "p0" in phases:
                            ht = [sbuf(PS, f"ht{i}", [128, 1024]) for i in range(3)]
                            hn2 = [sbuf(PS, f"hn{i}", [128, 1024]) for i in range(2)]; sqj2 = [sbuf(PS, f"sqj{i}", [128, 1024]) for i in range(2)]
                            ssqp2 = [sbuf(PS, f"ssq{i}", [128, 2]) for i in range(2)]; uTt = [sbuf(PS, f"uTt{i}", [128, 8, 128], BF16) for i in range(2)]

                            def load_h(ci):
                                t_ = ht[ci % 3]; k_ = f"ht{ci % 3}"
                                for (ap, p0, p1) in hsrc(l, ci):
                                    dma(t_[p0:p1, :], ap, k_, [f"h{l}_{ci}"], [k_])

                            def p0s1(ci):
                                t_ = ht[ci % 3]; k_ = f"ht{ci % 3}"
                                p_ = ci % 2; hn = hn2[p_]; khn = f"hn{p_}"; sq = sqj2[p_]; ksq = f"sqj{p_}"; ssq = ssqp2[p_]; kss = f"ssq{p_}"
                                mset("dve", ssq[:, 0:1], 0.0, [kss])
                                act(sq[:], t_[:], AF.Square, [k_, kss], [ksq, kss], accum=ssq[:, 0:1])
                                rstd(ssq[:, 1:2], ssq[:, 0:1], 1.0 / D, [kss], [kss])
                                tsc("dve", hn[:], t_[:], ssq[:, 1:2], None, ALU.mult, None, [k_, kss], [khn])

                            def p0s2(ci):
                                v = 1 if ci < 2 else 0
                                p_ = ci % 2; hn = hn2[p_]; khn = f"hn{p_}"
                                for k in range(8):
                                    bk = 2 * p_ + k // 4
                                    mm(banks[bk][:, (k % 4) * 128:(k % 4 + 1) * 128], hn[:, k * 128:(k + 1) * 128], IDENT,
                                       [khn, "cst"], [BK[bk]])
                                u_ = uTt[ci % 2]; uk = f"uTt{ci % 2}"
                                for k in range(8):
                                    bk = 2 * p_ + k // 4
                                    src_ = banks[bk][:, (k % 4) * 128:(k % 4 + 1) * 128]
                                    a_ = acol[:, v * 8 + k:v * 8 + k + 1]; b_ = shcol[:, v * 8 + k:v * 8 + k + 1]
                                    if k // 4 == 0:
                                        act(u_[:, k, :], src_, AF.Identity, [BK[bk], "acol", "shcol"], [uk], bias=b_, scale=a_)
                                    else:
                                        tsc("dve", u_[:, k, :], src_, a_, b_, ALU.mult, ALU.add, [BK[bk], "acol", "shcol"], [uk])
                                dma(uT_v[:, :, ci * 128:(ci + 1) * 128], u_[:], "uTst", [uk], [f"uT{ci}"])
                            load_h(0); load_h(1)
                            p0s1(0)
                            for ci in range(NCH):
                                if ci + 2 < NCH:
                                    load_h(ci + 2)
                                if ci + 1 < NCH:
                                    p0s1(ci + 1)
                                p0s2(ci)
                        P.barrier()

                def phase_p4():
                    with ExitStack() as PS:
                        wout = sbuf(PS, "wout", [128, 16, 1024], BF16)
                        dma(wout[:], wout_d[l].rearrange("(j p) n -> p j n", p=128), "wout", (), ["wout"], eng="pool")
                        ct = [sbuf(PS, f"ct{i}", [128, 2048], BF16) for i in range(2)]
                        hr = [sbuf(PS, f"hr{i}", [128, 1024]) for i in range(2)]
                        catT2 = [sbuf(PS, f"catT{i}", [128, 16, 128], BF16) for i in range(2)]
                        hnew2 = [sbuf(PS, f"hnew{i}", [128, 1024]) for i in range(2)]; sq2 = [sbuf(PS, f"sq4{i}", [128, 1024]) for i in range(2)]
                        ssq2 = [sbuf(PS, f"ssq4{i}", [128, 2]) for i in range(2)]
                        fnwb = sbuf(PS, "fnwb", [128, 1024])
                        if last:
                            dma(fnwb[:], fnw_d.unsqueeze(0).partition_broadcast(128), "pp", (), ["fnwb"])
                        chunks = list(range(2, NCH)) if last else list(range(NCH))

                        def load(i):
                            ci = chunks[i]
                            dma(ct[i % 2][:], cat_d[ci * 128:(ci + 1) * 128, :], f"ct{i % 2}",
                                [f"cat_ssd{ci}", f"cat_gla{ci}", f"cat_s5{ci}"], [f"ct{i % 2}"])
                            for (ap, p0, p1) in hsrc(l, ci):
                                dma(hr[i % 2][p0:p1, :], ap, f"hr{i % 2}", [f"h{l}_{ci}"], [f"hr{i % 2}"])
                        load(0)
                        for i, ci in enumerate(chunks):
                            if i + 1 < len(chunks):
                                load(i + 1)
                            p_ = i % 2
                            c_ = ct[p_]; ck = f"ct{p_}"; h_ = hr[p_]; hk = f"hr{p_}"; v = 1 if ci < 2 else 0
                            catT = catT2[p_]; kc = f"catT{p_}"; hnew = hnew2[p_]; kh = f"hnew{p_}"; sq = sq2[p_]; ksq = f"sq4{p_}"; ssq = ssq2[p_]; kss = f"ssq4{p_}"
                            TB = (0, 1) if p_ == 0 else (2, 3); OB = (4, 5) if p_ == 0 else (6, 7)
                            for r in range(2):
                                for jj in range(8):
                                    j = r * 8 + jj; bk = TB[jj // 4]
                                    mm(banks[bk][:, (jj % 4) * 128:(jj % 4 + 1) * 128], c_[:, j * 128:(j + 1) * 128], identb[:], [ck, "identb"], [BK[bk]])
                                for q in range(2):
                                    cp("act" if q % 2 else "dve", catT[:, r * 8 + q * 4:r * 8 + (q + 1) * 4, :].rearrange("p a b -> p (a b)"), banks[TB[q]][:, :],
                                       [BK[TB[q]]], [kc])
                            for nb in range(2):
                                for j in range(16):
                                    mm(banks[OB[nb]][:, :], catT[:, j, :], wout[:, j, nb * 512:(nb + 1) * 512], [kc, "wout"],
                                       [BK[OB[nb]]], start=(j == 0), stop=(j == 15))
                            for nb in range(2):
                                tt("dve", hnew[:, nb * 512:(nb + 1) * 512], banks[OB[nb]][:, :], gb[:, v, nb * 512:(nb + 1) * 512],
                                   ALU.mult, [BK[OB[nb]], "gb"], [kh])
                            tt("pool", hnew[:], hnew[:], h_[:], ALU.add, [kh, hk], [kh])
                            if last:
                                mset("dve", ssq[:, 0:1], 0.0, [kss])
                                act(sq[:], hnew[:], AF.Square, [kh, kss], [ksq, kss], accum=ssq[:, 0:1])
                                rstd(ssq[:, 1:2], ssq[:, 0:1], 1.0 / D, [kss], [kss])
                                stt(hnew[:], hnew[:], ssq[:, 1:2], fnwb[:], ALU.mult, ALU.mult, [kh, kss, "fnwb"], [kh])
                            for (ap, p0, p1) in hdst(l, ci):
                                dma(ap, hnew[p0:p1, :], f"hst{p_}", [kh], [f"h{l + 1}_{ci}"])
                        P.barrier()


                def load_uc(tile, key, ci, halo):
                    s0 = ci * 128
                    lo, hi = (0, LC) if ci < 2 else (LC, LT)
                    a, b = max(s0 - halo, lo), min(s0 + 128 + halo, hi)
                    if halo and (a > s0 - halo):
                        mset("pool", tile[:, :, 0:halo], 0.0, [key])
                    if halo and (b < s0 + 128 + halo):
                        mset("pool", tile[:, :, 128 + halo:128 + 2 * halo], 0.0, [key])
                    c0 = a - (s0 - halo)
                    rd = [f"uT{c}" for c in range(max(ci - 1, 0), min(ci + 2, NCH))]
                    dma(tile[:, :, c0:c0 + (b - a)], uT_v[:, :, a:b], key, rd, [key])

                def phase_ssd():
                    wv = win_d[l].rearrange("(k p) n -> p k n", p=128)
                    bc_v = bc_d.rearrange("(j p) t -> p j t", p=128)
                    with ExitStack() as PA:
                        wxbc = sbuf(PA, "wxbc", [128, 8, 1536], BF16); wdt = sbuf(PA, "wdt", [128, 8, 32], BF16)
                        wz = sbuf(PA, "wz", [128, 8, 1024], BF16); szt = [sbuf(PA, f"szt{i}", [128, 1024]) for i in range(2)]
                        dma(wz[:], wv[:, :, OZ:OZ + 1024], "w1", (), ["wz"], eng="pool")
                        dma(wxbc[:], wv[:, :, OXBC:OXBC + 1536], "w2", (), ["wxbc"], eng="pool")
                        dma(wdt[:], wv[:, :, ODT:ODT + 32], "w3", (), ["wdt"], eng="pool")
                        convw = sbuf(PA, "convw", [128, 60]); convb = sbuf(PA, "convb", [128, 12])
                        dma(convw[:], convw_d[l], "pp", (), ["convw"]); dma(convb[:], convb_d[l], "pp", (), ["convb"])
                        cdiag = sbuf(PA, "cdiag", [128, 60, 128], BF16)
                        for i in range(60):
                            if i % 2:
                                act(cdiag[:, i, :], IDENT, AF.Copy, ["cst", "convw"], ["cdiag"], scale=convw[:, i:i + 1])
                            else:
                                tsc("dve", cdiag[:, i, :], IDENT, convw[:, i:i + 1], None, ALU.mult, None, ["cst", "convw"], ["cdiag"])
                        dtb = sbuf(PA, "dtb", [128, 32])
                        dma(dtb[:], dtb_d[l:l + 1, :].partition_broadcast(128), "pp", (), ["dtb"])
                        uts = [sbuf(PA, f"auts{i}", [128, 8, 260], BF16) for i in range(2)]
                        xpre = sbuf(PA, "xpre", [128, 12, 260], BF16); xT = sbuf(PA, "xT", [128, 12, 256], BF16)
                        xbt = [sbuf(PA, f"xbt{i}", [128, 1280], BF16) for i in range(2)]
                        dts = sbuf(PA, "dts", [128, 2, 32])
                        NSC = LT // 256

                        def load_sc(i):
                            s0 = i * 256; t_ = uts[i % 2]; k_ = f"auts{i % 2}"
                            lo, hi = (0, LC) if i == 0 else (LC, LT)
                            a_, b_ = max(s0 - 2, lo), min(s0 + 258, hi)
                            if a_ > s0 - 2:
                                mset("pool", t_[:, :, 0:2], 0.0, [k_])
                            if b_ < s0 + 258:
                                mset("pool", t_[:, :, 258:260], 0.0, [k_])
                            c0 = a_ - (s0 - 2)
                            rd = [f"uT{c}" for c in range(max(2 * i - 1, 0), min(2 * i + 3, NCH))]
                            dma(t_[:, :, c0:c0 + (b_ - a_)], uT_v[:, :, a_:b_], k_, rd, [k_])
                        load_sc(0)
                        for i in range(NSC):
                            if i + 1 < NSC:
                                load_sc(i + 1)
                            s0 = i * 256; u_ = uts[i % 2]; uk = f"auts{i % 2}"
                            for j in range(12):
                                bk = j % 2
                                for k in range(8):
                                    mm(banks[bk][:, 0:260], wxbc[:, k, j * 128:(j + 1) * 128], u_[:, k, :], ["wxbc", uk], [BK[bk]],
                                       start=(k == 0), stop=(k == 7))
                                cp("act" if j % 2 else "dve", xpre[:, j, :], banks[bk][:, 0:260], [BK[bk]], ["xpre"])
                            for j in range(12):
                                bk = 2 + j % 2
                                for k in range(5):
                                    mm(banks[bk][:, 0:256], cdiag[:, j * 5 + k, :], xpre[:, j, k:k + 256], ["cdiag", "xpre"], [BK[bk]],
                                       start=(k == 0), stop=(k == 4))
                                act(xT[:, j, :], banks[bk][:, 0:256], AF.Silu, [BK[bk], "convb"], ["xT"], bias=convb[:, j:j + 1])
                            dma(bc_v[:, :, s0:s0 + 256], xT[:, 8:12, :], "bcst", ["xT"], [f"bc{i}"])
                            for hf in range(2):
                                xb_ = xbt[hf]; xk = f"xbt{hf}"
                                for j in range(8):
                                    bk = 4 + j // 4
                                    mm(banks[bk][:, (j % 4) * 128:(j % 4 + 1) * 128], xT[:, j, hf * 128:(hf + 1) * 128], identb[:], ["xT", "identb"], [BK[bk]])
                                for g in range(2):
                                    mm(banks[6][:, g * 128:(g + 1) * 128], xT[:, 8 + g, hf * 128:(hf + 1) * 128], identb[:], ["xT", "identb"], ["b6"])
                                cp("act", xb_[:, 0:512], banks[4][:, :], ["b4"], [xk])
                                cp("dve", xb_[:, 512:1024], banks[5][:, :], ["b5"], [xk])
                                cp("dve", xb_[:, 1024:1280], banks[6][:, 0:256], ["b6"], [xk])
                                dma(xb_d[s0 + hf * 128:s0 + (hf + 1) * 128, :], xb_[:], f"xbst{hf}", [xk], [f"xb{2 * i + hf}"])
                                for k in range(8):
                                    mm(banks[7][:, hf * 32:(hf + 1) * 32], u_[:, k, 2 + hf * 128:2 + (hf + 1) * 128], wdt[:, k, :], [uk, "wdt"], ["b7"],
                                       start=(k == 0), stop=(k == 7))
                                for nb in range(2):
                                    for k in range(8):
                                        mm(banks[nb][:, :], u_[:, k, 2 + hf * 128:2 + (hf + 1) * 128], wz[:, k, nb * 512:(nb + 1) * 512], [uk, "wz"], [BK[nb]],
                                           start=(k == 0), stop=(k == 7))
                                    act(szt[hf][:, nb * 512:(nb + 1) * 512], banks[nb][:, :], AF.Silu, [BK[nb]], [f"szt{hf}"])
                                dma(sz_d[s0 + hf * 128:s0 + (hf + 1) * 128, :], szt[hf][:], f"szst{hf}", [f"szt{hf}"], [f"sz{2 * i + hf}"])
                            tt("dve", dts[:], banks[7][:, 0:64].rearrange("p (a b) -> p a b", a=2), dtb[:].unsqueeze(1).to_broadcast([128, 2, 32]), ALU.add,
                               ["b7", "dtb"], ["dts"])
                            act(dts[:], dts[:], AF.Exp, ["dts"], ["dts"])
                            act(dts[:], dts[:], AF.Ln, ["dts"], ["dts"], bias=1.0)
                            dma(dt_d[s0:s0 + 256, :].rearrange("(a p) c -> p a c", p=128), dts[:], "dtst", ["dts"], [f"dt{2 * i}", f"dt{2 * i + 1}"])
                        P.barrier()
                    with ExitStack() as PS:
                        aneg = sbuf(PS, "aneg", [128, 32]); dskb = sbuf(PS, "dskb", [128, 16]); nwb = sbuf(PS, "nwb", [128, 1024])
                        dma(aneg[:], alog_d[l:l + 1, :].partition_broadcast(128), "pp", (), ["aneg"])
                        dma(dskb[:], ssdd_d[l:l + 1, :].partition_broadcast(128), "pp", (), ["dskb"])
                        dma(nwb[:], ssdnw_d[l:l + 1, :].partition_broadcast(128), "pp", (), ["nwb"])
                        act(aneg[:], aneg[:], AF.Exp, ["aneg"], ["aneg"])
                        tsc("dve", aneg[:], aneg[:], -1.0, None, ALU.mult, None, ["aneg"], ["aneg"])
                        uc = [sbuf(PS, f"uc{i}", [128, 1024]) for i in range(3)]
                        bct = [sbuf(PS, f"bct{i}", [128, 4, 128], BF16) for i in range(3)]
                        xbl = [sbuf(PS, f"xbl{i}", [128, 1280], BF16) for i in range(3)]
                        dtl = [sbuf(PS, f"dtl{i}", [128, 32]) for i in range(3)]
                        xw2 = [sbuf(PS, f"xw{i}", [128, 1024], BF16) for i in range(2)]
                        xdt = sbuf(PS, "xdt", [128, 1024], BF16)
                        da = sbuf(PS, "da", [128, 16]); ex2 = [sbuf(PS, f"ex{i}", [128, 48]) for i in range(2)]; wsc = sbuf(PS, "wsc", [128, 16])
                        cbm = sbuf(PS, "cbm", [128, 256])
                        lh = [sbuf(PS, f"lh{i}", [128, 128]) for i in range(8)]
                        E = [sbuf(PS, f"E{i}", [128, 512]) for i in range(2)]
                        ST = sbuf(PS, "ST", [128, 16, 128], BF16)
                        H = sbuf(PS, "H", [128, 1024]); Hb = sbuf(PS, "Hb", [128, 1024], BF16)
                        ydir = sbuf(PS, "ydir", [128, 1024]); ybl = sbuf(PS, "ybl", [128, 1024]); sz = sbuf(PS, "sz", [128, 1024])
                        tmp = sbuf(PS, "tmp", [128, 1024]); ss2 = sbuf(PS, "ss2", [128, 4]); cto = sbuf(PS, "cto", [128, 1024], BF16)
                        for d in (1, 0):
                            INC, STR = (UINC, USTR) if d == 0 else (LINC, LSTR)
                            order = sweep_order(d)
                            mset("dve", H[:], 0.0, ["H"]); mset("pool", Hb[:], 0.0, ["Hb"])

                            def loads(i):
                                ci = order[i]; s0 = ci * 128; b_ = i % 3
                                fin_ = (d == 0) and not (last and ci < 2)
                                dma(bct[b_][:], bc_v[:, :, s0:s0 + 128], f"bct{b_}", [f"bc{ci // 2}"], [f"bct{b_}"])
                                dma(xbl[b_][:], xb_d[s0:s0 + 128, :], f"xbl{b_}", [f"xb{ci}"], [f"xbl{b_}"])
                                dma(dtl[b_][:], dt_d[s0:s0 + 128, :], f"dtl{b_}", [f"dt{ci}"], [f"dtl{b_}"])
                                if fin_:
                                    dma(uc[b_][:], sz_d[s0:s0 + 128, :], f"uc{b_}", [f"sz{ci}"], [f"uc{b_}"])

                            def nm(i):
                                b_ = i % 3; p_ = i % 2
                                return dict(bc_=bct[b_], bck=f"bct{b_}", xb_=xbl[b_], xk=f"xbl{b_}", dt=dtl[b_], dk=f"dtl{b_}", u_=uc[b_], uk=f"uc{b_}",
                                            ex=ex2[p_], kex=f"ex{p_}", xw=xw2[p_], kxw=f"xw{p_}", YB=((2, 3) if p_ == 0 else (4, 5)))

                            def ss1(i):
                                n = nm(i); bc_, bck, xb_, xk, dt, dk, ex, kex, xw, kxw, YB = (n[k_] for k_ in ("bc_", "bck", "xb_", "xk", "dt", "dk", "ex", "kex", "xw", "kxw", "YB"))
                                xtm = xb_[:, 0:1024]
                                tt("pool", xdt[:].rearrange("p (h q) -> p h q", h=16), xtm.rearrange("p (h q) -> p h q", h=16),
                                   dt[:, d * 16:(d + 1) * 16].unsqueeze(2).to_broadcast([128, 16, 64]), ALU.mult, [xk, dk], ["xdt"])
                                tt("dve", da[:], dt[:, d * 16:(d + 1) * 16], aneg[:, d * 16:(d + 1) * 16], ALU.mult, [dk, "aneg"], ["da"])
                                mm(banks[7][:, 32:48], INC, da[:], ["cst", "da"], ["b7"])
                                mm(banks[7][:, 48:64], STR, da[:], ["cst", "da"], ["b7"])
                                mm(banks[7][:, 64:80], ONES, da[:], ["cst", "da"], ["b7"])
                                act(ex[:], banks[7][:, 32:80], AF.Exp, ["b7"], [kex])
                                tt("dve", wsc[:], ex[:, 16:32], dt[:, d * 16:(d + 1) * 16], ALU.mult, [kex, dk], ["wsc"])
                                tt("pool", xw[:].rearrange("p (h q) -> p h q", h=16), xtm.rearrange("p (h q) -> p h q", h=16),
                                   wsc[:].unsqueeze(2).to_broadcast([128, 16, 64]), ALU.mult, [xk, "wsc"], [kxw])
                                for g in range(2):
                                    mm(banks[6][:, 256 + g * 128:256 + (g + 1) * 128], bc_[:, g, :], bc_[:, 2 + g, :], [bck], ["b6"])
                                tt("dve", cbm[:].rearrange("p (g q) -> p g q", g=2), banks[6][:, 256:512].rearrange("p (g q) -> p g q", g=2),
                                   INC.unsqueeze(1).to_broadcast([128, 2, 128]), ALU.mult, ["b6", "cst"], ["cbm"])

                                def grp_lh(gq):
                                    for h in range(gq * 4, gq * 4 + 4):
                                        li = h % 8; bk = gq % 2
                                        act(lh[li][:], STR, AF.Copy, ["cst", "da"], [f"lh{li}"], scale=da[:, h:h + 1])
                                        mm(banks[bk][:, (h % 4) * 128:(h % 4 + 1) * 128], lh[li][:], INC, [f"lh{li}", "cst"], [BK[bk]])

                                def grp_E(gq):
                                    bk = gq % 2; g = gq // 2
                                    act(E[bk][:], banks[bk][:, :], AF.Exp, [BK[bk]], [f"E{bk}"])
                                    tt("dve", ST[:, gq * 4:(gq + 1) * 4, :], E[bk][:].rearrange("p (h q) -> p h q", h=4),
                                       cbm[:, g * 128:(g + 1) * 128].unsqueeze(1).to_broadcast([128, 4, 128]), ALU.mult, [f"E{bk}", "cbm"], ["ST"])
                                grp_lh(0); grp_lh(1); grp_E(0); grp_lh(2); grp_E(1); grp_lh(3); grp_E(2); grp_E(3)

                            def ss1b(i):
                                YB = nm(i)["YB"]
                                for h in range(16):
                                    bk = YB[h // 8]
                                    mm(banks[bk][:, (h % 8) * 64:(h % 8 + 1) * 64], ST[:, h, :], xdt[:, h * 64:(h + 1) * 64], ["ST", "xdt"], [BK[bk]])

                            def ss2f(i):
                                ci = order[i]; s0 = ci * 128
                                n = nm(i); bc_, bck, xb_, xk, u_, uk, ex, kex, xw, kxw, YB = (n[k_] for k_ in ("bc_", "bck", "xb_", "xk", "u_", "uk", "ex", "kex", "xw", "kxw", "YB"))
                                xtm = xb_[:, 0:1024]; btm = xb_[:, 1024:1280]
                                fin = (d == 0) and not (last and ci < 2)
                                for g in range(2):
                                    mm(banks[6 + g][:, :], bc_[:, 2 + g, :], Hb[:, g * 512:(g + 1) * 512], [bck, "Hb"], [BK[6 + g]])
                                for g in range(2):
                                    tt("dve", ydir[:, g * 512:(g + 1) * 512].rearrange("p (h q) -> p h q", h=8),
                                       banks[6 + g][:, :].rearrange("p (h q) -> p h q", h=8),
                                       ex[:, g * 8:(g + 1) * 8].unsqueeze(2).to_broadcast([128, 8, 64]), ALU.mult, [BK[6 + g], kex], ["ydir"])
                                    tt("dve", ydir[:, g * 512:(g + 1) * 512], ydir[:, g * 512:(g + 1) * 512], banks[YB[g]][:, :], ALU.add,
                                       ["ydir", BK[YB[g]]], ["ydir"])
                                for g in range(2):
                                    mm(banks[6 + g][:, :], btm[:, g * 128:(g + 1) * 128], xw[:, g * 512:(g + 1) * 512], [xk, kxw], [BK[6 + g]])
                                for g in range(2):
                                    tt("dve", H[:, g * 512:(g + 1) * 512], H[:, g * 512:(g + 1) * 512], banks[6 + g][:, :], ALU.add,
                                       ["H", BK[6 + g]], ["H"])
                                cp("act", Hb[:], H[:], ["H"], ["Hb"])
                                if d == 1:
                                    dma(yb_ssd_d[s0:s0 + 128, :], ydir[:], "ybst", ["ydir"], [f"ybs{ci}"])
                                elif fin:
                                    dma(ybl[:], yb_ssd_d[s0:s0 + 128, :], "ybld", [f"ybs{ci}"], ["ybl"])
                                    tt("dve", ydir[:], ydir[:], ybl[:], ALU.add, ["ydir", "ybl"], ["ydir"])
                                    tt("pool", tmp[:].rearrange("p (h q) -> p h q", h=16), xtm.rearrange("p (h q) -> p h q", h=16),
                                       dskb[:].unsqueeze(2).to_broadcast([128, 16, 64]), ALU.mult, [xk, "dskb"], ["tmp"])
                                    tt("dve", ydir[:], ydir[:], tmp[:], ALU.add, ["ydir", "tmp"], ["ydir"])
                                    tt("dve", ydir[:], ydir[:], u_[:], ALU.mult, ["ydir", uk], ["ydir"])
                                    mset("dve", ss2[:], 0.0, ["ss2"])
                                    for g in range(2):
                                        act(tmp[:, g * 512:(g + 1) * 512], ydir[:, g * 512:(g + 1) * 512], AF.Square, ["ydir", "ss2"], ["tmp", "ss2"],
                                            accum=ss2[:, g:g + 1])
                                    rstd(ss2[:, 2:4], ss2[:, 0:2], 1.0 / 512, ["ss2"], ["ss2"])
                                    for g in range(2):
                                        stt(cto[:, g * 512:(g + 1) * 512], ydir[:, g * 512:(g + 1) * 512], ss2[:, 2 + g:3 + g],
                                            nwb[:, g * 512:(g + 1) * 512], ALU.mult, ALU.mult, ["ydir", "ss2", "nwb"], ["cto"])
                                    dma(cat_d[s0:s0 + 128, 0:1024], cto[:], "catst", ["cto"], [f"cat_ssd{ci}"])
                            def sdec(i):
                                ex = ex2[i % 2]; kex = f"ex{i % 2}"
                                tt("pool", H[:].rearrange("p (h q) -> p h q", h=16), H[:].rearrange("p (h q) -> p h q", h=16),
                                   ex[:, 32:48].unsqueeze(2).to_broadcast([128, 16, 64]), ALU.mult, ["H", kex], ["H"])
                            n_ = len(order)
                            loads(0); loads(1)
                            ss1(0); ss1b(0)
                            for i in range(n_):
                                if i + 2 < n_:
                                    loads(i + 2)
                                if i > 0:
                                    sdec(i)
                                if i + 1 < n_:
                                    ss1(i + 1)
                                ss2f(i)
                                if i + 1 < n_:
                                    ss1b(i + 1)
                        P.barrier()

                def phase_gla():
                    wv = win_d[l].rearrange("(k p) n -> p k n", p=128)
                    qk_v = qk_d.rearrange("(j p) t -> p j t", p=128)
                    with ExitStack() as PA:
                        wq = sbuf(PA, "wq", [128, 8, 256], BF16); wk = sbuf(PA, "wk", [128, 8, 256], BF16)
                        wvv = sbuf(PA, "wvv", [128, 8, 512], BF16); wlr = sbuf(PA, "wlr", [128, 8, 32], BF16)
                        wgg = sbuf(PA, "wgg", [128, 8, 512], BF16); sgt2 = [sbuf(PA, f"sgt{i}", [128, 512]) for i in range(2)]
                        dma(wgg[:], wv[:, :, OGG:OGG + 512], "w3", (), ["wgg"], eng="pool")
                        dma(wq[:], wv[:, :, OQ:OQ + 256], "w1", (), ["wq"], eng="pool")
                        dma(wk[:], wv[:, :, OK_:OK_ + 256], "w2", (), ["wk"], eng="pool")
                        dma(wvv[:], wv[:, :, OV:OV + 512], "w3", (), ["wvv"], eng="pool")
                        dma(wlr[:], wv[:, :, OLR:OLR + 32], "w2", (), ["wlr"], eng="pool")
                        wlrp = sbuf(PA, "wlrp", [16, 512]); blrb = sbuf(PA, "blrb", [128, 512])
                        dma(wlrp[:], wlr_d[l], "pp", (), ["wlrp"])
                        dma(blrb[:], blr_d[l:l + 1, :].partition_broadcast(128), "pp", (), ["blrb"])
                        uts = [sbuf(PA, f"guts{i}", [128, 8, 256], BF16) for i in range(2)]
                        qkT = [sbuf(PA, f"qkT{i}", [128, 4, 256], BF16) for i in range(2)]
                        kvt = [sbuf(PA, f"kvt{i}", [128, 768], BF16) for i in range(2)]
                        lrT = sbuf(PA, "lrT", [16, 4, 128]); gsb = [sbuf(PA, f"gsb{i}", [128, 512]) for i in range(2)]
                        NSC = LT // 256

                        def load_sc(i):
                            s0 = i * 256
                            rd = [f"uT{2 * i}", f"uT{2 * i + 1}"]
                            dma(uts[i % 2][:], uT_v[:, :, s0:s0 + 256], f"guts{i % 2}", rd, [f"guts{i % 2}"])
                        load_sc(0)
                        for i in range(NSC):
                            if i + 1 < NSC:
                                load_sc(i + 1)
                            s0 = i * 256; u_ = uts[i % 2]; uk = f"guts{i % 2}"; qk_ = qkT[i % 2]; qkk = f"qkT{i % 2}"
                            for j in range(4):
                                w_ = wq if j < 2 else wk; c2 = j % 2; bk = j % 2
                                for k in range(8):
                                    mm(banks[bk][:, 0:256], w_[:, k, c2 * 128:(c2 + 1) * 128], u_[:, k, :], ["wq", "wk", uk], [BK[bk]], start=(k == 0), stop=(k == 7))
                                cp("act" if j % 2 else "dve", qk_[:, j, :], banks[bk][:, 0:256], [BK[bk]], [qkk])
                            dma(qk_v[:, :, s0:s0 + 256], qk_[:], f"qkst{i % 2}", [qkk], [f"qk{i}"])
                            for hf in range(2):
                                kv_ = kvt[hf]; kvk = f"kvt{hf}"; gs_ = gsb[hf]; gk = f"gsb{hf}"
                                usl = slice(hf * 128, (hf + 1) * 128)
                                for k in range(8):
                                    mm(banks[2][:, 0:256], u_[:, k, usl], wk[:, k, :], [uk, "wk"], ["b2"], start=(k == 0), stop=(k == 7))
                                for k in range(8):
                                    mm(banks[3][:, :], u_[:, k, usl], wvv[:, k, :], [uk, "wvv"], ["b3"], start=(k == 0), stop=(k == 7))
                                cp("dve", kv_[:, 0:256], banks[2][:, 0:256], ["b2"], [kvk])
                                cp("act", kv_[:, 256:768], banks[3][:, :], ["b3"], [kvk])
                                for k in range(8):
                                    mm(banks[7][:, :], u_[:, k, usl], wgg[:, k, :], [uk, "wgg"], ["b7"], start=(k == 0), stop=(k == 7))
                                act(sgt2[hf][:], banks[7][:, :], AF.Silu, ["b7"], [f"sgt{hf}"])
                                dma(sgg_d[s0 + hf * 128:s0 + (hf + 1) * 128, :], sgt2[hf][:], f"sgst{hf}", [f"sgt{hf}"], [f"sgg{2 * i + hf}"])
                                dma(kv_d[s0 + hf * 128:s0 + (hf + 1) * 128, :], kv_[:], f"kvst{hf}", [kvk], [f"kv{2 * i + hf}"])
                                for dd in range(2):
                                    for k in range(8):
                                        mm(banks[4][0:16, (hf * 2 + dd) * 128:(hf * 2 + dd + 1) * 128], wlr[:, k, dd * 16:(dd + 1) * 16], u_[:, k, usl], ["wlr", uk], ["b4"],
                                           start=(k == 0), stop=(k == 7))
                            cp("dve", lrT[:].rearrange("p a b -> p (a b)"), banks[4][0:16, :], ["b4"], ["lrT"])
                            for hf in range(2):
                                gs_ = gsb[hf]; gk = f"gsb{hf}"
                                for dd in range(2):
                                    mm(banks[5 + hf][:, dd * 256:(dd + 1) * 256], lrT[:, hf * 2 + dd, :], wlrp[:, dd * 256:(dd + 1) * 256], ["lrT", "wlrp"], [BK[5 + hf]])
                                tt("dve", gs_[:], banks[5 + hf][:, :], blrb[:], ALU.add, [BK[5 + hf], "blrb"], [gk])
                                act(gs_[:], gs_[:], AF.Exp, [gk], [gk], scale=-1.0)
                                act(gs_[:], gs_[:], AF.Ln, [gk], [gk], bias=1.0)
                                tsc("dve", gs_[:], gs_[:], -1.0 / 16.0, None, ALU.mult, None, [gk], [gk])
                                dma(g_d[s0 + hf * 128:s0 + (hf + 1) * 128, :], gs_[:], f"gst{hf}", [gk], [f"gg{2 * i + hf}"])
                        P.barrier()
                    with ExitStack() as PS:
                        gnwb = sbuf(PS, "gnwb", [128, 128])
                        dma(gnwb[:], gnw_d[l:l + 1, :].partition_broadcast(128), "pp", (), ["gnwb"])
                        uc = [sbuf(PS, f"guc{i}", [128, 512]) for i in range(3)]
                        qkl = [sbuf(PS, f"qkl{i}", [128, 4, 128], BF16) for i in range(3)]
                        kvl = [sbuf(PS, f"kvl{i}", [128, 768], BF16) for i in range(3)]
                        gl = [sbuf(PS, f"gl{i}", [128, 256]) for i in range(3)]
                        eb2 = [sbuf(PS, f"eb{i}", [128, 256]) for i in range(2)]; enb = sbuf(PS, "enb", [128, 256]); er = sbuf(PS, "er", [128, 256])
                        kd2 = [sbuf(PS, f"kd{i}", [128, 256], BF16) for i in range(2)]
                        STg2 = [sbuf(PS, f"STg{i}", [128, 4, 128], BF16) for i in range(2)]
                        Sg = sbuf(PS, "Sg", [128, 2, 128]); Sgb = sbuf(PS, "Sgb", [128, 2, 128], BF16)
                        od = sbuf(PS, "od", [128, 512]); obl = sbuf(PS, "obl", [128, 512]); sgt = sbuf(PS, "sgt", [128, 512])
                        qeTz2 = [sbuf(PS, f"qeTz{i}", [128, 4, 128], BF16) for i in range(2)]; keTz = sbuf(PS, "keTz", [128, 4, 128], BF16)
                        for i in range(2):
                            mset("pool", qeTz2[i][:], 0.0, [f"qeTz{i}"])
                        mset("pool", keTz[:], 0.0, ["keTz"])
                        er2 = sbuf(PS, "er2", [128, 256]); lnq = sbuf(PS, "lnq", [128, 1])
                        mset("dve", lnq[:], math.log(0.125), ["lnq"])
                        tmp = sbuf(PS, "gtmp", [128, 512]); ss4 = sbuf(PS, "ss4", [128, 8]); cto = sbuf(PS, "gcto", [128, 512], BF16)
                        for d in (1, 0):
                            INC, STR = (UINC, USTR) if d == 0 else (LINC, LSTR)
                            lastcol = 127 if d == 0 else 0
                            order = sweep_order(d)
                            mset("dve", Sg[:], 0.0, ["Sg"]); mset("pool", Sgb[:], 0.0, ["Sgb"])

                            def gloads(i):
                                ci = order[i]; s0 = ci * 128; b_ = i % 3
                                fin_ = (d == 0) and not (last and ci < 2)
                                dma(qkl[b_][:], qk_v[:, :, s0:s0 + 128], f"qkl{b_}", [f"qk{ci // 2}"], [f"qkl{b_}"])
                                dma(kvl[b_][:], kv_d[s0:s0 + 128, :], f"kvl{b_}", [f"kv{ci}"], [f"kvl{b_}"])
                                dma(gl[b_][:], g_d[s0:s0 + 128, d * 256:(d + 1) * 256], f"gl{b_}", [f"gg{ci}"], [f"gl{b_}"])
                                if fin_:
                                    dma(uc[b_][:], sgg_d[s0:s0 + 128, :], f"guc{b_}", [f"sgg{ci}"], [f"guc{b_}"])

                            def gs1(i):
                                p_ = i % 2; b_ = i % 3
                                qk_ = qkl[b_]; qkk = f"qkl{b_}"; kv_ = kvl[b_]; kvk = f"kvl{b_}"; g_ = gl[b_]; gk = f"gl{b_}"
                                eb = eb2[p_]; keb = f"eb{p_}"; kd = kd2[p_]; kkd = f"kd{p_}"
                                STg = STg2[p_]; kst = f"STg{p_}"; qeTz = qeTz2[p_]; kq = f"qeTz{p_}"
                                for c2 in range(2):
                                    mm(banks[4][:, c2 * 128:(c2 + 1) * 128], g_[:, c2 * 128:(c2 + 1) * 128], INC, [gk, "cst"], ["b4"])
                                mm(banks[4][:, 256:512], STR, g_[:], ["cst", gk], ["b4"])
                                act(eb[:], banks[4][:, 0:256], AF.Exp, ["b4"], [keb])
                                act(enb[:], banks[4][:, 0:256], AF.Exp, ["b4"], ["enb"], scale=-1.0)
                                act(er[:], banks[4][:, 256:512], AF.Exp, ["b4"], ["er"])
                                act(er2[:], banks[4][:, 0:256], AF.Exp, ["b4", "lnq"], ["er2"], bias=lnq[:, 0:1])
                                for h in range(4):
                                    c2 = h // 2; hb = (h % 2) * 64
                                    tt("dve", qeTz[hb:hb + 64, h, :], qk_[hb:hb + 64, c2, 0:128], er2[hb:hb + 64, c2 * 128:(c2 + 1) * 128],
                                       ALU.mult, [qkk, "er2"], [kq])
                                    tt("dve", keTz[hb:hb + 64, h, :], qk_[hb:hb + 64, 2 + c2, 0:128], enb[hb:hb + 64, c2 * 128:(c2 + 1) * 128],
                                       ALU.mult, [qkk, "enb"], ["keTz"])
                                tt("dve", kd[:], kv_[:, 0:256], er[:], ALU.mult, [kvk, "er"], [kkd])
                                for h in range(4):
                                    mm(banks[5][:, h * 128:(h + 1) * 128], keTz[:, h, :], qeTz[:, h, :], ["keTz", kq], ["b5"])

                            def gs1b(i):
                                p_ = i % 2
                                STg = STg2[p_]; kst = f"STg{p_}"
                                tt("dve", STg[:], banks[5][:, :].rearrange("p (h q) -> p h q", h=4), INC.unsqueeze(1).to_broadcast([128, 4, 128]), ALU.mult,
                                   ["b5", "cst"], [kst])

                            def gs2(i):
                                ci = order[i]; p_ = i % 2; s0 = ci * 128; b_ = i % 3
                                u_ = uc[b_]; uk = f"guc{b_}"; kv_ = kvl[b_]; kvk = f"kvl{b_}"
                                vtm = kv_[:, 256:768]
                                eb = eb2[p_]; keb = f"eb{p_}"; kd = kd2[p_]; kkd = f"kd{p_}"
                                STg = STg2[p_]; kst = f"STg{p_}"; qeTz = qeTz2[p_]; kq = f"qeTz{p_}"
                                fin = (d == 0) and not (last and ci < 2)
                                for h in range(4):
                                    c2 = h // 2
                                    mm(banks[6][:, h * 128:(h + 1) * 128], STg[:, h, :], vtm[:, h * 128:(h + 1) * 128], [kst, kvk], ["b6"], start=True, stop=False)
                                    mm(banks[6][:, h * 128:(h + 1) * 128], qeTz[:, h, :], Sgb[:, c2, :], [kq, "Sgb"], ["b6"], start=False, stop=True)
                                if d == 1:
                                    cp("dve", od[:], banks[6][:, :], ["b6"], ["od"])
                                    dma(yb_gla_d[s0:s0 + 128, :], od[:], "ybst", ["od"], [f"ybg{ci}"])
                                elif fin:
                                    dma(obl[:], yb_gla_d[s0:s0 + 128, :], "ybld", [f"ybg{ci}"], ["obl"])
                                    tt("dve", od[:], banks[6][:, :], obl[:], ALU.add, ["b6", "obl"], ["od"])
                                    mset("dve", ss4[:], 0.0, ["ss4"])
                                    for h in range(4):
                                        act(tmp[:, h * 128:(h + 1) * 128], od[:, h * 128:(h + 1) * 128], AF.Square, ["od", "ss4"], ["gtmp", "ss4"],
                                            accum=ss4[:, h:h + 1])
                                    rstd(ss4[:, 4:8], ss4[:, 0:4], 1.0 / 128, ["ss4"], ["ss4"])
                                    tt("dve", od[:].rearrange("p (h q) -> p h q", h=4), od[:].rearrange("p (h q) -> p h q", h=4),
                                       ss4[:, 4:8].unsqueeze(2).to_broadcast([128, 4, 128]), ALU.mult, ["od", "ss4"], ["od"])
                                    tt("dve", od[:].rearrange("p (h q) -> p h q", h=4), od[:].rearrange("p (h q) -> p h q", h=4),
                                       gnwb[:].unsqueeze(1).to_broadcast([128, 4, 128]), ALU.mult, ["od", "gnwb"], ["od"])
                                    tt("dve", cto[:], od[:], u_[:], ALU.mult, ["od", uk], ["gcto"])
                                    dma(cat_d[s0:s0 + 128, 1024:1536], cto[:], "catst", ["gcto"], [f"cat_gla{ci}"])
                                for c2 in range(2):
                                    for var in range(2):
                                        mm(banks[7][:, (c2 * 2 + var) * 128:(c2 * 2 + var + 1) * 128], kd[:, c2 * 128:(c2 + 1) * 128],
                                           vtm[:, (2 * c2 + var) * 128:(2 * c2 + var + 1) * 128], [kkd, kvk], ["b7"])
                                for c2 in range(2):
                                    for var in range(2):
                                        hb = var * 64
                                        stt(Sg[hb:hb + 64, c2, :], Sg[hb:hb + 64, c2, :], eb[hb:hb + 64, c2 * 128 + lastcol:c2 * 128 + lastcol + 1],
                                            banks[7][hb:hb + 64, (c2 * 2 + var) * 128:(c2 * 2 + var + 1) * 128], ALU.mult, ALU.add, ["Sg", keb, "b7"], ["Sg"])
                                cp("act", Sgb[:], Sg[:], ["Sg"], ["Sgb"])
                            n_ = len(order)
                            gloads(0); gloads(1)
                            gs1(0); gs1b(0)
                            for i in range(n_):
                                if i + 2 < n_:
                                    gloads(i + 2)
                                if i + 1 < n_:
                                    gs1(i + 1)
                                gs2(i)
                                if i + 1 < n_:
                                    gs1b(i + 1)
                        P.barrier()

                def phase_s5():
                    with ExitStack() as PS:
                        wv = win_d[l].rearrange("(k p) n -> p k n", p=128)
                        Uall = sbuf(PS, "Uall", [128, 32, 544], BF16)
                        c2 = sbuf(PS, "c2", [128, 1392])
                        dma(c2[:], cst2_d, "cst2", (), ["c2"])
                        NBv = (c2[:, 48:592], c2[:, 592:1136]); M8 = (c2[:, 1136:1264], c2[:, 1264:1392])
                        SUPER = [(0, 256)] + [(256 + 1024 * i_, 1024) for i_ in range(4)]
                        I32 = mybir.dt.int32
                        with ExitStack() as PA:
                            wu5 = sbuf(PA, "wu5", [128, 8, 512], BF16)
                            dma(wu5[:], wv[:, :, OU5:OU5 + 512], "w1", (), ["wu5"], eng="pool")
                            uts = sbuf(PA, "uts", [128, 8, 1024], BF16); Xf = sbuf(PA, "Xf", [128, 8, 512])
                            Xb = sbuf(PA, "Xb", [128, 32, 8, 16], BF16)
                            for (s0, T) in SUPER:
                                nb = T // 8; blk0 = s0 // 8
                                rd = [f"uT{c}" for c in range(s0 // 128, (s0 + T) // 128)]
                                dma(uts[:, :, 0:T], uT_v[:, :, s0:s0 + T], "uts", rd, ["uts"])
                                utv = uts[:, :, 0:T].rearrange("p k (b j) -> p k j b", j=8)
                                for j in range(8):
                                    bk = j % 2
                                    for k in range(8):
                                        mm(banks[bk][0:nb, :], utv[:, k, j, :], wu5[:, k, :], ["uts", "wu5"], [BK[bk]], start=(k == 0), stop=(k == 7))
                                    if not os.environ.get("NOXF"):
                                        act(Xf[0:nb, j, :], banks[bk][0:nb, :], AF.Identity, [BK[bk]], ["Xf"])
                                    if not os.environ.get("NOXB"):
                                        cp("dve", Xb[0:nb, :, j, :], banks[bk][0:nb, :].rearrange("p (g h) -> p g h", g=32), [BK[bk]], ["Xb"])
                                if SSTOP < 0.2:
                                    continue
                                dma(u5_d[s0:s0 + T, :].rearrange("(b j) c -> b j c", j=8), Xf[0:nb, :, :], "u5st", ["Xf"], [f"u5s{s0}"])
                                if SSTOP < 0.3:
                                    continue
                                for g in range(32):
                                    bk = 2 + (g // 4) % 2
                                    mm(banks[bk][:, (g % 4) * 128:(g % 4) * 128 + nb], Xb[0:nb, g, :, :].rearrange("p a b -> p (a b)"), identb[0:nb, 0:nb],
                                       ["Xb", "identb"], [BK[bk]])
                                    if g % 4 == 3:
                                        cp("dve", Uall[:, g - 3:g + 1, blk0:blk0 + nb], banks[bk][:, :].rearrange("p (a b) -> p a b", a=4)[:, :, 0:nb],
                                           [BK[bk]], ["Uall"])
                            P.barrier()
                        if SSTOP < 1:
                            return
                        WstRe = sbuf(PS, "WstRe", [128, 32, 128], BF16); WstIm = sbuf(PS, "WstIm", [128, 32, 128], BF16)
                        Mg = sbuf(PS, "Mg", [128, 32, 128], BF16)
                        WoRe = sbuf(PS, "WoRe", [128, 32, 128], BF16); WoIm = sbuf(PS, "WoIm", [128, 32, 128], BF16)
                        mset("pool", WoRe[:], 0.0, ["WoRe"]); mset("pool", WoIm[:], 0.0, ["WoIm"])
                        sm = sbuf(PS, "sm", [128, 20, 16]); S = lambda i_: sm[:, i_, :]
                        ti_t = sbuf(PS, "ti_t", [128, 544], I32); tf_t = sbuf(PS, "tf_t", [128, 544]); tm_t = sbuf(PS, "tm_t", [128, 544])
                        tq_t = sbuf(PS, "tq_t", [128, 544])
                        IDm = sbuf(PS, "IDm", [128, 2, 128])
                        mset("dve", IDm[:], 0.0, ["IDm"])
                        cp("dve", IDm[:, 0, 0:64], IDENT[:, 0:64], ["cst", "IDm"], ["IDm"])
                        cp("dve", IDm[:, 1, 64:128], IDENT[:, 64:128], ["cst", "IDm"], ["IDm"])

                        def fturn(dst, src, n, mul, add_turn, rd, wr):
                            ti = ti_t[:, 0:n]; tf = tf_t[:, 0:n]; tm = tm_t[:, 0:n]
                            tsc("dve", dst, src, mul, add_turn, ALU.mult, ALU.add, list(rd), list(wr))
                            cp("dve", ti, dst, list(wr), ["trg"])
                            cp("dve", tf, ti, ["trg"], ["trg"])
                            tt("dve", dst, dst, tf, ALU.subtract, list(wr) + ["trg"], list(wr))

                        def sincos_turn(dsin, dcos, f, n, rd, wr):
                            t_ = tq_t[:, 0:n]
                            act(dsin, f, AF.Sin, list(rd), list(wr), scale=6.28318)
                            act(t_, f, AF.Sin, list(rd), ["tq"], scale=3.14159)
                            tt("dve", t_, t_, t_, ALU.mult, ["tq"], ["tq"])
                            tsc("dve", dcos, t_, -2.0, 1.0, ALU.mult, ALU.add, ["tq"], list(wr))

                        for d in (1, 0):
                            K1 = ["sm"]
                            with ExitStack() as PC:
                                prm = sbuf(PC, "prm", [128, 48]); bbp = sbuf(PC, "bbp", [128, 512]); ccp = sbuf(PC, "ccp", [128, 512])
                                bbr = sbuf(PC, "bbr", [128, 256]); bbi = sbuf(PC, "bbi", [128, 256]); t16 = sbuf(PC, "t16", [128, 256])
                                pw = sbuf(PC, "pw", [128, 6, 16, 24])
                                KBre = sbuf(PC, "KBre", [128, 16, 8, 16]); KBim = sbuf(PC, "KBim", [128, 16, 8, 16])
                                QCre = sbuf(PC, "QCre", [128, 16, 8, 16]); QCim = sbuf(PC, "QCim", [128, 16, 8, 16])
                                QZre = sbuf(PC, "QZre", [128, 16, 128]); QZim = sbuf(PC, "QZim", [128, 16, 128]); tK = sbuf(PC, "tK", [128, 16, 8, 16])
                                dma(prm[:], s5p_d[l, d], "pp", (), ["prm"]); dma(bbp[:], s5b_d[l], "pp", (), ["bbp"]); dma(ccp[:], s5c_d[l, d], "pp", (), ["ccp"])
                                lre = prm[:, 0:16]; lim = prm[:, 16:32]
                                act(S(0), prm[:, 32:48], AF.Exp, ["prm"], K1)
                                tt("dve", S(1), lre, S(0), ALU.mult, ["prm"] + K1, K1)
                                tt("dve", S(2), lim, S(0), ALU.mult, ["prm"] + K1, K1)
                                act(S(3), S(1), AF.Exp, K1, K1)
                                fturn(S(6), S(2), 16, 1.0 / (2 * PI), 0.0, K1, K1)
                                sincos_turn(S(4), S(5), S(6), 16, K1, K1)
                                tt("dve", S(7), S(3), S(5), ALU.mult, K1, K1)
                                tsc("dve", S(7), S(7), -1.0, None, ALU.add, None, K1, K1)
                                tt("dve", S(8), S(3), S(4), ALU.mult, K1, K1)
                                tt("dve", S(9), lre, lre, ALU.mult, ["prm"], K1)
                                tt("dve", S(10), lim, lim, ALU.mult, ["prm"], K1)
                                tt("dve", S(9), S(9), S(10), ALU.add, K1, K1)
                                P.op("dve", lambda e: e.reciprocal(S(9), S(9)), K1, K1)
                                tt("dve", S(10), S(7), lre, ALU.mult, K1 + ["prm"], K1)
                                tt("dve", S(11), S(8), lim, ALU.mult, K1 + ["prm"], K1)
                                tt("dve", S(10), S(10), S(11), ALU.add, K1, K1)
                                tt("dve", S(10), S(10), S(9), ALU.mult, K1, K1)
                                tt("dve", S(11), S(8), lre, ALU.mult, K1 + ["prm"], K1)
                                tt("dve", S(12), S(7), lim, ALU.mult, K1 + ["prm"], K1)
                                tt("dve", S(11), S(11), S(12), ALU.subtract, K1, K1)
                                tt("dve", S(11), S(11), S(9), ALU.mult, K1, K1)
                                v3 = lambda t_: t_.rearrange("p (q h) -> p q h", q=16)
                                kreb = S(10).unsqueeze(2).to_broadcast([128, 16, 16]); kimb = S(11).unsqueeze(2).to_broadcast([128, 16, 16])
                                tt("dve", v3(bbr[:]), v3(bbp[:, 0:256]), kreb, ALU.mult, ["bbp"] + K1, ["bbr"])
                                tt("dve", v3(t16[:]), v3(bbp[:, 256:512]), kimb, ALU.mult, ["bbp"] + K1, ["t16"])
                                tt("dve", bbr[:], bbr[:], t16[:], ALU.subtract, ["bbr", "t16"], ["bbr"])
                                tt("dve", v3(bbi[:]), v3(bbp[:, 256:512]), kreb, ALU.mult, ["bbp"] + K1, ["bbi"])
                                tt("dve", v3(t16[:]), v3(bbp[:, 0:256]), kimb, ALU.mult, ["bbp"] + K1, ["t16"])
                                tt("dve", bbi[:], bbi[:], t16[:], ALU.add, ["bbi", "t16"], ["bbi"])
                                act(S(13), S(1), AF.Exp, K1, K1, scale=8.0)
                                fturn(S(14), S(2), 16, 8.0 / (2 * PI), 0.0, K1, K1)
                                kr = c2[:, d * 24:(d + 1) * 24].unsqueeze(1).to_broadcast([128, 16, 24])
                                PK = ["pw"]
                                tt("dve", pw[:, 0], S(2).unsqueeze(2).to_broadcast([128, 16, 24]), kr, ALU.mult, K1 + ["c2"], PK)
                                tt("dve", pw[:, 2], S(1).unsqueeze(2).to_broadcast([128, 16, 24]), kr, ALU.mult, K1 + ["c2"], PK)
                                f2 = lambda i_: pw[:, i_].rearrange("p a b -> p (a b)")
                                act(f2(2), f2(2), AF.Exp, PK, PK)
                                fturn(f2(1), f2(0), 384, 1.0 / (2 * PI), 0.0, PK, PK)
                                sincos_turn(f2(3), f2(4), f2(1), 384, PK, PK)
                                tt("dve", f2(4), f2(4), f2(2), ALU.mult, PK, PK)
                                tt("dve", f2(5), f2(3), f2(2), ALU.mult, PK, PK)
                                PWre = pw[:, 4]; PWim = pw[:, 5]

                                def cprod(ore, oim, row, xr, xi, kx, neg_im=False):
                                    pr = PWre[:, :, row * 8:(row + 1) * 8].unsqueeze(3).to_broadcast([128, 16, 8, 16])
                                    pi_ = PWim[:, :, row * 8:(row + 1) * 8].unsqueeze(3).to_broadcast([128, 16, 8, 16])
                                    xr4 = v3(xr).unsqueeze(2).to_broadcast([128, 16, 8, 16]); xi4 = v3(xi).unsqueeze(2).to_broadcast([128, 16, 8, 16])
                                    tt("dve", ore[:], pr, xr4, ALU.mult, PK + [kx], ["cpo"])
                                    tt("dve", tK[:], pi_, xi4, ALU.mult, PK + [kx], ["tK"])
                                    tt("dve", ore[:], ore[:], tK[:], ALU.subtract, ["cpo", "tK"], ["cpo"])
                                    tt("dve", oim[:], pr, xi4, ALU.mult, PK + [kx], ["cpo"])
                                    tt("dve", tK[:], pi_, xr4, ALU.mult, PK + [kx], ["tK"])
                                    if neg_im:
                                        stt(oim[:], oim[:], -1.0, tK[:], ALU.mult, ALU.subtract, ["cpo", "tK"], ["cpo"])
                                    else:
                                        tt("dve", oim[:], oim[:], tK[:], ALU.add, ["cpo", "tK"], ["cpo"])
                                f3 = lambda t_: t_[:].rearrange("p q a b -> p q (a b)")
                                cprod(KBre, KBim, 0, bbr[:], bbi[:], "bbr")
                                cprod(QCre, QCim, 1, ccp[:, 0:256], ccp[:, 256:512], "ccp", neg_im=True)
                                for (src_, dst_) in ((KBre, WstRe), (KBim, WstIm)):
                                    dv = dst_[:].rearrange("p (q m) c -> p q m c", m=2)
                                    for m in range(2):
                                        for q in range(16):
                                            bk = 4 + (q // 4) % 2
                                            mm(banks[bk][:, (q % 4) * 128:(q % 4 + 1) * 128], f3(src_)[:, q, :], IDm[:, m, :], ["cpo", "IDm"], [BK[bk]])
                                            if q % 4 == 3:
                                                cp("dve", dv[:, q - 3:q + 1, m, :], banks[bk][:, :].rearrange("p (a b) -> p a b", a=4), [BK[bk]], ["Wst"])
                                mgv = Mg[:].rearrange("p (q m) c -> p q m c", m=2)
                                for m in range(2):
                                    o = 1 - m
                                    mset("pool", QZre[o * 64:(o + 1) * 64, :, :], 0.0, ["QZ"]); mset("pool", QZim[o * 64:(o + 1) * 64, :, :], 0.0, ["QZ"])
                                    cp("pool", QZre[m * 64:(m + 1) * 64, :, :], f3(QCre)[m * 64:(m + 1) * 64, :, :], ["cpo", "QZ"], ["QZ"])
                                    cp("pool", QZim[m * 64:(m + 1) * 64, :, :], f3(QCim)[m * 64:(m + 1) * 64, :, :], ["cpo", "QZ"], ["QZ"])
                                    for q in range(16):
                                        bk = 6 + (q // 4) % 2
                                        osl = banks[bk][:, (q % 4) * 128:(q % 4 + 1) * 128]
                                        mm(osl, f3(KBre)[:, q, :], QZre[:, q, :], ["cpo", "QZ"], [BK[bk]], start=True, stop=False)
                                        mm(osl, f3(KBim)[:, q, :], QZim[:, q, :], ["cpo", "QZ"], [BK[bk]], start=False, stop=True)
                                        if q % 4 == 3:
                                            tt("dve", mgv[:, q - 3:q + 1, m, :], banks[bk][:, :].rearrange("p (a b) -> p a b", a=4),
                                               M8[d].unsqueeze(1).to_broadcast([128, 4, 128]), ALU.mult, [BK[bk], "c2"], ["Mg"])
                                cprod(QCre, QCim, 2, ccp[:, 0:256], ccp[:, 256:512], "ccp", neg_im=True)
                                for (src_, dst_, kk) in ((QCre, WoRe, "WoRe"), (QCim, WoIm, "WoIm")):
                                    dv = dst_[:].rearrange("p (q m) c -> p q m c", m=2)
                                    for m in range(2):
                                        cp("dve", dv[m * 64:(m + 1) * 64, :, m, :], f3(src_)[m * 64:(m + 1) * 64, :, :], ["cpo"], [kk])
                                P.barrier()
                            if dbg:
                                dsm = nc.dram_tensor(f"dbg_sm{l}_{d}", [128, 320], F32, kind="ExternalOutput").ap()
                                dma(dsm, sm[:].rearrange("p a b -> p (a b)"), "dbg", ["sm"], [f"dbg_sm{d}"])
                            if SSTOP < 2:
                                return
                            with ExitStack() as PY:
                              Yall = sbuf(PY, "Yall", [128, 32, 544])
                              with ExitStack() as PR:
                                cosB = sbuf(PR, "cosB", [128, 544]); sinB = sbuf(PR, "sinB", [128, 544]); fq = sbuf(PR, "fq", [128, 544])
                                ta = sbuf(PR, "ta", [128, 544]); tb = sbuf(PR, "tb", [128, 544]); tc_ = sbuf(PR, "tc", [128, 544]); td = sbuf(PR, "td", [128, 544])
                                Sre = sbuf(PR, "Sre", [128, 544]); Sim = sbuf(PR, "Sim", [128, 544])
                                Wre = sbuf(PR, "Wre", [128, 544]); Wim = sbuf(PR, "Wim", [128, 544])
                                Hre = sbuf(PR, "Hre", [128, 544], BF16); Him = sbuf(PR, "Him", [128, 544], BF16)
                                car = sbuf(PR, "car", [128, 4])
                                NB = NBv[d]
                                bc = 31 if d == 0 else 0
                                SEG = ((0, 32), (32, 544))
                                PIECES = ((0, 512), (512, 32))
                                cosB2 = [cosB, sbuf(PR, "cosB1", [128, 544])]; sinB2 = [sinB, sbuf(PR, "sinB1", [128, 544])]

                                def s_mm_tab(q):
                                    for (c0, n) in PIECES:
                                        for m in range(2):
                                            mm(big[0][:, c0:c0 + n], WstRe[:, 2 * q + m, :], Uall[:, 2 * q + m, c0:c0 + n], ["Wst", "Uall"], ["b0", "b1"],
                                               start=(m == 0), stop=(m == 1))
                                        for m in range(2):
                                            mm(big[1][:, c0:c0 + n], WstIm[:, 2 * q + m, :], Uall[:, 2 * q + m, c0:c0 + n], ["Wst", "Uall"], ["b2", "b3"],
                                               start=(m == 0), stop=(m == 1))
                                    fturn(fq[:], NB, 544, S(14)[:, q:q + 1], 0.0, ["c2"] + K1, ["fq"])
                                    sincos_turn(sinB2[q % 2][:], cosB2[q % 2][:], fq[:], 544, ["fq"], [f"trig{q % 2}"])
                                s_mm_tab(0)
                                for q in range(16):
                                    cosB = cosB2[q % 2]; sinB = sinB2[q % 2]; ktr = f"trig{q % 2}"
                                    pR = big[0][:, 0:544]; pI = big[1][:, 0:544]
                                    tt("dve", ta[:], pR, cosB[:], ALU.mult, ["b0", "b1", ktr], ["ta"])
                                    tt("dve", tb[:], pI, sinB[:], ALU.mult, ["b2", "b3", ktr], ["tb"])
                                    tt("pool", Sre[:], ta[:], tb[:], ALU.add, ["ta", "tb"], ["Sre"])
                                    tt("dve", tc_[:], pI, cosB[:], ALU.mult, ["b2", "b3", ktr], ["tc"])
                                    tt("dve", td[:], pR, sinB[:], ALU.mult, ["b0", "b1", ktr], ["td"])
                                    tt("pool", Sim[:], tc_[:], td[:], ALU.subtract, ["tc", "td"], ["Sim"])
                                    if q + 1 < 16:
                                        s_mm_tab(q + 1)
                                    r8b = S(13)[:, q:q + 1]

                                    def scan(w_, s_, a0, a1, init, wk, sk, extra, r8b=r8b):
                                        o_ap, d1 = w_[:, a0:a1], s_[:, a0:a1]
                                        if d == 1:
                                            o_ap, d1 = o_ap[:, ::-1], d1[:, ::-1]
                                        P.op("dve", lambda e: e.tensor_tensor_scan(o_ap, r8b.to_broadcast([128, a1 - a0]), d1, init, ALU.mult, ALU.add),
                                             [sk] + K1 + extra, [wk])
                                    scan(Wre, Sre, 0, 32, 0.0, "Wre", "Sre", [])
                                    scan(Wim, Sim, 0, 32, 0.0, "Wim", "Sim", [])
                                    c_ = cosB[:, bc:bc + 1]; s_ = sinB[:, bc:bc + 1]
                                    tt("dve", car[:, 2:3], Wim[:, bc:bc + 1], s_, ALU.mult, ["Wim", ktr], ["car"])
                                    stt(car[:, 0:1], Wre[:, bc:bc + 1], c_, car[:, 2:3], ALU.mult, ALU.subtract, ["Wre", ktr, "car"], ["car"])
                                    tt("dve", car[:, 3:4], Wre[:, bc:bc + 1], s_, ALU.mult, ["Wre", ktr], ["car"])
                                    stt(car[:, 1:2], Wim[:, bc:bc + 1], c_, car[:, 3:4], ALU.mult, ALU.add, ["Wim", ktr, "car"], ["car"])
                                    scan(Wre, Sre, 32, 544, car[:, 0:1], "Wre", "Sre", ["car"])
                                    scan(Wim, Sim, 32, 544, car[:, 1:2], "Wim", "Sim", ["car"])
                                    tt("pool", ta[:], Wre[:], cosB[:], ALU.mult, ["Wre", ktr], ["ta"])
                                    tt("pool", tb[:], Wim[:], sinB[:], ALU.mult, ["Wim", ktr], ["tb"])
                                    tt("pool", tc_[:], Wim[:], cosB[:], ALU.mult, ["Wim", ktr], ["tc"])
                                    tt("pool", td[:], Wre[:], sinB[:], ALU.mult, ["Wre", ktr], ["td"])
                                    for (a0, a1) in SEG:
                                        if d == 0:
                                            so, si = slice(a0 + 1, a1), slice(a0, a1 - 1); ic = a0
                                        else:
                                            so, si = slice(a0, a1 - 1), slice(a0 + 1, a1); ic = a1 - 1
                                        tt("dve", Hre[:, so], ta[:, si], tb[:, si], ALU.subtract, ["ta", "tb"], ["Hre"])
                                        tt("dve", Him[:, so], tc_[:, si], td[:, si], ALU.add, ["tc", "td"], ["Him"])
                                        if a0 == 0:
                                            mset("pool", Hre[:, ic:ic + 1], 0.0, ["Hre"]); mset("pool", Him[:, ic:ic + 1], 0.0, ["Him"])
                                        else:
                                            cp("dve", Hre[:, ic:ic + 1], car[:, 0:1], ["car", "Hre"], ["Hre"])
                                            cp("dve", Him[:, ic:ic + 1], car[:, 1:2], ["car", "Him"], ["Him"])
                                    for m in range(2):
                                        g = 2 * q + m; bY = big[2 + m]; kY = [BK[4 + 2 * m], BK[5 + 2 * m]]
                                        for (c0, n) in PIECES:
                                            mm(bY[:, c0:c0 + n], Mg[:, g, :], Uall[:, g, c0:c0 + n], ["Mg", "Uall"], kY, start=True, stop=False)
                                            mm(bY[:, c0:c0 + n], WoRe[:, g, :], Hre[:, c0:c0 + n], ["WoRe", "Hre"], kY, start=False, stop=False)
                                            mm(bY[:, c0:c0 + n], WoIm[:, g, :], Him[:, c0:c0 + n], ["WoIm", "Him"], kY, start=False, stop=True)
                                        if m == 0:
                                            act(Yall[:, g, :], bY[:, 0:544], AF.Identity, kY, ["Yall"])
                                        else:
                                            cp("dve", Yall[:, g, :], bY[:, 0:544], kY, ["Yall"])
                                P.barrier()
                              if SSTOP < 3:
                                return
                              if True:
                                Ytm = sbuf(PY, "Ytm", [128, 8, 512])
                                for (s0, T) in SUPER:
                                    nb = T // 8; blk0 = s0 // 8
                                    for g in range(32):
                                        bk = (g // 4) % 2
                                        mm(banks[bk][0:nb, (g % 4) * 128:(g % 4 + 1) * 128], Yall[:, g, blk0:blk0 + nb], IDENT, ["Yall", "cst"], [BK[bk]])
                                        if g % 4 == 3:
                                            g0 = g - 3
                                            cp("dve" if (g // 4) % 2 else "act_id", Ytm[0:nb, :, g0 * 16:(g0 + 4) * 16].rearrange("p j (g h) -> p j g h", g=4),
                                               banks[bk][0:nb, :].rearrange("p (g j h) -> p j g h", g=4, j=8), [BK[bk]], ["Ytm"])
                                    dma(y5_d[d][s0:s0 + T, :].rearrange("(b j) c -> b j c", j=8), Ytm[0:nb, :, :], "y5st", ["Ytm"], [f"y5_{d}_{s0}"])
                                P.barrier()
                        P.barrier()
                    if SSTOP < 4:
                        return
                    with ExitStack() as PF:
                        wv = win_d[l].rearrange("(k p) n -> p k n", p=128)
                        wsg = sbuf(PF, "wsg", [128, 8, 512], BF16); wglu = sbuf(PF, "wglu", [128, 4, 1024], BF16)
                        dma(wsg[:], wv[:, :, OSG:OSG + 512], "w2", (), ["wsg"], eng="pool")
                        dma(wglu[:], gluw_d[l].rearrange("(c p) n -> p c n", p=128), "w3", (), ["wglu"], eng="pool")
                        glubb = sbuf(PF, "glubb", [128, 1024]); dskb = sbuf(PF, "dskb5", [128, 512])
                        dma(glubb[:], glub_d[l:l + 1, :].partition_broadcast(128), "pp", (), ["glubb"])
                        dma(dskb[:], s5dr_d[l:l + 1, :].partition_broadcast(128), "pp", (), ["dskb5"])
                        uc = [sbuf(PF, f"suc{i_}", [128, 8, 128], BF16) for i_ in range(2)]
                        yf = [sbuf(PF, f"yf{i_}", [128, 512]) for i_ in range(2)]; yb = [sbuf(PF, f"yb{i_}", [128, 512]) for i_ in range(2)]
                        u5 = [sbuf(PF, f"u5t{i_}", [128, 512]) for i_ in range(2)]
                        ge2 = [sbuf(PF, f"ge{i_}", [128, 512], BF16) for i_ in range(2)]; geT2 = [sbuf(PF, f"geT{i_}", [128, 4, 128], BF16) for i_ in range(2)]
                        pa2 = [sbuf(PF, f"pa{i_}", [128, 512]) for i_ in range(2)]; pg2 = [sbuf(PF, f"pg{i_}", [128, 512]) for i_ in range(2)]
                        ssg2 = [sbuf(PF, f"ssg{i_}", [128, 512]) for i_ in range(2)]; cto2 = [sbuf(PF, f"scto{i_}", [128, 512], BF16) for i_ in range(2)]
                        chunks = list(range(2, NCH)) if last else list(range(NCH))
                        sck = lambda ci: 0 if ci < 2 else 256 + 1024 * ((ci - 2) // 8)

                        def loadf(i_):
                            ci = chunks[i_]; s0 = ci * 128; b_ = i_ % 2
                            load_uc(uc[b_], f"suc{b_}", ci, 0)
                            dma(yf[b_][:], y5_d[0][s0:s0 + 128, :], f"yf{b_}", [f"y5_0_{sck(ci)}"], [f"yf{b_}"])
                            dma(yb[b_][:], y5_d[1][s0:s0 + 128, :], f"yb{b_}", [f"y5_1_{sck(ci)}"], [f"yb{b_}"])
                            dma(u5[b_][:], u5_d[s0:s0 + 128, :], f"u5t{b_}", [f"u5s{sck(ci)}"], [f"u5t{b_}"])
                        def names(i_):
                            b_ = i_ % 2
                            return (b_, ge2[b_], geT2[b_], pa2[b_], pg2[b_], ssg2[b_], cto2[b_],
                                    f"ge{b_}", f"geT{b_}", f"pa{b_}", f"pg{b_}", f"ssg{b_}", f"scto{b_}",
                                    ((0, 2, 3, 4) if b_ == 0 else (1, 5, 6, 7)))

                        def s1(i_):
                            ci = chunks[i_]
                            b_, ge, geT, pa, pg, ssg, cto, kge, kgeT, kpa, kpg, kssg, kcto, (B0, B2, B3, B4) = names(i_)
                            u_ = uc[b_]; uk = f"suc{b_}"
                            tt("dve", yf[b_][:], yf[b_][:], yb[b_][:], ALU.add, [f"yf{b_}", f"yb{b_}"], [f"yf{b_}"])
                            tt("dve", u5[b_][:], u5[b_][:], dskb[:], ALU.mult, [f"u5t{b_}", "dskb5"], [f"u5t{b_}"])
                            tt("dve", yf[b_][:], yf[b_][:], u5[b_][:], ALU.add, [f"yf{b_}", f"u5t{b_}"], [f"yf{b_}"])
                            act(ge[:], yf[b_][:], AF.Gelu, [f"yf{b_}"], [kge])
                            for c4 in range(4):
                                mm(banks[B0][:, c4 * 128:(c4 + 1) * 128], ge[:, c4 * 128:(c4 + 1) * 128], identb[:], [kge, "identb"], [BK[B0]])
                            cp("dve", geT[:].rearrange("p a b -> p (a b)"), banks[B0][:, :], [BK[B0]], [kgeT])
                            for nb_, BB in ((0, B2), (1, B3)):
                                for c4 in range(4):
                                    mm(banks[BB][:, :], geT[:, c4, :], wglu[:, c4, nb_ * 512:(nb_ + 1) * 512], [kgeT, "wglu"], [BK[BB]],
                                       start=(c4 == 0), stop=(c4 == 3))
                            for k in range(8):
                                mm(banks[B4][:, :], u_[:, k, :], wsg[:, k, :], [uk, "wsg"], [BK[B4]], start=(k == 0), stop=(k == 7))

                        def s2(i_):
                            ci = chunks[i_]; s0 = ci * 128
                            b_, ge, geT, pa, pg, ssg, cto, kge, kgeT, kpa, kpg, kssg, kcto, (B0, B2, B3, B4) = names(i_)
                            tt("dve", pa[:], banks[B2][:, :], glubb[:, 0:512], ALU.add, [BK[B2], "glubb"], [kpa])
                            tt("dve", pg[:], banks[B3][:, :], glubb[:, 512:1024], ALU.add, [BK[B3], "glubb"], [kpg])
                            act(pg[:], pg[:], AF.Sigmoid, [kpg], [kpg])
                            act(ssg[:], banks[B4][:, :], AF.Silu, [BK[B4]], [kssg])
                            tt("dve", pa[:], pa[:], pg[:], ALU.mult, [kpa, kpg], [kpa])
                            tt("dve", cto[:], pa[:], ssg[:], ALU.mult, [kpa, kssg], [kcto])
                            dma(cat_d[s0:s0 + 128, 1536:2048], cto[:], f"catst{b_}", [kcto], [f"cat_s5{ci}"])
                        loadf(0)
                        if len(chunks) > 1:
                            loadf(1)
                        s1(0)
                        for i_ in range(len(chunks)):
                            if i_ + 2 < len(chunks):
                                loadf(i_ + 2)
                            if i_ + 1 < len(chunks):
                                s1(i_ + 1)
                            s2(i_)
                        P.barrier()

                if "ssd" in phases:
                    phase_ssd()
                if "gla" in phases:
                    phase_gla()
                if "s5" in phases:
                    phase_s5()
                if "p4" in phases:
                    phase_p4()
        finals = [k for k in P.last_write if k.startswith(f"h{DEPTH}_") or (dbg and not k.startswith("b"))]
        P.wait_all("sp", finals)
        P.emit()
    return nc


def sweep_order(d):
    if d == 0:
        return list(range(NCH))
    return [1, 0] + list(range(NCH - 1, 1, -1))


def _consts():
    k = np.arange(128)[:, None]; j = np.arange(128)[None, :]
    mats = [np.eye(128), (k <= j), (k >= j), (k > j), (k < j), np.ones((128, 128)),
            np.broadcast_to(np.arange(1, 129)[None, :], (128, 128)), np.broadcast_to(np.arange(128, 0, -1)[None, :], (128, 128))]
    return np.concatenate([m.astype(np.float32) for m in mats], axis=1)


def _consts2():
    kr = np.zeros((2, 3, 8), np.float32)
    i = np.arange(8)
    kr[0, 0] = 7 - i; kr[0, 1] = i - 7; kr[0, 2] = i + 1
    kr[1, 0] = i; kr[1, 1] = -i; kr[1, 2] = 8 - i
    nbf = np.concatenate([np.arange(1, 33), np.arange(1, 513)]).astype(np.float32)
    nbb = np.concatenate([np.arange(32, 0, -1), np.arange(512, 0, -1)]).astype(np.float32)
    r = np.arange(128)[:, None] // 16; c = np.arange(128)[None, :] // 16
    m8f = (c >= r).astype(np.float32); m8b = (r >= c).astype(np.float32)
    row = np.concatenate([kr.reshape(-1), nbf, nbb])
    return np.ascontiguousarray(np.concatenate([np.broadcast_to(row[None, :], (128, row.size)), m8f, m8b], axis=1).astype(np.float32))


def _pairlay(a):
    a = np.asarray(a)
    rest = a.shape[2:]
    a = a.reshape(16, 2, 64, *rest)
    a = np.moveaxis(a, 0, 2)
    return np.ascontiguousarray(a.reshape(128, 16, *rest))


def prep_inputs(inp, b):
    f = lambda a: np.ascontiguousarray(np.asarray(a, dtype=np.float32))
    col = lambda v, n: f(np.asarray(v).reshape(n, 128).T)
    m = {}
    m["x"] = f(inp["x"][b]); m["ctx"] = f(inp["ctx"][b])
    m["ccol"] = f(np.concatenate([col(inp["c"][b], 8), col(inp["c_ctx"], 8)], axis=1))
    m["mod_w"] = f(inp["mod_w"]); m["mod_b"] = f(inp["mod_b"])
    m["modb_col"] = f(np.stack([col(inp["mod_b"][l], 24) for l in range(DEPTH)]))
    m["nw_col"] = f(np.stack([col(inp["norm_w"][l], 8) for l in range(DEPTH)]))
    m["w_in"] = f(inp["w_in"]); m["w_out"] = f(inp["w_out"])
    cw = np.asarray(inp["ssd_conv_w"])
    m["convw_col"] = f(np.stack([cw[l].reshape(5, 12, 128).transpose(2, 1, 0).reshape(128, 60) for l in range(DEPTH)]))
    m["convb_col"] = f(np.stack([col(inp["ssd_conv_b"][l], 12) for l in range(DEPTH)]))
    m["a_log"] = f(np.asarray(inp["ssd_a_log"]).reshape(DEPTH, 32)); m["dt_bias"] = f(np.asarray(inp["ssd_dt_bias"]).reshape(DEPTH, 32))
    m["ssd_d"] = f(inp["ssd_d"]); m["ssd_nw"] = f(inp["ssd_norm_w"])
    wl = np.asarray(inp["gla_w_lr"])
    m["wlr"] = f(wl.transpose(0, 2, 1, 3).reshape(DEPTH, 16, 512)); m["blr"] = f(np.asarray(inp["gla_b_lr"]).reshape(DEPTH, 512))
    m["gla_nw"] = f(inp["gla_norm_w"])
    s5p = np.zeros((DEPTH, 2, 128, 48), np.float32)
    s5c = np.zeros((DEPTH, 2, 128, 512), np.float32)
    s5b = np.zeros((DEPTH, 128, 512), np.float32)
    for l in range(DEPTH):
        s5b[l, :, 0:256] = _pairlay(inp["s5_b_re"][l]).reshape(128, 256)
        s5b[l, :, 256:512] = _pairlay(inp["s5_b_im"][l]).reshape(128, 256)
        for d in range(2):
            s5p[l, d, :, 0:16] = _pairlay(inp["s5_lam_re"][l, d])
            s5p[l, d, :, 16:32] = _pairlay(inp["s5_lam_im"][l, d])
            s5p[l, d, :, 32:48] = _pairlay(np.broadcast_to(np.asarray(inp["s5_log_step"][l, d])[:, None], (32, 64)))
            s5c[l, d, :, 0:256] = _pairlay(np.asarray(inp["s5_c_re"][l, d]).transpose(0, 2, 1)).reshape(128, 256)
            s5c[l, d, :, 256:512] = _pairlay(np.asarray(inp["s5_c_im"][l, d]).transpose(0, 2, 1)).reshape(128, 256)
    m["s5p"] = s5p; m["s5b"] = s5b; m["s5c"] = s5c
    m["s5d_col"] = f(np.stack([col(inp["s5_d"][l], 4) for l in range(DEPTH)]))
    m["glu_w"] = f(inp["s5_glu_w"]); m["glu_b"] = f(inp["s5_glu_b"]); m["fnw"] = f(inp["final_norm_w"])
    m["consts"] = _consts(); m["consts2"] = _consts2(); m["s5d_row"] = f(inp["s5_d"])
    return m


def kernel(**inputs):
    nc = build()
    in_maps = [prep_inputs(inputs, b) for b in range(8)]
    res = run_bass_kernel_spmd(nc, in_maps, core_ids=list(range(8)))
    return np.stack([np.asarray(r["out"], dtype=np.float32) for r in res.results], axis=0)
```
